# Optimizing a Trainium2 kernel written in Bass

```python
import jax
import jax.numpy as jnp
from jax import lax
import numpy as np

D_MODEL = 1024
BATCH = 8
SEQ = 2048
DEPTH = 2

GRID_W = 64
CTX_LEN = 256
N_ADA = 9
D_FF = 2816
NORM_EPS = 1e-6

RET_HEADS = 8
RET_QK_DIM = 64
RET_V_DIM = 128
RET_CHUNK = 128
RET_ROPE_BASE = 10000.0

ATT_HEADS = 8
ATT_KV_HEADS = 2
ATT_HEAD_DIM = 64
ATT_GROUP = ATT_HEADS // ATT_KV_HEADS
WINDOW = 128
ATT_BLOCK = 128
ROPE_BASE = 10000.0

RWKV_HEADS = 8
RWKV_HEAD_DIM = 64
DECAY_LORA = 64
AAA_LORA = 64
MV_LORA = 32
GATE_LORA = 128
RWKV_GN_EPS = 64e-5
RWKV_DECAY_SCALE = 0.6065306597126334

RET_QK_W = RET_HEADS * RET_QK_DIM
RET_V_W = RET_HEADS * RET_V_DIM
ATT_Q_W = ATT_HEADS * ATT_HEAD_DIM
ATT_KV_W = ATT_KV_HEADS * ATT_HEAD_DIM
RWKV_W = RWKV_HEADS * RWKV_HEAD_DIM
RWKV_SPLITS = (RWKV_W, RWKV_W, RWKV_W, DECAY_LORA, DECAY_LORA, AAA_LORA, AAA_LORA, GATE_LORA)
RWKV_IN = sum(RWKV_SPLITS)
N_BRANCH = 3
IN_SPLITS = (RET_QK_W, RET_QK_W, RET_V_W, RET_V_W, ATT_Q_W, ATT_KV_W, ATT_KV_W, RWKV_IN, N_BRANCH * D_MODEL)
N_IN = sum(IN_SPLITS)

kernel_name = 'hybrid_retention_swa_rwkv7_macaron_dit'


def split_cols(t, sizes):
    idx = np.cumsum(sizes)[:-1].tolist()
    return jnp.split(t, idx, axis=-1)


def split_heads(t, n_heads):
    return t.reshape(t.shape[0], t.shape[1], n_heads, -1)


def rms_norm(x, w, eps=NORM_EPS):
    xf = x.astype(jnp.float32)
    y = xf * lax.rsqrt(jnp.mean(jnp.square(xf), -1, keepdims=True) + eps)
    return (y * w.astype(jnp.float32)).astype(x.dtype)


def head_layer_norm(x, eps):
    xf = x.astype(jnp.float32)
    mu = jnp.mean(xf, -1, keepdims=True)
    var = jnp.mean(jnp.square(xf - mu), -1, keepdims=True)
    return (xf - mu) * lax.rsqrt(var + eps)


def modulate(h, shift, scale):
    return h * (1.0 + scale) + shift


def swiglu(h, w_in, w_out):
    gate, up = jnp.split(h @ w_in, 2, axis=-1)
    return (jax.nn.silu(gate) * up) @ w_out


def ffn_sublayer(x, mod, base, norm_w, w_in, w_out):
    h = modulate(rms_norm(x, norm_w), mod[:, :, base], mod[:, :, base + 1])
    return x + 0.5 * mod[:, :, base + 2] * swiglu(h, w_in, w_out)


def rope_angles_1d(pos, dim, base):
    n_freq = dim // 2
    inv = jnp.power(base, -(jnp.arange(n_freq, dtype=jnp.float32) / n_freq))
    return pos.astype(jnp.float32)[:, None] * inv[None, :]


def axial_rope_angles(rows, dim):
    row = jnp.repeat(jnp.arange(rows), GRID_W)
    col = jnp.arange(rows * GRID_W) % GRID_W
    return jnp.concatenate([rope_angles_1d(row, dim // 2, ROPE_BASE),
                            rope_angles_1d(col, dim // 2, ROPE_BASE)], -1)


def apply_rope(t, ang):
    half = t.shape[-1] // 2
    cos = jnp.cos(ang)[:, None, :]
    sin = jnp.sin(ang)[:, None, :]
    t1, t2 = t[..., :half], t[..., half:]
    return jnp.concatenate([t1 * cos - t2 * sin, t1 * sin + t2 * cos], -1)


def flip_seq(t):
    return jnp.flip(t, axis=2)


def centred_conv3(t, w):
    zero = jnp.zeros_like(t[:, :1])
    prev = jnp.concatenate([zero, t[:, :-1]], 1)
    nxt = jnp.concatenate([t[:, 1:], zero], 1)
    return prev * w[0] + t * w[1] + nxt * w[2]


def retention_qkv(q, k, v, ang):
    f32 = jnp.float32
    q = apply_rope(split_heads(q, RET_HEADS).astype(f32), ang)
    k = apply_rope(split_heads(k, RET_HEADS).astype(f32), ang) * (RET_QK_DIM ** -0.5)
    v = split_heads(v, RET_HEADS).astype(f32)
    return q.transpose(0, 2, 1, 3), k.transpose(0, 2, 1, 3), v.transpose(0, 2, 1, 3)


def retention_direction(q, k, v, log_g, s0):
    b, h, n, dk = q.shape
    dv = v.shape[-1]
    nc = n // RET_CHUNK
    qc = q.reshape(b, h, nc, RET_CHUNK, dk)
    kc = k.reshape(b, h, nc, RET_CHUNK, dk)
    vc = v.reshape(b, h, nc, RET_CHUNK, dv)
    pos = jnp.arange(RET_CHUNK, dtype=jnp.float32)
    lg = log_g[:, None]
    diff = pos[:, None] - pos[None, :]
    decay_in = jnp.where(diff >= 0, jnp.exp(lg[:, :, None] * jnp.maximum(diff, 0.0)), 0.0)
    zeta = jnp.exp(lg * (RET_CHUNK - 1.0 - pos))
    xi = jnp.exp(lg * (pos + 1.0))
    chunk_decay = jnp.exp(log_g * RET_CHUNK)[None, :, None, None]
    u = jnp.einsum('bhcmk,hm,bhcmv->bhckv', kc, zeta, vc)

    def step(state, u_c):
        return chunk_decay * state + u_c, state

    s_last, s_prev = lax.scan(step, s0, jnp.moveaxis(u, 2, 0))
    s_prev = jnp.moveaxis(s_prev, 0, 2)
    scores = jnp.einsum('bhcjk,bhcmk->bhcjm', qc, kc) * decay_in[None, :, None]
    out = (jnp.einsum('bhcjm,bhcmv->bhcjv', scores, vc)
           + jnp.einsum('bhcjk,bhckv->bhcjv', qc, s_prev) * xi[None, :, None, :, None])
    return out.reshape(b, h, n, dv), s_last


def retention_output(o, g):
    y = head_layer_norm(o, NORM_EPS).transpose(0, 2, 1, 3)
    y = y.reshape(y.shape[0], y.shape[1], RET_V_W)
    return (jax.nn.silu(g.astype(jnp.float32)) * y).astype(g.dtype)


def attention_qkv(q, k, v, p, ang):
    f32 = jnp.float32
    q = rms_norm(split_heads(q, ATT_HEADS), p['q_norm']).astype(f32)
    k = rms_norm(split_heads(k, ATT_KV_HEADS), p['k_norm']).astype(f32)
    if ang is not None:
        q = apply_rope(q, ang)
        k = apply_rope(k, ang)
    b, n = q.shape[:2]
    q = q.reshape(b, n, ATT_KV_HEADS, ATT_GROUP, ATT_HEAD_DIM).transpose(0, 2, 3, 1, 4)
    k = k.transpose(0, 2, 1, 3)
    v = split_heads(v, ATT_KV_HEADS).astype(f32).transpose(0, 2, 1, 3)
    return q, k, v


def window_context_attention(q, k, v, kc, vc, sink):
    b, hk, g, n, d = q.shape
    nb = n // ATT_BLOCK
    halo = WINDOW // ATT_BLOCK
    span = (2 * halo + 1) * ATT_BLOCK
    pad = ((0, 0), (0, 0), (halo * ATT_BLOCK, halo * ATT_BLOCK), (0, 0))
    kb = jnp.pad(k, pad).reshape(b, hk, nb + 2 * halo, ATT_BLOCK, d)
    vb = jnp.pad(v, pad).reshape(b, hk, nb + 2 * halo, ATT_BLOCK, d)
    kwin = jnp.concatenate([kb[:, :, o:o + nb] for o in range(2 * halo + 1)], axis=3)
    vwin = jnp.concatenate([vb[:, :, o:o + nb] for o in range(2 * halo + 1)], axis=3)
    qb = q.reshape(b, hk, g, nb, ATT_BLOCK, d)
    scale = d ** -0.5
    s_win = jnp.einsum('bhgnqd,bhnkd->bhgnqk', qb, kwin) * scale
    qi = jnp.arange(ATT_BLOCK)
    kj = jnp.arange(span)
    rel = kj[None, :] - halo * ATT_BLOCK - qi[:, None]
    kpos = (jnp.arange(nb)[:, None] - halo) * ATT_BLOCK + kj[None, :]
    valid = (jnp.abs(rel) <= WINDOW)[None] & ((kpos >= 0) & (kpos < n))[:, None, :]
    s_win = jnp.where(valid, s_win, -jnp.inf)
    s_ctx = jnp.einsum('bhgnqd,bhld->bhgnql', qb, kc) * scale
    s_sink = jnp.broadcast_to(sink[None, :, :, None, None, None], s_win.shape[:-1] + (1,))
    prob = jax.nn.softmax(jnp.concatenate([s_win, s_ctx, s_sink], -1), axis=-1)
    n_ctx = kc.shape[2]
    out = (jnp.einsum('bhgnqk,bhnkd->bhgnqd', prob[..., :span], vwin)
           + jnp.einsum('bhgnql,bhld->bhgnqd', prob[..., span:span + n_ctx], vc))
    return out.reshape(b, hk, g, n, d)


def context_attention(q, k, v, sink):
    s = jnp.einsum('bhgqd,bhkd->bhgqk', q, k) * (q.shape[-1] ** -0.5)
    s_sink = jnp.broadcast_to(sink[None, :, :, None, None], s.shape[:-1] + (1,))
    prob = jax.nn.softmax(jnp.concatenate([s, s_sink], -1), axis=-1)
    return jnp.einsum('bhgqk,bhkd->bhgqd', prob[..., :-1], v)


def merge_att_heads(o, dtype):
    b, hk, g, n, d = o.shape
    return o.transpose(0, 3, 1, 2, 4).reshape(b, n, ATT_Q_W).astype(dtype)


def rwkv_prepare(cols, h, v_first, p):
    f32 = jnp.float32

    def heads(t):
        return split_heads(t, RWKV_HEADS).astype(f32)

    cols = centred_conv3(cols, p['shift'])
    r, k, v, wd_f, wd_b, ad_f, ad_b, gd = split_cols(cols, RWKV_SPLITS)
    if v_first is None:
        v_first = v
    else:
        v = v + (v_first - v) * jax.nn.sigmoid(p['v0'] + (h @ p['v_down']) @ p['v_up'])
    g = jax.nn.sigmoid(gd) @ p['g_up']
    kk = heads(k * p['k_k'])
    kk = kk / jnp.maximum(jnp.sqrt(jnp.sum(kk * kk, -1, keepdims=True)), 1e-12)
    dirs = []
    for i, (wd, ad) in enumerate(((wd_f, ad_f), (wd_b, ad_b))):
        z = (p['w0'][i] + jnp.tanh(wd) @ p['w_up'][i]).astype(f32)
        w = jnp.exp(-RWKV_DECAY_SCALE * jax.nn.sigmoid(z))
        a = jax.nn.sigmoid(p['a0'][i] + ad @ p['a_up'][i])
        k_d = k * (1.0 + (a - 1.0) * p['k_a'])
        dirs.append((heads(w), heads(k_d), heads(a)))
    return heads(r), heads(v), kk, g, dirs, v_first


def rwkv7_scan(r, w, k, a, v, kk, s0, reverse):
    def step(S, inp):
        r_t, w_t, k_t, a_t, v_t, kk_t = inp
        sa = -jnp.einsum('bhvk,bhk->bhv', S, kk_t)
        S = (S * w_t[:, :, None, :] + sa[..., None] * (kk_t * a_t)[:, :, None, :]
             + v_t[..., None] * k_t[:, :, None, :])
        return S, jnp.einsum('bhvk,bhk->bhv', S, r_t)

    xs = tuple(jnp.moveaxis(t, 1, 0) for t in (r, w, k, a, v, kk))
    S, ys = lax.scan(step, s0, xs, reverse=reverse)
    return jnp.moveaxis(ys, 0, 1), S


def rwkv_output(y, r, v, dirs, g, p):
    b, n = y.shape[:2]
    yn = head_layer_norm(y, RWKV_GN_EPS).reshape(b, n, RWKV_W) * p['gn_w'] + p['gn_b']
    (_, k_f, _), (_, k_b, _) = dirs
    bonus = (jnp.sum(r * k_f * p['r_k'], -1, keepdims=True)
             + jnp.sum(r * k_b * p['r_k'], -1, keepdims=True)) * v
    return ((yn + bonus.reshape(b, n, RWKV_W)) * g).astype(g.dtype)


def branch_merge(ret, att, rw, gate_logits, p):
    g_ret, g_att, g_rw = jnp.split(jax.nn.sigmoid(gate_logits), N_BRANCH, axis=-1)
    merged = g_ret * (ret @ p['w_ret']) + g_att * (att @ p['w_att']) + g_rw * (rw @ p['w_rwkv'])
    return merged @ p['w_out']


def token_mixing(hx, hs, p, v_first, att_ang, ret_ang_x, ret_ang_s, want_ctx):
    f32 = jnp.float32
    bsz = hx.shape[0]
    cols_x = split_cols(hx @ p['w_in'], IN_SPLITS)
    cols_s = split_cols(hs @ p['w_in'], IN_SPLITS)

    log_g = jax.nn.log_sigmoid(p['ret_decay_logit'].astype(f32))
    qx, kx, vx = retention_qkv(cols_x[0], cols_x[1], cols_x[2], ret_ang_x)
    qs, ks, vs = retention_qkv(cols_s[0], cols_s[1], cols_s[2], ret_ang_s)
    zero_ret = jnp.zeros((bsz, RET_HEADS, RET_QK_DIM, RET_V_DIM), f32)
    os_f, st_f = retention_direction(qs, ks, vs, log_g[0], zero_ret)
    os_b, st_b = retention_direction(flip_seq(qs), flip_seq(ks), flip_seq(vs), log_g[1], zero_ret)
    ox_f, _ = retention_direction(qx, kx, vx, log_g[0], st_f)
    ox_b, _ = retention_direction(flip_seq(qx), flip_seq(kx), flip_seq(vx), log_g[1], st_b)
    ret_x = retention_output(ox_f + flip_seq(ox_b), cols_x[3])

    sink = p['att_sink'].astype(f32).reshape(ATT_KV_HEADS, ATT_GROUP)
    aqx, akx, avx = attention_qkv(cols_x[4], cols_x[5], cols_x[6], p, att_ang)
    aqs, aks, avs = attention_qkv(cols_s[4], cols_s[5], cols_s[6], p, None)
    att_x = merge_att_heads(window_context_attention(aqx, akx, avx, aks, avs, sink), hx.dtype)

    r_x, v_x, kk_x, g_x, dirs_x, vf_x = rwkv_prepare(cols_x[7], hx, v_first[0], p)
    r_s, v_s, kk_s, g_s, dirs_s, vf_s = rwkv_prepare(cols_s[7], hs, v_first[1], p)
    zero_rw = jnp.zeros((bsz, RWKV_HEADS, RWKV_HEAD_DIM, RWKV_HEAD_DIM), f32)
    ys_f, S_f = rwkv7_scan(r_s, *dirs_s[0], v_s, kk_s, zero_rw, False)
    ys_b, S_b = rwkv7_scan(r_s, *dirs_s[1], v_s, kk_s, zero_rw, True)
    yx_f, _ = rwkv7_scan(r_x, *dirs_x[0], v_x, kk_x, S_f, False)
    yx_b, _ = rwkv7_scan(r_x, *dirs_x[1], v_x, kk_x, S_b, True)
    rwkv_x = rwkv_output(yx_f + yx_b, r_x, v_x, dirs_x, g_x, p)

    out_x = branch_merge(ret_x, att_x, rwkv_x, cols_x[8], p)
    out_s = None
    if want_ctx:
        ret_s = retention_output(os_f + flip_seq(os_b), cols_s[3])
        att_s = merge_att_heads(context_attention(aqs, aks, avs, sink), hs.dtype)
        rwkv_s = rwkv_output(ys_f + ys_b, r_s, v_s, dirs_s, g_s, p)
        out_s = branch_merge(ret_s, att_s, rwkv_s, cols_s[8], p)
    return out_x, out_s, (vf_x, vf_s)


def setup_inputs(seed: int = 0) -> dict:
    key = jax.random.key(seed)
    keys = iter(jax.random.split(key, 48))
    f32 = jnp.float32

    def nrm(shape, scale):
        return jax.random.normal(next(keys), shape, f32) * scale

    D = D_MODEL
    h_idx = jnp.arange(RET_HEADS, dtype=f32)
    ret_logit = jnp.log(jnp.exp2(5.0 + h_idx) - 1.0)
    ratio = jnp.linspace(0.0, 1.0, RWKV_W, dtype=f32)
    w0_base = -6.5 + 5.0 * ratio ** 0.85
    shift_base = jnp.array([0.25, 0.5, 0.25], f32)[None, :, None]
    return {
        'x': nrm((BATCH, SEQ, D), 1.0),
        'c': nrm((BATCH, D), 1.0),
        'ctx': nrm((BATCH, CTX_LEN, D), 1.0),
        'c_ctx': nrm((D,), 1.0),
        'ada_w': nrm((DEPTH, D, N_ADA * D), 0.5 * D ** -0.5),
        'ada_b': nrm((DEPTH, N_ADA * D), 0.02),
        'norm_w': 1.0 + nrm((DEPTH, 3, D), 0.05),
        'ffn1_w_in': nrm((DEPTH, D, 2 * D_FF), D ** -0.5),
        'ffn1_w_out': nrm((DEPTH, D_FF, D), D_FF ** -0.5),
        'ffn2_w_in': nrm((DEPTH, D, 2 * D_FF), D ** -0.5),
        'ffn2_w_out': nrm((DEPTH, D_FF, D), D_FF ** -0.5),
        'mix_w_in': nrm((DEPTH, D, N_IN), D ** -0.5),
        'ret_decay_logit': ret_logit[None, None, :] + nrm((DEPTH, 2, RET_HEADS), 0.05),
        'att_q_norm': 1.0 + nrm((DEPTH, ATT_HEAD_DIM), 0.05),
        'att_k_norm': 1.0 + nrm((DEPTH, ATT_HEAD_DIM), 0.05),
        'att_sink': nrm((DEPTH, ATT_HEADS), 0.5),
        'rwkv_shift': shift_base + nrm((DEPTH, 3, RWKV_IN), 0.05),
        'rwkv_w0': w0_base + nrm((DEPTH, 2, RWKV_W), 0.1),
        'rwkv_w_up': nrm((DEPTH, 2, DECAY_LORA, RWKV_W), 0.5 * DECAY_LORA ** -0.5),
        'rwkv_a0': nrm((DEPTH, 2, RWKV_W), 0.1),
        'rwkv_a_up': nrm((DEPTH, 2, AAA_LORA, RWKV_W), 0.5 * AAA_LORA ** -0.5),
        'rwkv_g_up': nrm((DEPTH, GATE_LORA, RWKV_W), GATE_LORA ** -0.5),
        'rwkv_k_k': 0.85 + nrm((DEPTH, RWKV_W), 0.02),
        'rwkv_k_a': 1.0 + nrm((DEPTH, RWKV_W), 0.02),
        'rwkv_r_k': -0.04 + nrm((DEPTH, RWKV_HEADS, RWKV_HEAD_DIM), 0.02),
        'rwkv_v0': 1.0 + nrm((DEPTH - 1, RWKV_W), 0.1),
        'rwkv_v_down': nrm((DEPTH - 1, D, MV_LORA), D ** -0.5),
        'rwkv_v_up': nrm((DEPTH - 1, MV_LORA, RWKV_W), MV_LORA ** -0.5),
        'rwkv_gn_w': 1.0 + nrm((DEPTH, RWKV_W), 0.05),
        'rwkv_gn_b': nrm((DEPTH, RWKV_W), 0.01),
        'w_branch_ret': nrm((DEPTH, RET_V_W, D), RET_V_W ** -0.5),
        'w_branch_att': nrm((DEPTH, ATT_Q_W, D), ATT_Q_W ** -0.5),
        'w_branch_rwkv': nrm((DEPTH, RWKV_W, D), RWKV_W ** -0.5),
        'w_out': nrm((DEPTH, D, D), D ** -0.5),
    }


def reference(x, c, ctx, c_ctx, ada_w, ada_b, norm_w, ffn1_w_in, ffn1_w_out, ffn2_w_in, ffn2_w_out,
              mix_w_in, ret_decay_logit, att_q_norm, att_k_norm, att_sink, rwkv_shift, rwkv_w0, rwkv_w_up,
              rwkv_a0, rwkv_a_up, rwkv_g_up, rwkv_k_k, rwkv_k_a, rwkv_r_k, rwkv_v0, rwkv_v_down, rwkv_v_up,
              rwkv_gn_w, rwkv_gn_b, w_branch_ret, w_branch_att, w_branch_rwkv, w_out):
    n_lat = x.shape[1]
    n_ctx = ctx.shape[1]
    rows = n_lat // GRID_W
    att_ang = axial_rope_angles(rows, ATT_HEAD_DIM)
    ret_ang_s = rope_angles_1d(jnp.arange(n_ctx), RET_QK_DIM, RET_ROPE_BASE)
    ret_ang_x = rope_angles_1d(n_ctx + jnp.arange(n_lat), RET_QK_DIM, RET_ROPE_BASE)
    cond_x = jax.nn.silu(c)[:, None, :]
    cond_s = jax.nn.silu(c_ctx)[None, None, :]
    s = ctx
    v_first = (None, None)
    for l in range(DEPTH):
        last = l == DEPTH - 1
        mod_x = (cond_x @ ada_w[l] + ada_b[l]).reshape(x.shape[0], 1, N_ADA, D_MODEL)
        mod_s = (cond_s @ ada_w[l] + ada_b[l]).reshape(1, 1, N_ADA, D_MODEL)
        x = ffn_sublayer(x, mod_x, 0, norm_w[l, 0], ffn1_w_in[l], ffn1_w_out[l])
        s = ffn_sublayer(s, mod_s, 0, norm_w[l, 0], ffn1_w_in[l], ffn1_w_out[l])
        p = {
            'w_in': mix_w_in[l], 'ret_decay_logit': ret_decay_logit[l],
            'q_norm': att_q_norm[l], 'k_norm': att_k_norm[l], 'att_sink': att_sink[l],
            'shift': rwkv_shift[l], 'w0': rwkv_w0[l], 'w_up': rwkv_w_up[l],
            'a0': rwkv_a0[l], 'a_up': rwkv_a_up[l], 'g_up': rwkv_g_up[l],
            'k_k': rwkv_k_k[l], 'k_a': rwkv_k_a[l], 'r_k': rwkv_r_k[l],
            'gn_w': rwkv_gn_w[l], 'gn_b': rwkv_gn_b[l],
            'w_ret': w_branch_ret[l], 'w_att': w_branch_att[l], 'w_rwkv': w_branch_rwkv[l],
            'w_out': w_out[l],
        }
        if l > 0:
            p['v0'] = rwkv_v0[l - 1]
            p['v_down'] = rwkv_v_down[l - 1]
            p['v_up'] = rwkv_v_up[l - 1]
        hx = modulate(rms_norm(x, norm_w[l, 1]), mod_x[:, :, 3], mod_x[:, :, 4])
        hs = modulate(rms_norm(s, norm_w[l, 1]), mod_s[:, :, 3], mod_s[:, :, 4])
        out_x, out_s, v_first = token_mixing(hx, hs, p, v_first, att_ang, ret_ang_x, ret_ang_s, not last)
        x = x + mod_x[:, :, 5] * out_x
        x = ffn_sublayer(x, mod_x, 6, norm_w[l, 2], ffn2_w_in[l], ffn2_w_out[l])
        if not last:
            s = s + mod_s[:, :, 5] * out_s
            s = ffn_sublayer(s, mod_s, 6, norm_w[l, 2], ffn2_w_in[l], ffn2_w_out[l])
    return x
```

```python
import numpy as np
from contextlib import ExitStack
import concourse.bass as bass
import concourse.mybir as mybir
from concourse.bass_utils import run_bass_kernel_spmd

F32 = mybir.dt.float32
BF16 = mybir.dt.bfloat16
AF = mybir.ActivationFunctionType
ALU = mybir.AluOpType
AX = mybir.AxisListType


class _Rec:
    def __init__(self):
        self.call = None

    def __getattr__(self, name):
        def f(*a, **k):
            self.call = (name, a, k)
            return self
        return f


def _capture(fn):
    r = _Rec()
    fn(r)
    name, a, k = r.call
    return lambda e: getattr(e, name)(*a, **k)


class Sched:
    ENG = ['pe', 'act', 'dve', 'pool', 'sp']

    def __init__(self, nc, stack, ndma=16):
        self.nc = nc
        self.prog = {e: [] for e in self.ENG}
        self.sems = {}
        self.count = {}
        for e in ['pe', 'act', 'dve', 'pool']:
            self.sems[e] = stack.enter_context(nc.semaphore('s_' + e))
            self.count[e] = 0
        self.ndma = ndma
        self.dma_rr = {}
        for q in ('sp', 'pool', 'act'):
            self.dma_rr[q] = 0
            for i in range(ndma):
                nm = 'd_%s%d' % (q, i)
                self.sems[nm] = stack.enter_context(nc.semaphore('s_' + nm))
                self.count[nm] = 0
        self.waited = {e: {} for e in self.ENG}
        self.state = {}
        self.keys = {}
        self.nops = 0

    def _st(self, slot):
        s = self.state.get(slot)
        if s is None:
            s = {'w': None, 'r': {}}
            self.state[slot] = s
        return s

    def _slots(self, tid, key):
        ks = self.keys.setdefault(tid, set())
        if key is None:
            return [(tid, k) for k in ks] + [(tid, None)]
        ks.add(key)
        return [(tid, key), (tid, None)]

    def _norm(self, accs):
        out = []
        for a in accs:
            if isinstance(a, tuple):
                out.append((a[0], a[1]))
            else:
                out.append((a, None))
        return out

    def _deps(self, eng, reads, writes):
        own = eng if eng in self.count else None
        need = {}

        def add(sem, val, kind):
            if sem == own:
                if eng == 'pe':
                    return
            if need.get(sem, 0) < val:
                need[sem] = val

        for (tid, key) in reads:
            is_psum = isinstance(tid, str) and tid.startswith('ps')
            for sl in self._slots(tid, key):
                s = self.state.get(sl)
                if s and s['w']:
                    add(s['w'][0], s['w'][1], 'RAW')
                if s and is_psum:
                    for sem, val in s['r'].items():
                        if sem != own:
                            add(sem, val, 'RAR')
        for (tid, key) in writes:
            for sl in self._slots(tid, key):
                s = self.state.get(sl)
                if s:
                    if s['w']:
                        add(s['w'][0], s['w'][1], 'WAW')
                    for sem, val in s['r'].items():
                        add(sem, val, 'WAR')
        waits = []
        wd = self.waited[eng]
        for sem, val in need.items():
            if wd.get(sem, 0) < val:
                wd[sem] = val
                waits.append((sem, val))
        return waits

    def _record(self, reads, writes, sem, val):
        for (tid, key) in reads:
            self._slots(tid, key)
            self._st((tid, key))['r'][sem] = val
        for (tid, key) in writes:
            if key is None:
                for k in list(self.keys.get(tid, ())):
                    self.state.pop((tid, k), None)
                self.keys[tid] = set()
            else:
                self._slots(tid, key)
            s = self._st((tid, key))
            s['w'] = (sem, val)
            s['r'] = {}

    def op(self, eng, fn, reads=(), writes=()):
        reads = self._norm(reads)
        writes = self._norm(writes)
        waits = self._deps(eng, reads, writes)
        self.count[eng] += 1
        val = self.count[eng]
        self.prog[eng].append((waits, _capture(fn), (eng, 1)))
        self._record(reads, writes, eng, val)
        self.nops += 1

    def dma(self, queue, fn, reads=(), writes=()):
        reads = self._norm(reads)
        writes = self._norm(writes)
        nm = 'd_%s%d' % (queue, self.dma_rr[queue])
        self.dma_rr[queue] = (self.dma_rr[queue] + 1) % self.ndma
        waits = self._deps(queue, reads, writes)
        wd = self.waited[queue]
        if wd.get(nm, 0) < self.count[nm]:
            wd[nm] = self.count[nm]
            waits.append((nm, self.count[nm]))
        self.count[nm] += 16
        val = self.count[nm]
        self.prog[queue].append((waits, _capture(fn), (nm, 16)))
        self._record(reads, writes, nm, val)
        self.nops += 1

    def barrier(self):
        for e in self.ENG:
            waits = []
            wd = self.waited[e]
            for sem, cnt in self.count.items():
                if cnt > 0 and wd.get(sem, 0) < cnt:
                    wd[sem] = cnt
                    waits.append((sem, cnt))
            if waits:
                self.prog[e].append((waits, None, None))
        self.state = {}
        self.keys = {}

    def emit(self):
        nc = self.nc
        self.barrier()
        with nc.Block() as block:
            def run(name):
                def f(e):
                    for waits, fn, inc in self.prog[name]:
                        for (s, v) in waits:
                            e.wait_ge(self.sems[s], v)
                        if fn is not None:
                            ins = fn(e)
                            ins.then_inc(self.sems[inc[0]], inc[1])
                return f
            block.tensor(run('pe'))
            block.scalar(run('act'))
            block.vector(run('dve'))
            block.gpsimd(run('pool'))
            block.sync(run('sp'))


NT = 2304
NCTX = 256
NLAT = 2048
D = 1024
KC = 8
DFF = 2816
NFC = 22
NIN = 8832
BLOCKS = [(0, 256), (256, 768), (768, 1280), (1280, 1792), (1792, 2304)]
SUPER = [[0, 1], [2], [3], [4]]
EPS = 1e-6


RW_INPUT_SHAPES = [
    ("rw_shT", [2, 128, 15, 3]), ("rw_w0T", [2, 128, 2, 4]), ("rw_a0T", [2, 128, 2, 4]), ("rw_vecT", [2, 128, 5, 4]),
    ("rw_v0T", [128, 4]), ("rwkv_w_up", [2, 2, 64, 512]), ("rwkv_a_up", [2, 2, 64, 512]), ("rwkv_g_up", [2, 128, 512]),
    ("rwkv_v_down", [1, 1024, 32]), ("rwkv_v_up", [1, 32, 512]),
]


def rw_host_consts():
    return {}


def rw_prep_shared(inp):
    f = lambda a: np.ascontiguousarray(a, dtype=np.float32)
    sh = {}
    sh['rw_shT'] = f(inp['rwkv_shift'].reshape(2, 3, 15, 128).transpose(0, 3, 2, 1))
    sh['rw_w0T'] = f(inp['rwkv_w0'].reshape(2, 2, 4, 128).transpose(0, 3, 1, 2))
    sh['rw_a0T'] = f(inp['rwkv_a0'].reshape(2, 2, 4, 128).transpose(0, 3, 1, 2))
    vec = np.stack([inp['rwkv_k_k'], inp['rwkv_k_a'], inp['rwkv_r_k'].reshape(2, 512), inp['rwkv_gn_w'], inp['rwkv_gn_b']], axis=1)
    sh['rw_vecT'] = f(vec.reshape(2, 5, 4, 128).transpose(0, 3, 1, 2))
    sh['rw_v0T'] = f(inp['rwkv_v0'][0].reshape(4, 128).T)
    for k in ['rwkv_w_up', 'rwkv_a_up', 'rwkv_g_up', 'rwkv_v_down', 'rwkv_v_up']:
        sh[k] = f(inp[k])
    return sh


class Ctx:
    pass


AW = 50000
XTW = KC * NT


def build(stop_after=None, taps=(), ret_stop=None):
    nc = bass.Bass("TRN2", target_bir_lowering=False)
    g = Ctx()
    g.nc = nc
    g.taps = {}
    g.ret_stop = ret_stop

    def din(name, shape):
        return nc.dram_tensor(name, list(shape), F32, kind="ExternalInput").ap()

    def dout(name, shape, dt=F32):
        return nc.dram_tensor(name, list(shape), dt, kind="ExternalOutput").ap()

    I = {}
    for name, shape in INPUT_SHAPES:
        I[name] = din(name, shape)
    outT = dout("outT", [D, NLAT])
    g.I = I
    yT_d = nc.dram_tensor("yT_d", [2048, NT], BF16).ap()
    xsp_d = nc.dram_tensor("xsp_d", [128, KC, NT], F32).ap()
    vf_d = nc.dram_tensor("vf_d", [512, NT], F32).ap()
    scrT = nc.dram_tensor("scrT", [2, 36, 64, 8, 4, 64], BF16).ap()
    scrM = nc.dram_tensor("scrM", [2, 2, NT, 512], BF16).ap()
    vtm_d = nc.dram_tensor("vtm_d", [NT, 512], BF16).ap()
    pcs_d = nc.dram_tensor("pcs_d", [2, 512, 36], F32).ap()
    bon_d = nc.dram_tensor("bon_d", [512, NT], F32).ap()
    g_d = nc.dram_tensor("g_d", [512, NT], F32).ap()
    wg2_d = nc.dram_tensor("wg2_d", [2, 4, 128, KC, 3, 256], BF16).ap()
    wb2_d = nc.dram_tensor("wb2_d", [2, 4, 128, 16, 256], BF16).ap()
    wo2_d = nc.dram_tensor("wo2_d", [2, 4, 128, KC, 256], BF16).ap()

    with ExitStack() as st:
        S = Sched(nc, st)
        g.S = S
        uid = [0]

        def T(stack, name, shape, dt):
            uid[0] += 1
            return stack.enter_context(nc.sbuf_tensor("%s_%d" % (name, uid[0]), list(shape), dt))

        def PS(stack, name, shape, dt):
            uid[0] += 1
            return stack.enter_context(nc.psum_tensor("%s_%d" % (name, uid[0]), list(shape), dt))

        arena = T(st, "arena", [128, AW], F32)

        class Bump:
            def __init__(self, ranges):
                self.ranges = [list(r) for r in ranges]

            def alloc(self, shape, dt):
                n = 1
                for s_ in shape[1:]:
                    n *= s_
                words = n if dt == F32 else (n + 1) // 2
                for r in self.ranges:
                    if r[0] + words <= r[1]:
                        off = r[0]
                        r[0] += words
                        break
                else:
                    raise RuntimeError("arena overflow %s %s" % (shape, self.ranges))
                ap = arena[:shape[0], off:off + words]
                if dt != F32:
                    ap = ap.bitcast(dt)[:, :n]
                if len(shape) > 2:
                    names = ["d%d" % i for i in range(len(shape) - 1)]
                    kw = {nm: s_ for nm, s_ in zip(names, shape[1:])}
                    ap = ap.rearrange("p (%s) -> p %s" % (" ".join(names), " ".join(names)), **kw)
                return ap
        g.Bump = Bump

        XT = arena[:, 0:XTW].rearrange("p (k n) -> p k n", k=KC)
        g.XT = XT

        cond = T(st, "cond", [128, KC, 2], F32)
        modT = T(st, "modT", [128, 2, 72, 2], F32)
        Amod = T(st, "Amod", [128, 2, 3, KC, 2], F32)
        Gmod = T(st, "Gmod", [128, 2, 3, KC, 2], F32)
        normw = T(st, "normw", [128, 2, 3, KC], F32)
        adab = T(st, "adab", [128, 2, 72], F32)
        ones_bf = T(st, "ones_bf", [128, 128], BF16)
        ident = T(st, "ident", [128, 128], F32)
        ident_bf = T(st, "ident_bf", [128, 128], BF16)
        epst = T(st, "epst", [128, 1], F32)
        cmask = T(st, "cmask", [128, 7, 128], F32)
        g.cmask = cmask

        def blk_of(c0):
            for i, (a, b) in enumerate(BLOCKS):
                if a <= c0 < b:
                    return i
            raise ValueError(c0)

        for bi, (c0, c1) in enumerate(BLOCKS):
            S.dma('sp', lambda e, c0=c0, c1=c1: e.dma_start(
                out=XT[:, :, c0:c1], in_=I['xT'][:, c0:c1].rearrange("(k p) n -> p k n", p=128)),
                writes=[('XT', bi)])
        S.dma('sp', lambda e: e.dma_start(out=cond[:], in_=I['condT']), writes=['cond'])
        S.dma('sp', lambda e: e.dma_start(out=normw[:], in_=I['norm_wT']), writes=['normw'])
        S.dma('sp', lambda e: e.dma_start(out=adab[:], in_=I['ada_bT']), writes=['adab'])
        S.dma('sp', lambda e: e.dma_start(out=cmask[:], in_=I['cmask']), writes=['cmask'])
        S.op('pool', lambda e: e.memset(ones_bf[:], 1.0), writes=['ones_bf'])
        S.op('pool', lambda e: e.memset(epst[:], EPS), writes=['epst'])
        S.op('pool', lambda e: e.memset(ident[:], 1.0), writes=['ident'])
        S.op('pool', lambda e: e.affine_select(out=ident[:], in_=ident[:], pattern=[[-1, 128]],
                                               compare_op=ALU.is_equal, fill=0.0, base=0, channel_multiplier=1),
             reads=['ident'], writes=['ident'])
        S.op('dve', lambda e: e.tensor_copy(out=ident_bf[:], in_=ident[:]), reads=['ident'], writes=['ident_bf'])
        S.op('act', lambda e: e.activation(out=cond[:], in_=cond[:], func=AF.Silu), reads=['cond'], writes=['cond'])

        def make_banks(ph, nf32=8):
            banks = [PS(ph, "ps%d" % i, [128, 512], F32) for i in range(nf32)]
            g.bank_i = 0

            def bank():
                i = g.bank_i
                g.bank_i = (i + 1) % nf32
                return banks[i], "ps%d" % i
            g.bank = bank
            return bank

        with ExitStack() as ph:
            bp = Bump([(XTW, AW)])
            wa = [bp.alloc([128, KC, 512], F32) for i in range(3)]
            psm = PS(ph, "psmod", [128, 72, 2], F32)
            n = 0
            for l in range(2):
                for slab in range(18):
                    w = wa[n % 3]
                    wn = "adaw%d" % (n % 3)
                    n += 1
                    S.dma('sp', lambda e, w=w, l=l, slab=slab: e.dma_start(
                        out=w, in_=I['ada_w'][l, :, slab * 512:(slab + 1) * 512].rearrange("(k p) n -> p k n", p=128)),
                        writes=[wn])
                    for j in range(4):
                        ch = slab * 4 + j
                        for k in range(KC):
                            S.op('pe', lambda e, w=w, j=j, k=k, ch=ch: e.matmul(
                                psm[:, ch, :], lhsT=w[:, k, j * 128:(j + 1) * 128], rhs=cond[:, k, :],
                                start=(k == 0), stop=(k == KC - 1)),
                                reads=[wn, 'cond'], writes=['psmod'])
                S.op('dve', lambda e, l=l: e.tensor_tensor(
                    out=modT[:, l], in0=psm[:], in1=adab[:, l].unsqueeze(2).to_broadcast([128, 72, 2]), op=ALU.add),
                    reads=['psmod', 'adab'], writes=['modT'])
                for sub in range(3):
                    ms = 3 * sub + 1
                    mg = 3 * sub + 2
                    S.op('dve', lambda e, l=l, sub=sub, ms=ms: e.scalar_tensor_tensor(
                        out=Amod[:, l, sub], in0=modT[:, l, ms * 8:(ms + 1) * 8, :], scalar=1.0,
                        in1=normw[:, l, sub].unsqueeze(2).to_broadcast([128, KC, 2]), op0=ALU.add, op1=ALU.mult),
                        reads=['modT', 'normw'], writes=['Amod'])
                    S.op('dve', lambda e, l=l, sub=sub, mg=mg: e.tensor_scalar(
                        out=Gmod[:, l, sub], in0=modT[:, l, mg * 8:(mg + 1) * 8, :],
                        scalar1=(1.0 if sub == 1 else 0.5), scalar2=None, op0=ALU.mult),
                        reads=['modT'], writes=['Gmod'])
            S.barrier()

        def tap(name, ap_fn, shape, reads, dt=F32):
            if name in taps:
                t = dout("tap_" + name, shape, dt)
                g.taps[name] = t
                S.dma('sp', lambda e: e.dma_start(out=t, in_=ap_fn()), reads=reads)

        tap('modT', lambda: modT[:], [128, 2, 72, 2], ['modT'])

        def norm_mod(ph_t, l, sub, c0, c1, hT, hname, hkey, off):
            n = c1 - c0
            bi = blk_of(c0)
            which = 1 if c0 < NCTX else 0
            sq, rstd, tmps = ph_t['sq'], ph_t['rstd'], ph_t['tmp']
            S.op('act', lambda e: e.activation(out=sq[:, :, :n], in_=XT[:, :, c0:c1], func=AF.Square),
                 reads=[('XT', bi)], writes=['sq'])
            ps, pn = g.bank()
            for k in range(KC):
                S.op('pe', lambda e, k=k: e.matmul(ps[:, :n], lhsT=ones_bf[:], rhs=sq[:, k, :n],
                                                   start=(k == 0), stop=(k == KC - 1)),
                     reads=['sq', 'ones_bf'], writes=[pn])
            S.op('act', lambda e: e.activation(out=rstd[:, :n], in_=ps[:, :n], func=AF.Sqrt,
                                               bias=epst[:, 0:1], scale=1.0 / D),
                 reads=[pn, 'epst'], writes=['rstd'])
            S.op('dve', lambda e: e.reciprocal(out=rstd[:, :n], in_=rstd[:, :n]), reads=['rstd'], writes=['rstd'])
            msh = 3 * sub
            for k in range(KC):
                tmp = tmps[k % 2]
                tn = 'nm_tmp%d' % (k % 2)
                S.op('dve', lambda e, k=k, tmp=tmp: e.scalar_tensor_tensor(
                    out=tmp[:, :n], in0=XT[:, k, c0:c1], scalar=Amod[:, l, sub, k, which:which + 1],
                    in1=rstd[:, :n], op0=ALU.mult, op1=ALU.mult),
                    reads=[('XT', bi), 'Amod', 'rstd'], writes=[tn])
                S.op('act', lambda e, k=k, tmp=tmp: e.activation(
                    out=hT[:, k, off:off + n], in_=tmp[:, :n], func=AF.Identity,
                    bias=modT[:, l, msh * 8 + k, which:which + 1], scale=1.0),
                    reads=[tn, 'modT'], writes=[(hname, hkey)])

        def norm_tiles(bp):
            return {'sq': bp.alloc([128, KC, 512], BF16), 'rstd': bp.alloc([128, 512], F32),
                    'tmp': [bp.alloc([128, 512], F32), bp.alloc([128, 512], F32)]}

        def ffn(l, sub, w_in, w_out, blocks_sel=None):
            with ExitStack() as ph:
                bank = make_banks(ph, 8)
                bp = Bump([(XTW, AW)])
                nt = norm_tiles(bp)
                hT = bp.alloc([128, KC, 768], BF16)
                actT = bp.alloc([128, NFC, 768], BF16)
                wi = [bp.alloc([128, KC, 2, 512], BF16) for i in range(2)]
                wo = [bp.alloc([128, NFC, 256], BF16) for i in range(2)]
                sg = [bp.alloc([128, 512], F32) for i in range(2)]
                cnt = {'wi': 0, 'wo': 0, 'sg': 0}
                sbs = []
                for sb0 in SUPER:
                    sb = [b for b in sb0 if (blocks_sel is None or b in blocks_sel)]
                    if not sb:
                        continue
                    offs = []
                    o = 0
                    for b in sb:
                        offs.append(o)
                        o += BLOCKS[b][1] - BLOCKS[b][0]
                    sbs.append((sb, offs))

                def do_norm(i):
                    sb, offs = sbs[i]
                    for lb, b in enumerate(sb):
                        norm_mod(nt, l, sub, BLOCKS[b][0], BLOCKS[b][1], hT, 'hT', lb, offs[lb])

                def do_win(i):
                    sb, offs = sbs[i]
                    for gi in range(6):
                        nfc = min(4, NFC - gi * 4)
                        w = wi[cnt['wi'] % 2]
                        wn = "wi%d" % (cnt['wi'] % 2)
                        cnt['wi'] += 1
                        for half in range(2):
                            cs = half * DFF + gi * 512
                            S.dma('pool', lambda e: e.dma_start(
                                out=w[:, :, half, :nfc * 128],
                                in_=w_in[:, cs:cs + nfc * 128].rearrange("(k p) n -> p k n", p=128)),
                                writes=[(wn, half)])
                        for j in range(nfc):
                            fc = gi * 4 + j
                            for lb, b in enumerate(sb):
                                n = BLOCKS[b][1] - BLOCKS[b][0]
                                off = offs[lb]
                                psg, png = bank()
                                psu, pnu = bank()
                                for half, (pp, pnn) in enumerate(((psg, png), (psu, pnu))):
                                    for k in range(KC):
                                        S.op('pe', lambda e: e.matmul(
                                            pp[:, :n], lhsT=w[:, k, half, j * 128:(j + 1) * 128], rhs=hT[:, k, off:off + n],
                                            start=(k == 0), stop=(k == KC - 1)),
                                            reads=[(wn, half), ('hT', lb)], writes=[pnn])
                                s_ = sg[cnt['sg'] % 2]
                                sn = "sg%d" % (cnt['sg'] % 2)
                                cnt['sg'] += 1
                                S.op('act', lambda e: e.activation(out=s_[:, :n], in_=psg[:, :n], func=AF.Silu),
                                     reads=[png], writes=[sn])
                                S.op('dve', lambda e: e.tensor_tensor(
                                    out=actT[:, fc, off:off + n], in0=psu[:, :n], in1=s_[:, :n], op=ALU.mult),
                                    reads=[pnu, sn], writes=[('actT', (fc, lb))])

                def do_wout(i):
                    sb, offs = sbs[i]
                    for dp in range(4):
                        w = wo[cnt['wo'] % 2]
                        wn = "wo%d" % (cnt['wo'] % 2)
                        cnt['wo'] += 1
                        S.dma('pool', lambda e: e.dma_start(
                            out=w, in_=w_out[:, dp * 256:(dp + 1) * 256].rearrange("(f p) n -> p f n", p=128)),
                            writes=[wn])
                        for dl in range(2):
                            dc = dp * 2 + dl
                            for lb, b in enumerate(sb):
                                c0, c1 = BLOCKS[b]
                                n = c1 - c0
                                off = offs[lb]
                                which = 1 if c0 < NCTX else 0
                                ps, pn = bank()
                                for f in range(NFC):
                                    S.op('pe', lambda e: e.matmul(
                                        ps[:, :n], lhsT=w[:, f, dl * 128:(dl + 1) * 128], rhs=actT[:, f, off:off + n],
                                        start=(f == 0), stop=(f == NFC - 1)),
                                        reads=[wn, ('actT', (f, lb))], writes=[pn])
                                S.op('dve', lambda e: e.scalar_tensor_tensor(
                                    out=XT[:, dc, c0:c1], in0=ps[:, :n], scalar=Gmod[:, l, sub, dc, which:which + 1],
                                    in1=XT[:, dc, c0:c1], op0=ALU.mult, op1=ALU.add),
                                    reads=[pn, 'Gmod', ('XT', b)], writes=[('XT', b)])

                do_norm(0)
                for i in range(len(sbs)):
                    do_win(i)
                    if i + 1 < len(sbs):
                        do_norm(i + 1)
                    do_wout(i)
                S.barrier()

        def store_out():
            for bi, (c0, c1) in enumerate(BLOCKS):
                if c0 < NCTX:
                    continue
                S.dma('sp', lambda e, c0=c0, c1=c1: e.dma_start(
                    out=outT[:, c0 - NCTX:c1 - NCTX].rearrange("(k p) n -> p k n", p=128), in_=XT[:, :, c0:c1]),
                    reads=[('XT', bi)])

        def tapX(name):
            tap(name, lambda: XT, [128, KC, NT], ['XT'])

        def tapY(name, r0, r1):
            tap(name, lambda: yT_d[r0:r1, :], [r1 - r0, NT], ['yT_d'], BF16)
        HXW = KC * NT // 2

        def bc(ap, axis, shape):
            return ap.unsqueeze(axis).to_broadcast(list(shape))

        def compute_hx(l):
            hxT = Bump([(XTW, XTW + HXW)]).alloc([128, KC, NT], BF16)
            with ExitStack() as ph:
                make_banks(ph, 8)
                bp = Bump([(XTW + HXW, AW)])
                nt = norm_tiles(bp)
                for bi, (c0, c1) in enumerate(BLOCKS):
                    norm_mod(nt, l, 1, c0, c1, hxT, 'hxT', bi, c0)
                for bi, (c0, c1) in enumerate(BLOCKS):
                    S.dma('sp', lambda e, c0=c0, c1=c1: e.dma_start(out=xsp_d[:, :, c0:c1], in_=XT[:, :, c0:c1]),
                          reads=[('XT', bi)], writes=[('xsp', bi)])
                S.barrier()
            return hxT

        def attention(l, W, hxT, qtiles):
            with ExitStack() as ph:
                bank = make_banks(ph, 6)
                pstT = [PS(ph, "pst%d" % i, [128, 1024], BF16) for i in range(2)]
                bp = Bump([(0, XTW), (XTW + HXW, AW)])
                wA = bp.alloc([128, KC, 768], BF16)
                S.dma('pool', lambda e: e.dma_start(out=wA, in_=W[:, 3072:3840].rearrange("(k p) n -> p k n", p=128)),
                      writes=['wA'])
                qw = bp.alloc([128, 64], F32)
                kw = bp.alloc([128, 64], F32)
                esink = bp.alloc([128, 8], F32)
                acs = bp.alloc([128, 16, 2, 64], F32)
                S.dma('sp', lambda e: e.dma_start(out=qw, in_=I['att_qn'][l:l + 1, :].partition_broadcast(128)), writes=['qw'])
                S.dma('sp', lambda e: e.dma_start(out=kw, in_=I['att_kn'][l:l + 1, :].partition_broadcast(128)), writes=['kw'])
                S.dma('sp', lambda e: e.dma_start(out=esink, in_=I['att_sink'][l:l + 1, :].partition_broadcast(128)), writes=['esink'])
                S.dma('sp', lambda e: e.dma_start(out=acs, in_=I['attcs']), writes=['acs'])
                S.op('act', lambda e: e.activation(out=esink, in_=esink, func=AF.Exp), reads=['esink'], writes=['esink'])
                vext = bp.alloc([128, 18, 2, 65], BF16)
                S.op('pool', lambda e: e.memset(vext, 1.0), writes=['vext'])
                qkT = bp.alloc([128, 18, 5, 128], BF16)
                mge = bp.alloc([128, 128], BF16)
                mle = bp.alloc([128, 128], BF16)
                S.op('dve', lambda e: e.tensor_copy(out=mge, in_=cmask[:, 5, :]), reads=['cmask'], writes=['mge'])
                S.op('dve', lambda e: e.tensor_copy(out=mle, in_=cmask[:, 6, :]), reads=['cmask'], writes=['mle'])
                qsq = bp.alloc([128, 512], F32)
                ss = bp.alloc([128, 8], F32)
                qn = bp.alloc([128, 8, 64], F32)
                t1 = bp.alloc([128, 8, 64], F32)
                t2 = bp.alloc([128, 8, 64], F32)
                qr = bp.alloc([128, 10, 64], BF16)
                def a1_proj(t):
                    bi = blk_of(t * 128)
                    c0, c1 = t * 128, (t + 1) * 128
                    psq, pnq = bank()
                    pkv, pnk = bank()
                    for k in range(KC):
                        S.op('pe', lambda e, k=k, psq=psq, c0=c0, c1=c1: e.matmul(
                            psq[:, :512], lhsT=hxT[:, k, c0:c1], rhs=wA[:, k, 0:512], start=(k == 0), stop=(k == KC - 1)),
                            reads=[('hxT', bi), 'wA'], writes=[pnq])
                    for k in range(KC):
                        S.op('pe', lambda e, k=k, pkv=pkv, c0=c0, c1=c1: e.matmul(
                            pkv[:, :256], lhsT=hxT[:, k, c0:c1], rhs=wA[:, k, 512:768], start=(k == 0), stop=(k == KC - 1)),
                            reads=[('hxT', bi), 'wA'], writes=[pnk])
                    return psq, pnq, pkv, pnk

                def a1_rest(t, psq, pnq, pkv, pnk):
                    bi = blk_of(t * 128)
                    c0, c1 = t * 128, (t + 1) * 128
                    S.op('act', lambda e, pkv=pkv, t=t: e.activation(
                        out=vext[:, t, :, 0:64], in_=pkv[:, 128:256].rearrange("p (g d) -> p g d", g=2), func=AF.Copy),
                        reads=[pnk], writes=[('vext', t)])
                    for (src, pn_, nh, wt, wtn, off) in ((psq[:, 0:512], pnq, 8, qw, 'qw', 0), (pkv[:, 0:128], pnk, 2, kw, 'kw', 8)):
                        src3 = src.rearrange("p (h d) -> p h d", h=nh)
                        S.op('act', lambda e, src=src, nh=nh: e.activation(out=qsq[:, :nh * 64], in_=src, func=AF.Square),
                             reads=[pn_], writes=['qsq'])
                        S.op('dve', lambda e, nh=nh: e.tensor_reduce(
                            out=ss[:, :nh], in_=qsq[:, :nh * 64].rearrange("p (h d) -> p h d", h=nh), axis=AX.X, op=ALU.add),
                            reads=['qsq'], writes=['ss'])
                        S.op('act', lambda e, nh=nh: e.activation(out=ss[:, :nh], in_=ss[:, :nh], func=AF.Sqrt,
                                                                  bias=epst[:, 0:1], scale=1.0 / 64),
                             reads=['ss', 'epst'], writes=['ss'])
                        S.op('dve', lambda e, nh=nh: e.reciprocal(out=ss[:, :nh], in_=ss[:, :nh]), reads=['ss'], writes=['ss'])
                        S.op('dve', lambda e, nh=nh, src3=src3: e.tensor_tensor(
                            out=qn[:, :nh], in0=src3, in1=bc(ss[:, :nh], 2, [128, nh, 64]), op=ALU.mult),
                            reads=[pn_, 'ss'], writes=['qn'])
                        S.op('pool', lambda e, nh=nh, wt=wt: e.tensor_tensor(
                            out=qn[:, :nh], in0=qn[:, :nh], in1=bc(wt, 1, [128, nh, 64]), op=ALU.mult),
                            reads=['qn', wtn], writes=['qn'])
                        if t >= 2:
                            xi = t - 2
                            S.op('dve', lambda e, nh=nh, xi=xi: e.tensor_tensor(
                                out=t1[:, :nh], in0=qn[:, :nh], in1=bc(acs[:, xi, 0, :], 1, [128, nh, 64]), op=ALU.mult),
                                reads=['qn', 'acs'], writes=['t1'])
                            S.op('pool', lambda e, nh=nh, xi=xi: e.tensor_tensor(
                                out=t2[:, :nh, 0:32], in0=qn[:, :nh, 32:64], in1=bc(acs[:, xi, 1, 0:32], 1, [128, nh, 32]), op=ALU.mult),
                                reads=['qn', 'acs'], writes=[('t2', 0)])
                            S.op('pool', lambda e, nh=nh, xi=xi: e.tensor_tensor(
                                out=t2[:, :nh, 32:64], in0=qn[:, :nh, 0:32], in1=bc(acs[:, xi, 1, 32:64], 1, [128, nh, 32]), op=ALU.mult),
                                reads=['qn', 'acs'], writes=[('t2', 1)])
                            if nh == 8:
                                S.op('dve', lambda e: e.tensor_tensor(
                                    out=qr[:, 0:8].rearrange("p (j g) d -> p g j d", g=2),
                                    in0=t1.rearrange("p (g j) d -> p g j d", g=2),
                                    in1=t2.rearrange("p (g j) d -> p g j d", g=2), op=ALU.add),
                                    reads=['t1', 't2'], writes=['qr'])
                            else:
                                S.op('dve', lambda e, nh=nh, off=off: e.tensor_tensor(
                                    out=qr[:, off:off + nh], in0=t1[:, :nh], in1=t2[:, :nh], op=ALU.add),
                                    reads=['t1', 't2'], writes=['qr'])
                        else:
                            if nh == 8:
                                S.op('dve', lambda e: e.tensor_copy(
                                    out=qr[:, 0:8].rearrange("p (j g) d -> p g j d", g=2),
                                    in_=qn.rearrange("p (g j) d -> p g j d", g=2)),
                                    reads=['qn'], writes=['qr'])
                            else:
                                S.op('dve', lambda e, nh=nh, off=off: e.tensor_copy(out=qr[:, off:off + nh], in_=qn[:, :nh]),
                                     reads=['qn'], writes=['qr'])
                    pst = pstT[t % 2]
                    ptn = "pst%d" % (t % 2)
                    for j in range(4):
                        S.op('pe', lambda e, j=j, pst=pst: e.transpose(
                            out=pst[:, j * 128:(j + 1) * 128], in_=qr[:, 2 * j:2 * j + 2, :].rearrange("p a d -> p (a d)"), identity=ident_bf[:]),
                            reads=['qr', 'ident_bf'], writes=[ptn])
                    S.op('pe', lambda e, pst=pst: e.transpose(
                        out=pst[:, 512:640], in_=qr[:, 8:10, :].rearrange("p a d -> p (a d)"), identity=ident_bf[:]),
                        reads=['qr', 'ident_bf'], writes=[ptn])
                    S.op('act', lambda e, pst=pst, t=t: e.activation(
                        out=qkT[:, t], in_=pst[:, 0:640].rearrange("p (a b) -> p a b", a=5), func=AF.Copy),
                        reads=[ptn], writes=[('qkT', t)])

                nxp = a1_proj(0)
                for t in range(18):
                    cup = nxp
                    if t + 1 < 18:
                        nxp = a1_proj(t + 1)
                    a1_rest(t, *cup)
                pTs = [bp.alloc([128, 512], BF16) for i in range(10)]
                npt = 0
                den = bp.alloc([128, 4], F32)
                atoks = [bp.alloc([128, 8, 64], BF16) for i in range(2)]
                astage = [bp.alloc([128, 4, 512], BF16) for i in range(2)]
                nst = 0
                ast2 = {'npt': 0, 'nst': 0}

                def at_scores(qi, n, g_):
                    if n >= 2:
                        keys = ([(n - 1, 'prev')] if n - 1 >= 2 else []) + [(n, 'self')] + \
                               ([(n + 1, 'next')] if n + 1 <= 17 else []) + [(0, 'ctx'), (1, 'ctx')]
                    else:
                        keys = [(0, 'ctx'), (1, 'ctx')]
                    r0, r1 = g_ * 64, (g_ + 1) * 64
                    cur = []
                    for (kt, kind) in keys:
                        ps, pn = bank()
                        S.op('pe', lambda e: e.matmul(
                            ps[:, :512], lhsT=qkT[r0:r1, kt, 4, :], rhs=qkT[r0:r1, n, 0:4, :].rearrange("p a b -> p (a b)"), start=True, stop=True),
                            reads=[('qkT', kt), ('qkT', n)], writes=[pn])
                        pT = pTs[ast2['npt'] % 10]
                        ptn = "pT%d" % (ast2['npt'] % 10)
                        ast2['npt'] += 1
                        S.op('act', lambda e: e.activation(out=pT, in_=ps[:, :512], func=AF.Exp, scale=0.125), reads=[pn], writes=[ptn])
                        if kind in ('prev', 'next'):
                            mk, mkn = (mge, 'mge') if kind == 'prev' else (mle, 'mle')
                            S.op('dve', lambda e: e.tensor_tensor(
                                out=pT.rearrange("p (h q) -> p h q", h=4), in0=pT.rearrange("p (h q) -> p h q", h=4),
                                in1=bc(mk, 1, [128, 4, 128]), op=ALU.mult), reads=[ptn, mkn], writes=[ptn])
                        cur.append((pT, ptn, kt))
                    return cur

                def at_pv(qi, n, g_, cur):
                    atok = atoks[qi % 2]
                    an = "atok%d" % (qi % 2)
                    po, pon = bank()
                    for hh in range(4):
                        for i, (pT, ptn, kt) in enumerate(cur):
                            S.op('pe', lambda e, hh=hh, pT=pT, kt=kt, i=i: e.matmul(
                                po[:, hh * 65:(hh + 1) * 65], lhsT=pT[:, hh * 128:(hh + 1) * 128], rhs=vext[:, kt, g_, :],
                                start=(i == 0), stop=(i == len(cur) - 1)),
                                reads=[ptn, ('vext', kt)], writes=[pon])
                    po3 = po[:, 0:260].rearrange("p (h d) -> p h d", h=4)
                    S.op('dve', lambda e: e.tensor_tensor(out=den, in0=po3[:, :, 64], in1=esink[:, g_ * 4:(g_ + 1) * 4], op=ALU.add),
                         reads=[pon, 'esink'], writes=['den'])
                    S.op('dve', lambda e: e.reciprocal(out=den, in_=den), reads=['den'], writes=['den'])
                    S.op('dve', lambda e: e.tensor_tensor(
                        out=atok[:, g_ * 4:(g_ + 1) * 4, :], in0=po3[:, :, 0:64], in1=bc(den, 2, [128, 4, 64]), op=ALU.mult),
                        reads=[pon, 'den'], writes=[(an, g_)])
                    if g_ == 1:
                        pst = pstT[qi % 2]
                        ptn2 = "pst%d" % (qi % 2)
                        for c in range(4):
                            S.op('pe', lambda e, c=c: e.transpose(
                                out=pst[:, c * 128:(c + 1) * 128], in_=atok[:, 2 * c:2 * c + 2, :].rearrange("p a d -> p (a d)"), identity=ident_bf[:]),
                                reads=[an, 'ident_bf'], writes=[ptn2])
                        slot = qi % 4
                        ast = astage[ast2['nst'] % 2]
                        asn = "astage%d" % (ast2['nst'] % 2)
                        S.op('act', lambda e: e.activation(
                            out=ast[:, :, slot * 128:(slot + 1) * 128], in_=pst[:, 0:512].rearrange("p (a b) -> p a b", a=4), func=AF.Copy),
                            reads=[ptn2], writes=[(asn, slot)])
                        if slot == 3 or qi == len(qtiles) - 1:
                            n0 = qtiles[qi - slot]
                            ncol = (slot + 1) * 128
                            S.dma('sp', lambda e: e.dma_start(
                                out=yT_d[1024:1536, n0 * 128:n0 * 128 + ncol].rearrange("(c p) n -> p c n", p=128),
                                in_=ast[:, :, 0:ncol]), reads=[asn], writes=['yT_d'])
                            ast2['nst'] += 1

                work = [(qi, n, g_) for qi, n in enumerate(qtiles) for g_ in range(2)]
                nxa = at_scores(*work[0])
                for wi, w_ in enumerate(work):
                    cua = nxa
                    if wi + 1 < len(work):
                        nxa = at_scores(*work[wi + 1])
                    at_pv(w_[0], w_[1], w_[2], cua)
                S.barrier()

        def retention(l, W, hxT):
            with ExitStack() as ph:
                bank = make_banks(ph, 6)
                pstT = [PS(ph, "pst%d" % i, [128, 1024], BF16) for i in range(2)]
                bp = Bump([(0, XTW), (XTW + HXW, AW)])
                lg = bp.alloc([128, 16], F32)
                pos4 = bp.alloc([128, 4], F32)
                rcs = bp.alloc([128, 18, 2, 64], F32)
                S.dma('sp', lambda e: e.dma_start(out=lg, in_=I['ret_lg'][l:l + 1, :].partition_broadcast(128)), writes=['lg'])
                S.dma('sp', lambda e: e.dma_start(out=pos4, in_=I['pos4']), writes=['pos4'])
                S.dma('sp', lambda e: e.dma_start(out=rcs, in_=I['retcs']), writes=['rcs'])
                S.op('act', lambda e: e.activation(out=lg, in_=lg, func=AF.Sigmoid), reads=['lg'], writes=['lg'])
                S.op('act', lambda e: e.activation(out=lg, in_=lg, func=AF.Ln), reads=['lg'], writes=['lg'])
                ZX = bp.alloc([128, 4, 8], F32)
                for i, (d0, pc) in enumerate(((0, 0), (8, 1), (0, 2), (8, 3))):
                    S.op('act', lambda e, i=i, d0=d0, pc=pc: e.activation(
                        out=ZX[:, i, :], in_=lg[:, d0:d0 + 8], func=AF.Exp, scale=pos4[:, pc:pc + 1]),
                        reads=['lg', 'pos4'], writes=['ZX'])
                S.op('dve', lambda e: e.tensor_scalar(out=ZX[:, 0:2, :], in0=ZX[:, 0:2, :], scalar1=0.125, scalar2=None, op0=ALU.mult),
                     reads=['ZX'], writes=['ZX'])
                XIb = bp.alloc([128, 8, 2, 64], F32)
                ZEb = bp.alloc([128, 8, 2, 64], F32)
                S.op('dve', lambda e: e.tensor_copy(out=XIb, in_=ZX[:, 2:4, :].rearrange("p d h -> p h d").unsqueeze(3).to_broadcast([128, 8, 2, 64])),
                     reads=['ZX'], writes=['XIb'])
                S.op('dve', lambda e: e.tensor_copy(out=ZEb, in_=ZX[:, 0:2, :].rearrange("p d h -> p h d").unsqueeze(3).to_broadcast([128, 8, 2, 64])),
                     reads=['ZX'], writes=['ZEb'])
                GAM = bp.alloc([128, 8], F32)
                S.op('act', lambda e: e.activation(out=GAM[0:64, :], in_=lg[0:64, 0:8], func=AF.Exp, scale=128.0),
                     reads=['lg'], writes=['GAM'])
                S.op('act', lambda e: e.activation(out=GAM[64:128, :], in_=lg[64:128, 8:16], func=AF.Exp, scale=128.0),
                     reads=['lg'], writes=['GAM'])
                maskT = bp.alloc([128, 8, 128], F32)
                e1 = bp.alloc([128, 128], F32)
                e2 = bp.alloc([128, 128], F32)
                for h in range(8):
                    S.op('act', lambda e, h=h: e.activation(out=e1, in_=cmask[:, 3, :], func=AF.Exp, scale=lg[:, h:h + 1]),
                         reads=['cmask', 'lg'], writes=['e1'])
                    S.op('dve', lambda e: e.tensor_tensor(out=e1, in0=e1, in1=cmask[:, 0, :], op=ALU.mult),
                         reads=['e1', 'cmask'], writes=['e1'])
                    S.op('act', lambda e, h=h: e.activation(out=e2, in_=cmask[:, 4, :], func=AF.Exp, scale=lg[:, 8 + h:9 + h]),
                         reads=['cmask', 'lg'], writes=['e2'])
                    S.op('dve', lambda e: e.tensor_tensor(out=e2, in0=e2, in1=cmask[:, 1, :], op=ALU.mult),
                         reads=['e2', 'cmask'], writes=['e2'])
                    S.op('dve', lambda e: e.tensor_tensor(out=e1, in0=e1, in1=e2, op=ALU.add), reads=['e1', 'e2'], writes=['e1'])
                    S.op('dve', lambda e: e.tensor_tensor(out=e1, in0=e1, in1=cmask[:, 2, :], op=ALU.add), reads=['e1', 'cmask'], writes=['e1'])
                    S.op('dve', lambda e, h=h: e.tensor_scalar(out=maskT[:, h, :], in0=e1, scalar1=0.125, scalar2=None, op0=ALU.mult),
                         reads=['e1'], writes=[('maskT', h)])
                if g.ret_stop == 'const':
                    S.barrier()
                    return
                onesF = bp.alloc([128, 128], F32)
                S.op('pool', lambda e: e.memset(onesF, 1.0 / 128), writes=['onesF'])
                wRs = [bp.alloc([128, KC, 512], BF16) for i in range(2)]
                wGs = [bp.alloc([128, KC, 256], BF16) for i in range(2)]
                RT = bp.alloc([128, 18, 4, 128], BF16)
                kz = bp.alloc([128, 18, 2, 128], BF16)
                vb = bp.alloc([128, 18, 2, 128], BF16)
                sgT = bp.alloc([128, 2, NT], BF16)
                t1s = [bp.alloc([128, 4, 64], F32) for i in range(2)]
                t2s = [bp.alloc([128, 4, 64], F32) for i in range(2)]
                qkrs = [bp.alloc([128, 4, 64], F32) for i in range(2)]
                qbs = [bp.alloc([128, 4, 64], BF16) for i in range(2)]
                qxs = [bp.alloc([128, 2, 2, 64], BF16) for i in range(2)]
                qkss = [bp.alloc([128, 4, 64], F32) for i in range(2)]
                U = bp.alloc([128, 18, 128], F32)
                Sst = bp.alloc([128, 18, 128], BF16)
                R = bp.alloc([128, 128], F32)
                oT = bp.alloc([128, NT], F32)
                Ps = [bp.alloc([128, 128], BF16) for i in range(3)]
                osq = bp.alloc([128, 512], F32)
                mean = bp.alloc([128, 512], F32)
                var = bp.alloc([128, 512], F32)
                dd = bp.alloc([128, 512], F32)
                retS = bp.alloc([128, NT], BF16)
                for hp in range(4):
                    wR = wRs[hp % 2]
                    wrn = "wR%d" % (hp % 2)
                    wG = wGs[hp % 2]
                    wgn = "wG%d" % (hp % 2)
                    for (d0, s0, n_) in ((0, hp * 128, 128), (128, 512 + hp * 128, 128), (256, 1024 + hp * 256, 256)):
                        S.dma('pool', lambda e, wR=wR, d0=d0, s0=s0, n_=n_: e.dma_start(
                            out=wR[:, :, d0:d0 + n_], in_=W[:, s0:s0 + n_].rearrange("(k p) n -> p k n", p=128)),
                            writes=[(wrn, d0)])
                    S.dma('pool', lambda e, wG=wG, hp=hp: e.dma_start(
                        out=wG, in_=W[:, 2048 + hp * 256:2048 + (hp + 1) * 256].rearrange("(k p) n -> p k n", p=128)),
                        writes=[wgn])
                    def p1_proj(t):
                        bi = blk_of(t * 128)
                        c0, c1 = t * 128, (t + 1) * 128
                        t1, t2, qkr, qb, qx, qks = t1s[t % 2], t2s[t % 2], qkrs[t % 2], qbs[t % 2], qxs[t % 2], qkss[t % 2]
                        sfx = '_%d' % (t % 2)
                        ps, pn = bank()
                        for k in range(KC):
                            S.op('pe', lambda e, k=k, ps=ps, c0=c0, c1=c1, wR=wR: e.matmul(
                                ps[:, :512], lhsT=hxT[:, k, c0:c1], rhs=wR[:, k, :], start=(k == 0), stop=(k == KC - 1)),
                                reads=[('hxT', bi), wrn], writes=[pn])
                        return ps, pn

                    def p1_rest(t, ps, pn):
                        t1, t2, qkr, qb, qx, qks = t1s[t % 2], t2s[t % 2], qkrs[t % 2], qbs[t % 2], qxs[t % 2], qkss[t % 2]
                        sfx = '_%d' % (t % 2)
                        S.op('act', lambda e, ps=ps: e.activation(out=qks, in_=ps[:, 0:256].rearrange("p (a d) -> p a d", a=4), func=AF.Copy),
                             reads=[pn], writes=['qks' + sfx])
                        S.op('act', lambda e, ps=ps, t=t: e.activation(
                            out=vb[:, t], in_=ps[:, 256:512].rearrange("p (a d) -> p a d", a=2), func=AF.Copy),
                            reads=[pn], writes=[('vb', t)])
                        S.op('dve', lambda e, t=t: e.tensor_tensor(
                            out=t1, in0=qks, in1=bc(rcs[:, t, 0, :], 1, [128, 4, 64]), op=ALU.mult),
                            reads=['qks' + sfx, 'rcs'], writes=['t1' + sfx])
                        S.op('pool', lambda e, t=t: e.tensor_tensor(
                            out=t2[:, :, 0:32], in0=qks[:, :, 32:64], in1=bc(rcs[:, t, 1, 0:32], 1, [128, 4, 32]), op=ALU.mult),
                            reads=['qks' + sfx, 'rcs'], writes=[('t2' + sfx, 0)])
                        S.op('pool', lambda e, t=t: e.tensor_tensor(
                            out=t2[:, :, 32:64], in0=qks[:, :, 0:32], in1=bc(rcs[:, t, 1, 32:64], 1, [128, 4, 32]), op=ALU.mult),
                            reads=['qks' + sfx, 'rcs'], writes=[('t2' + sfx, 1)])
                        S.op('pool', lambda e: e.tensor_tensor(out=qkr, in0=t1, in1=t2, op=ALU.add),
                             reads=['t1' + sfx, 't2' + sfx], writes=['qkr' + sfx])
                        S.op('act', lambda e: e.activation(out=qb, in_=qkr, func=AF.Copy), reads=['qkr' + sfx], writes=['qb' + sfx])
                        S.op('dve', lambda e: e.tensor_tensor(
                            out=qx, in0=qkr[:, 0:2, :].unsqueeze(2).to_broadcast([128, 2, 2, 64]), in1=XIb[:, hp * 2:hp * 2 + 2], op=ALU.mult),
                            reads=['qkr' + sfx, 'XIb'], writes=['qx' + sfx])
                        S.op('dve', lambda e, t=t: e.tensor_tensor(
                            out=kz[:, t].rearrange("p h (d k) -> p h d k", d=2), in0=qkr[:, 2:4, :].unsqueeze(2).to_broadcast([128, 2, 2, 64]),
                            in1=ZEb[:, hp * 2:hp * 2 + 2], op=ALU.mult),
                            reads=['qkr' + sfx, 'ZEb'], writes=[('kz', t)])
                        pst = pstT[t % 2]
                        ptn = "pst%d" % (t % 2)
                        srcs = (qb[:, 0:2, :].rearrange("p a d -> p (a d)"), qb[:, 2:4, :].rearrange("p a d -> p (a d)"), qx[:, 0].rearrange("p a d -> p (a d)"), qx[:, 1].rearrange("p a d -> p (a d)"))
                        for j in range(4):
                            S.op('pe', lambda e, j=j, pst=pst, s_=srcs[j]: e.transpose(
                                out=pst[:, j * 128:(j + 1) * 128], in_=s_, identity=ident_bf[:]),
                                reads=['qb' + sfx, 'qx' + sfx, 'ident_bf'], writes=[ptn])
                        S.op('act', lambda e, pst=pst, t=t: e.activation(
                            out=RT[:, t], in_=pst[:, 0:512].rearrange("p (a b) -> p a b", a=4), func=AF.Copy),
                            reads=[ptn], writes=[('RT', t)])

                    nxt = p1_proj(0)
                    for t in range(18):
                        cur = nxt
                        if t + 1 < 18:
                            nxt = p1_proj(t + 1)
                        p1_rest(t, *cur)
                    if g.ret_stop == 'pass1':
                        S.barrier()
                        return
                    for hc in range(2):
                        for bi, (c0, c1) in enumerate(BLOCKS):
                            n = c1 - c0
                            ps, pn = bank()
                            for k in range(KC):
                                S.op('pe', lambda e, k=k, ps=ps, c0=c0, c1=c1, n=n, hc=hc, wG=wG: e.matmul(
                                    ps[:, :n], lhsT=wG[:, k, hc * 128:(hc + 1) * 128], rhs=hxT[:, k, c0:c1],
                                    start=(k == 0), stop=(k == KC - 1)),
                                    reads=[('hxT', bi), wgn], writes=[pn])
                            S.op('act', lambda e, ps=ps, c0=c0, c1=c1, n=n, hc=hc: e.activation(
                                out=sgT[:, hc, c0:c1], in_=ps[:, :n], func=AF.Silu), reads=[pn], writes=[('sgT', (hc, bi))])
                    for h in range(2):
                        hd = hp * 2 + h
                        for t in range(18):
                            ps, pn = bank()
                            S.op('pe', lambda e, ps=ps, t=t, h=h: e.matmul(
                                ps[:, :128], lhsT=kz[:, t, h, :], rhs=vb[:, t, h, :], start=True, stop=True),
                                reads=['kz', ('vb', t)], writes=[pn])
                            S.op('act', lambda e, ps=ps, t=t: e.activation(out=U[:, t, :], in_=ps[:, :128], func=AF.Copy),
                                 reads=[pn], writes=[('U', t)])
                        S.op('pool', lambda e: e.memset(R, 0.0), writes=['R'])
                        ts_f = list(range(18))
                        ts_b = [1, 0] + list(range(17, 1, -1))
                        for i in range(18):
                            for (lo, hi, tt, rn, se) in ((0, 64, ts_f[i], 'Rf', 'act'), (64, 128, ts_b[i], 'Rb', 'pool')):
                                if se == 'act':
                                    S.op('act', lambda e, lo=lo, hi=hi, tt=tt: e.activation(out=Sst[lo:hi, tt, :], in_=R[lo:hi, :], func=AF.Copy),
                                         reads=[('R', rn)], writes=[('Sst', (tt, rn))])
                                else:
                                    S.op('pool', lambda e, lo=lo, hi=hi, tt=tt: e.tensor_copy(out=Sst[lo:hi, tt, :], in_=R[lo:hi, :]),
                                         reads=[('R', rn)], writes=[('Sst', (tt, rn))])
                                S.op('dve', lambda e, lo=lo, hi=hi, tt=tt, hd=hd: e.scalar_tensor_tensor(
                                    out=R[lo:hi, :], in0=R[lo:hi, :], scalar=GAM[lo:hi, hd:hd + 1], in1=U[lo:hi, tt, :],
                                    op0=ALU.mult, op1=ALU.add),
                                    reads=[('R', rn), 'GAM', ('U', tt)], writes=[('R', rn)])
                        if g.ret_stop == 'scan':
                            S.barrier()
                            return
                        r0, r1 = h * 64, (h + 1) * 64
                        cst = {'npp': 0, 'psO': None, 'pnO': None}

                        def rc_score(t):
                            psS, pnS = bank()
                            S.op('pe', lambda e: e.matmul(
                                psS[:, :128], lhsT=RT[r0:r1, t, 1, :], rhs=RT[r0:r1, t, 0, :], start=True, stop=True),
                                reads=[('RT', t)], writes=[pnS])
                            P_ = Ps[cst['npp'] % 3]
                            ppn = "Pm%d" % (cst['npp'] % 3)
                            cst['npp'] += 1
                            S.op('dve', lambda e: e.tensor_tensor(out=P_, in0=psS[:, :128], in1=maskT[:, hd, :], op=ALU.mult),
                                 reads=[pnS, ('maskT', hd)], writes=[ppn])
                            return P_, ppn

                        def rc_pv(t, P_, ppn):
                            if t % 4 == 0:
                                cst['psO'], cst['pnO'] = bank()
                            psO, pnO = cst['psO'], cst['pnO']
                            cb = (t % 4) * 128
                            S.op('pe', lambda e: e.matmul(psO[:, cb:cb + 128], lhsT=vb[:, t, h, :], rhs=P_, start=True, stop=False),
                                 reads=[('vb', t), ppn], writes=[pnO])
                            S.op('pe', lambda e: e.matmul(psO[:, cb:cb + 128], lhsT=Sst[:, t, :], rhs=RT[:, t, 2 + h, :], start=False, stop=True),
                                 reads=[('Sst', (t, 'Rf')), ('Sst', (t, 'Rb')), ('RT', t)], writes=[pnO])
                            if t % 4 == 3 or t == 17:
                                t0 = t - (t % 4)
                                ncol = (t - t0 + 1) * 128
                                S.op('act', lambda e: e.activation(out=oT[:, t0 * 128:t0 * 128 + ncol], in_=psO[:, :ncol], func=AF.Copy),
                                     reads=[pnO], writes=[('oT', t0)])
                        nx = rc_score(0)
                        for t in range(18):
                            cu = nx
                            if t + 1 < 18:
                                nx = rc_score(t + 1)
                            rc_pv(t, *cu)
                        for bi, (c0, c1) in enumerate(BLOCKS):
                            n = c1 - c0
                            S.op('act', lambda e, c0=c0, c1=c1, n=n: e.activation(out=osq[:, :n], in_=oT[:, c0:c1], func=AF.Square),
                                 reads=['oT'], writes=['osq'])
                            psM, pnM = bank()
                            psQ, pnQ = bank()
                            S.op('pe', lambda e, psM=psM, c0=c0, c1=c1, n=n: e.matmul(
                                psM[:, :n], lhsT=onesF, rhs=oT[:, c0:c1], start=True, stop=True), reads=['onesF', 'oT'], writes=[pnM])
                            S.op('pe', lambda e, psQ=psQ, n=n: e.matmul(
                                psQ[:, :n], lhsT=onesF, rhs=osq[:, :n], start=True, stop=True), reads=['onesF', 'osq'], writes=[pnQ])
                            S.op('act', lambda e, psM=psM, n=n: e.activation(out=mean[:, :n], in_=psM[:, :n], func=AF.Copy),
                                 reads=[pnM], writes=['mean'])
                            S.op('pool', lambda e, n=n: e.tensor_tensor(out=var[:, :n], in0=mean[:, :n], in1=mean[:, :n], op=ALU.mult),
                                 reads=['mean'], writes=['var'])
                            S.op('dve', lambda e, psQ=psQ, n=n: e.tensor_tensor(out=var[:, :n], in0=psQ[:, :n], in1=var[:, :n], op=ALU.subtract),
                                 reads=[pnQ, 'var'], writes=['var'])
                            S.op('act', lambda e, n=n: e.activation(out=var[:, :n], in_=var[:, :n], func=AF.Sqrt, bias=epst[:, 0:1], scale=1.0),
                                 reads=['var', 'epst'], writes=['var'])
                            S.op('dve', lambda e, n=n: e.reciprocal(out=var[:, :n], in_=var[:, :n]), reads=['var'], writes=['var'])
                            S.op('pool', lambda e, c0=c0, c1=c1, n=n: e.tensor_tensor(out=dd[:, :n], in0=oT[:, c0:c1], in1=mean[:, :n], op=ALU.subtract),
                                 reads=['oT', 'mean'], writes=['dd'])
                            S.op('dve', lambda e, n=n: e.tensor_tensor(out=dd[:, :n], in0=dd[:, :n], in1=var[:, :n], op=ALU.mult),
                                 reads=['dd', 'var'], writes=['dd'])
                            S.op('dve', lambda e, c0=c0, c1=c1, n=n, h=h: e.tensor_tensor(
                                out=retS[:, c0:c1], in0=dd[:, :n], in1=sgT[:, h, c0:c1], op=ALU.mult),
                                reads=['dd', ('sgT', (h, bi))], writes=['retS'])
                        S.dma('sp', lambda e, hd=hd: e.dma_start(out=yT_d[hd * 128:(hd + 1) * 128, :], in_=retS),
                              reads=['retS'], writes=['yT_d'])
                S.barrier()

        def precast_merge(l, W):
            BW = [I['w_branch_ret'][l], I['w_branch_att'][l], I['w_branch_rwkv'][l]]
            KB = [(0, 8), (8, 12), (12, 16)]
            for dp in range(4):
                for br in range(3):
                    cs = 5760 + br * 1024 + dp * 256
                    S.dma('pool', lambda e: e.dma_start(
                        out=wg2_d[l, dp][:, :, br, :], in_=W[:, cs:cs + 256].rearrange("(k p) n -> p k n", p=128)),
                        writes=[('wg2', (l, dp, br))])
                    k0, k1 = KB[br]
                    S.dma('pool', lambda e: e.dma_start(
                        out=wb2_d[l, dp][:, k0:k1, :], in_=BW[br][:, dp * 256:(dp + 1) * 256].rearrange("(k p) n -> p k n", p=128)),
                        writes=[('wb2', (l, dp, br))])
                S.dma('pool', lambda e: e.dma_start(
                    out=wo2_d[l, dp], in_=I['w_out'][l][:, dp * 256:(dp + 1) * 256].rearrange("(k p) n -> p k n", p=128)),
                    writes=[('wo2', (l, dp))])

        def merge(l, W, hxT, blocks_sel):
            with ExitStack() as ph:
                bank = make_banks(ph, 8)
                for bi, (c0, c1) in enumerate(BLOCKS):
                    S.dma('sp', lambda e, c0=c0, c1=c1: e.dma_start(out=XT[:, :, c0:c1], in_=xsp_d[:, :, c0:c1]),
                          writes=[('XT', bi)])
                bp = Bump([(XTW + HXW, AW)])
                yS = bp.alloc([128, 16, 512], BF16)
                mT = bp.alloc([128, KC, 512], BF16)
                wgs = [bp.alloc([128, KC, 3, 256], BF16) for i in range(2)]
                wbs = [bp.alloc([128, 16, 256], BF16) for i in range(2)]
                wos = [bp.alloc([128, KC, 256], BF16) for i in range(2)]
                gs = [bp.alloc([128, 512], F32) for i in range(3)]
                acc = bp.alloc([128, 512], F32)
                tt_ = bp.alloc([128, 512], F32)
                nw = 0
                nwo = 0
                BW = [I['w_branch_ret'][l], I['w_branch_att'][l], I['w_branch_rwkv'][l]]
                KB = [(0, 8), (8, 12), (12, 16)]
                for sb0 in [[0], [1], [2], [3], [4]]:
                    sb = [b for b in sb0 if b in blocks_sel]
                    if not sb:
                        continue
                    offs = []
                    o = 0
                    for b in sb:
                        offs.append(o)
                        o += BLOCKS[b][1] - BLOCKS[b][0]
                    for lb, b in enumerate(sb):
                        c0, c1 = BLOCKS[b]
                        S.dma('sp', lambda e, c0=c0, c1=c1, off=offs[lb]: e.dma_start(
                            out=yS[:, :, off:off + c1 - c0], in_=yT_d[:, c0:c1].rearrange("(kb p) n -> p kb n", p=128)),
                            writes=[('yS', lb)])
                    for dp in range(4):
                        wg = wgs[nw % 2]
                        wgn = "wg%d" % (nw % 2)
                        wb = wbs[nw % 2]
                        wbn = "wb%d" % (nw % 2)
                        nw += 1
                        S.dma('sp', lambda e: e.dma_start(out=wg.rearrange("p k b n -> p (k b n)"), in_=wg2_d[l, dp].rearrange("p k b n -> p (k b n)")),
                              writes=[wgn])
                        S.dma('sp', lambda e: e.dma_start(out=wb.rearrange("p k n -> p (k n)"), in_=wb2_d[l, dp].rearrange("p k n -> p (k n)")),
                              writes=[wbn])
                        for dl in range(2):
                            dc = dp * 2 + dl
                            for lb, b in enumerate(sb):
                                c0, c1 = BLOCKS[b]
                                n = c1 - c0
                                off = offs[lb]
                                for br in range(3):
                                    psg, png = bank()
                                    for k in range(KC):
                                        S.op('pe', lambda e, k=k, psg=psg, wg=wg, br=br, dl=dl, c0=c0, c1=c1, n=n: e.matmul(
                                            psg[:, :n], lhsT=wg[:, k, br, dl * 128:(dl + 1) * 128], rhs=hxT[:, k, c0:c1],
                                            start=(k == 0), stop=(k == KC - 1)),
                                            reads=[wgn, ('hxT', b)], writes=[png])
                                    S.op('act', lambda e, psg=psg, br=br, n=n: e.activation(out=gs[br][:, :n], in_=psg[:, :n], func=AF.Sigmoid),
                                         reads=[png], writes=['gs%d' % br])
                                for br in range(3):
                                    k0, k1 = KB[br]
                                    psp, pnp = bank()
                                    for kb in range(k0, k1):
                                        S.op('pe', lambda e, kb=kb, psp=psp, wb=wb, dl=dl, off=off, n=n, k0=k0, k1=k1: e.matmul(
                                            psp[:, :n], lhsT=wb[:, kb, dl * 128:(dl + 1) * 128], rhs=yS[:, kb, off:off + n],
                                            start=(kb == k0), stop=(kb == k1 - 1)),
                                            reads=[wbn, ('yS', lb)], writes=[pnp])
                                    if br == 0:
                                        S.op('dve', lambda e, psp=psp, n=n: e.tensor_tensor(out=acc[:, :n], in0=psp[:, :n], in1=gs[0][:, :n], op=ALU.mult),
                                             reads=[pnp, 'gs0'], writes=['acc'])
                                    elif br == 1:
                                        S.op('dve', lambda e, psp=psp, n=n: e.tensor_tensor(out=tt_[:, :n], in0=psp[:, :n], in1=gs[1][:, :n], op=ALU.mult),
                                             reads=[pnp, 'gs1'], writes=['tt_'])
                                        S.op('pool', lambda e, n=n: e.tensor_tensor(out=acc[:, :n], in0=acc[:, :n], in1=tt_[:, :n], op=ALU.add),
                                             reads=['acc', 'tt_'], writes=['acc'])
                                    else:
                                        S.op('dve', lambda e, psp=psp, n=n: e.tensor_tensor(out=tt_[:, :n], in0=psp[:, :n], in1=gs[2][:, :n], op=ALU.mult),
                                             reads=[pnp, 'gs2'], writes=['tt_'])
                                        S.op('pool', lambda e, n=n, dc=dc, off=off: e.tensor_tensor(
                                            out=mT[:, dc, off:off + n], in0=acc[:, :n], in1=tt_[:, :n], op=ALU.add),
                                            reads=['acc', 'tt_'], writes=[('mT', (dc, lb))])
                    for dp in range(4):
                        wo = wos[nwo % 2]
                        won = "wom%d" % (nwo % 2)
                        nwo += 1
                        S.dma('sp', lambda e: e.dma_start(out=wo.rearrange("p k n -> p (k n)"), in_=wo2_d[l, dp].rearrange("p k n -> p (k n)")),
                              writes=[won])
                        for dl in range(2):
                            dc = dp * 2 + dl
                            for lb, b in enumerate(sb):
                                c0, c1 = BLOCKS[b]
                                n = c1 - c0
                                off = offs[lb]
                                which = 1 if c0 < NCTX else 0
                                ps, pn = bank()
                                for k in range(KC):
                                    S.op('pe', lambda e, k=k, ps=ps, wo=wo, dl=dl, off=off, n=n: e.matmul(
                                        ps[:, :n], lhsT=wo[:, k, dl * 128:(dl + 1) * 128], rhs=mT[:, k, off:off + n],
                                        start=(k == 0), stop=(k == KC - 1)),
                                        reads=[won, ('mT', (k, lb))], writes=[pn])
                                S.op('dve', lambda e, ps=ps, n=n, dc=dc, c0=c0, c1=c1, which=which: e.scalar_tensor_tensor(
                                    out=XT[:, dc, c0:c1], in0=ps[:, :n], scalar=Gmod[:, l, 1, dc, which:which + 1],
                                    in1=XT[:, dc, c0:c1], op0=ALU.mult, op1=ALU.add),
                                    reads=[pn, 'Gmod', ('XT', b)], writes=[('XT', b)])
                S.barrier()
        CDEC = 0.6065306597126334
        RO = 3840
        NCH = NT // 64

        def rwkv(l, W, hxT):
            with ExitStack() as ph:
                bank = make_banks(ph, 8)
                bp = Bump([(0, XTW), (XTW + HXW, AW)])

                def FT():
                    return bp.alloc([128, NT], F32)
                raw = FT()
                rT = FT()
                kT = FT()
                vT = FT()
                kk = FT()
                sg_ = FT()
                aa = FT()
                cum = FT()
                tA = FT()
                tB = FT()
                tC = FT()
                asum = FT()
                pre = FT()
                rmask = bp.alloc([128, NT], BF16)
                stg = bp.alloc([128, 6, 128], BF16)
                pcst = bp.alloc([128, 36], F32)
                ob = bp.alloc([128, NT], BF16)
                shT = bp.alloc([128, 15, 3], F32)
                w0T = bp.alloc([128, 2, 4], F32)
                a0T = bp.alloc([128, 2, 4], F32)
                vecT = bp.alloc([128, 5, 4], F32)
                v0T = bp.alloc([128, 4], F32)
                S.dma('sp', lambda e: e.dma_start(out=shT, in_=I['rw_shT'][l]), writes=['shT'])
                S.dma('sp', lambda e: e.dma_start(out=w0T, in_=I['rw_w0T'][l]), writes=['w0T'])
                S.dma('sp', lambda e: e.dma_start(out=a0T, in_=I['rw_a0T'][l]), writes=['a0T'])
                S.dma('sp', lambda e: e.dma_start(out=vecT, in_=I['rw_vecT'][l]), writes=['vecT'])
                S.dma('sp', lambda e: e.dma_start(out=v0T, in_=I['rw_v0T']), writes=['v0T'])
                wupP = [bp.alloc([128, 512], BF16) for d in range(2)]
                aupP = [bp.alloc([128, 512], BF16) for d in range(2)]
                for d in range(2):
                    S.op('pool', lambda e, d=d: e.memset(wupP[d], 0.0), writes=['wupP%d' % d])
                    S.op('pool', lambda e, d=d: e.memset(aupP[d], 0.0), writes=['aupP%d' % d])
                    S.dma('pool', lambda e, d=d: e.dma_start(out=wupP[d][d * 64:(d + 1) * 64, :], in_=I['rwkv_w_up'][l, d]),
                          reads=['wupP%d' % d], writes=['wupP%d' % d])
                    S.dma('pool', lambda e, d=d: e.dma_start(out=aupP[d][d * 64:(d + 1) * 64, :], in_=I['rwkv_a_up'][l, d]),
                          reads=['aupP%d' % d], writes=['aupP%d' % d])
                gup = bp.alloc([128, 512], BF16)
                S.dma('pool', lambda e: e.dma_start(out=gup, in_=I['rwkv_g_up'][l]), writes=['gup'])
                if l == 1:
                    vdn = bp.alloc([128, KC, 32], BF16)
                    vup = bp.alloc([32, 512], BF16)
                    S.dma('pool', lambda e: e.dma_start(out=vdn, in_=I['rwkv_v_down'][0].rearrange("(k p) n -> p k n", p=128)), writes=['vdn'])
                    S.dma('pool', lambda e: e.dma_start(out=vup, in_=I['rwkv_v_up'][0]), writes=['vup'])
                    vdT = bp.alloc([32, NT], BF16)
                bones = bp.alloc([128, 128], F32)
                S.op('pool', lambda e: e.memset(bones, 0.0), writes=['bones'])
                S.op('pool', lambda e: e.memset(bones[0:64, 0:64], 1.0), reads=['bones'], writes=['bones'])
                S.op('pool', lambda e: e.memset(bones[64:128, 64:128], 1.0), reads=['bones'], writes=['bones'])
                S.op('pool', lambda e: e.memset(rmask, 1.0), writes=['rmask'])
                S.op('pool', lambda e: e.memset(rmask.rearrange("p (c t) -> p c t", t=64)[:, :, 0:1], 0.0),
                     reads=['rmask'], writes=['rmask'])
                twT = bp.alloc([128, NT], BF16)
                adT = bp.alloc([128, NT], BF16)
                sgdT = bp.alloc([128, NT], BF16)
                wq = [bp.alloc([128, KC, 128], BF16) for i in range(2)]
                nwq = [0]

                def proj_chunk(ci, dst, dstn):
                    w = wq[nwq[0] % 2]
                    wn = 'wq%d' % (nwq[0] % 2)
                    nwq[0] += 1
                    S.dma('pool', lambda e: e.dma_start(
                        out=w, in_=W[:, RO + ci * 128:RO + (ci + 1) * 128].rearrange("(k p) n -> p k n", p=128)), writes=[wn])
                    for bi, (c0, c1) in enumerate(BLOCKS):
                        n = c1 - c0
                        ps, pn = bank()
                        for k in range(KC):
                            S.op('pe', lambda e, k=k, ps=ps, c0=c0, c1=c1, n=n: e.matmul(
                                ps[:, :n], lhsT=w[:, k, :], rhs=hxT[:, k, c0:c1], start=(k == 0), stop=(k == KC - 1)),
                                reads=[wn, ('hxT', bi)], writes=[pn])
                        S.op('act', lambda e, ps=ps, c0=c0, c1=c1, n=n: e.activation(out=dst[:, c0:c1], in_=ps[:, :n], func=AF.Copy),
                             reads=[pn], writes=[(dstn, bi)])

                def conv(raw, rawn, out, outn, ci):
                    for (a, b) in ((0, NCTX), (NCTX, NT)):
                        S.op('act', lambda e, a=a, b=b: e.activation(out=out[:, a:b], in_=raw[:, a:b], func=AF.Identity,
                                                                     scale=shT[:, ci, 1:2]),
                             reads=[rawn, 'shT'], writes=[(outn, a)])
                        S.op('dve', lambda e, a=a, b=b: e.scalar_tensor_tensor(
                            out=out[:, a + 1:b], in0=raw[:, a:b - 1], scalar=shT[:, ci, 0:1], in1=out[:, a + 1:b],
                            op0=ALU.mult, op1=ALU.add), reads=[rawn, 'shT', (outn, a)], writes=[(outn, a)])
                        S.op('dve', lambda e, a=a, b=b: e.scalar_tensor_tensor(
                            out=out[:, a:b - 1], in0=raw[:, a + 1:b], scalar=shT[:, ci, 2:3], in1=out[:, a:b - 1],
                            op0=ALU.mult, op1=ALU.add), reads=[rawn, 'shT', (outn, a)], writes=[(outn, a)])

                def fm_matmul(lhsT, ln, rhs_tile, rn, evac, kparts=128):
                    for bi, (c0, c1) in enumerate(BLOCKS):
                        n = c1 - c0
                        ps, pn = bank()
                        S.op('pe', lambda e, ps=ps, c0=c0, c1=c1, n=n: e.matmul(
                            ps[:, :n], lhsT=lhsT, rhs=rhs_tile[:kparts, c0:c1], start=True, stop=True),
                            reads=[ln, rn], writes=[pn])
                        evac(ps, pn, c0, c1, n)

                for (ci, dst, dn, fn) in ((12, twT, 'twT', AF.Tanh), (13, adT, 'adT', AF.Identity), (14, sgdT, 'sgdT', AF.Sigmoid)):
                    proj_chunk(ci, raw, 'raw')
                    conv(raw, 'raw', tA, 'tA', ci)
                    S.op('act', lambda e, dst=dst, fn=fn: e.activation(out=dst, in_=tA, func=fn), reads=['tA'], writes=[dn])
                if l == 1:
                    for bi, (c0, c1) in enumerate(BLOCKS):
                        n = c1 - c0
                        ps, pn = bank()
                        for k in range(KC):
                            S.op('pe', lambda e, k=k, ps=ps, c0=c0, c1=c1, n=n: e.matmul(
                                ps[:32, :n], lhsT=vdn[:, k, :], rhs=hxT[:, k, c0:c1], start=(k == 0), stop=(k == KC - 1)),
                                reads=['vdn', ('hxT', bi)], writes=[pn])
                        S.op('act', lambda e, ps=ps, c0=c0, c1=c1, n=n: e.activation(out=vdT[:, c0:c1], in_=ps[:32, :n], func=AF.Copy),
                             reads=[pn], writes=['vdT'])

                def to_tm(src, srcn, dst_ap_fn):
                    for part in range(3):
                        for tt in range(6):
                            t = part * 6 + tt
                            ps, pn = bank()
                            S.op('pe', lambda e, ps=ps, t=t: e.transpose(out=ps[:, 0:128], in_=src[:, t * 128:(t + 1) * 128], identity=ident[:]),
                                 reads=[srcn, 'ident'], writes=[pn])
                            S.op('act', lambda e, ps=ps, tt=tt: e.activation(out=stg[:, tt, :], in_=ps[:, 0:128], func=AF.Copy),
                                 reads=[pn], writes=[('stg', tt)])
                        r0 = part * 6 * 128
                        S.dma('sp', lambda e: e.dma_start(
                            out=dst_ap_fn()[r0:r0 + 6 * 128, :].rearrange("(t p) c -> p t c", p=128), in_=stg),
                            reads=['stg'], writes=['scr'])

                for cc in range(4):
                    cs = slice(cc * 128, (cc + 1) * 128)
                    for (q, dst, dn) in ((0, rT, 'rT'), (1, kT, 'kT'), (2, vT, 'vT')):
                        proj_chunk(q * 4 + cc, raw, 'raw')
                        conv(raw, 'raw', dst, dn, q * 4 + cc)
                    if l == 0:
                        S.dma('sp', lambda e, cs=cs: e.dma_start(out=vf_d[cs, :], in_=vT), reads=['vT'], writes=['vf_d'])
                    else:
                        S.dma('sp', lambda e, cs=cs: e.dma_start(out=tA, in_=vf_d[cs, :]), writes=['tA'])

                        def ev_v(ps, pn, c0, c1, n):
                            S.op('act', lambda e: e.activation(out=tB[:, c0:c1], in_=ps[:, :n], func=AF.Sigmoid, bias=v0T[:, cc:cc + 1]),
                                 reads=[pn, 'v0T'], writes=['tB'])
                        fm_matmul(vup[:, cs], 'vup', vdT, 'vdT', ev_v, kparts=32)
                        S.op('dve', lambda e: e.tensor_tensor(out=tA, in0=tA, in1=vT, op=ALU.subtract), reads=['tA', 'vT'], writes=['tA'])
                        S.op('dve', lambda e: e.tensor_tensor(out=tA, in0=tA, in1=tB, op=ALU.mult), reads=['tA', 'tB'], writes=['tA'])
                        S.op('dve', lambda e: e.tensor_tensor(out=vT, in0=vT, in1=tA, op=ALU.add), reads=['tA', 'vT'], writes=['vT'])
                    to_tm(vT, 'vT', lambda cs=cs: vtm_d[:, cs])
                    S.op('dve', lambda e: e.tensor_scalar(out=tA, in0=kT, scalar1=vecT[:, 0, cc:cc + 1], scalar2=None, op0=ALU.mult),
                         reads=['kT', 'vecT'], writes=['tA'])
                    S.op('act', lambda e: e.activation(out=tB, in_=tA, func=AF.Square), reads=['tA'], writes=['tB'])

                    def ev_kk(ps, pn, c0, c1, n):
                        S.op('act', lambda e: e.activation(out=tC[:, c0:c1], in_=ps[:, :n], func=AF.Sqrt), reads=[pn], writes=['tC'])
                    fm_matmul(bones, 'bones', tB, 'tB', ev_kk)
                    S.op('dve', lambda e: e.tensor_scalar(out=tC, in0=tC, scalar1=1e-12, scalar2=None, op0=ALU.max),
                         reads=['tC'], writes=['tC'])
                    S.op('dve', lambda e: e.reciprocal(out=tC, in_=tC), reads=['tC'], writes=['tC'])
                    S.op('dve', lambda e: e.tensor_tensor(out=kk, in0=tA, in1=tC, op=ALU.mult), reads=['tA', 'tC'], writes=['kk'])
                    for d in range(2):
                        def ev_a(ps, pn, c0, c1, n, d=d):
                            S.op('act', lambda e: e.activation(out=aa[:, c0:c1], in_=ps[:, :n], func=AF.Sigmoid, bias=a0T[:, d, cc:cc + 1]),
                                 reads=[pn, 'a0T'], writes=['aa'])
                        fm_matmul(aupP[d][:, cs], 'aupP%d' % d, adT, 'adT', ev_a)

                        def ev_s(ps, pn, c0, c1, n, d=d):
                            S.op('act', lambda e: e.activation(out=sg_[:, c0:c1], in_=ps[:, :n], func=AF.Sigmoid, bias=w0T[:, d, cc:cc + 1]),
                                 reads=[pn, 'w0T'], writes=['sg_'])
                        fm_matmul(wupP[d][:, cs], 'wupP%d' % d, twT, 'twT', ev_s)
                        cum3 = cum.rearrange("p (c t) -> p c t", t=64)
                        if d == 0:
                            S.op('dve', lambda e: e.tensor_tensor_scan(out=cum, data0=rmask, data1=sg_, initial=0.0, op0=ALU.mult, op1=ALU.add),
                                 reads=['rmask', 'sg_'], writes=['cum'])
                            psrc, psn = cum, 'cum'
                        else:
                            S.op('dve', lambda e: e.tensor_tensor_scan(out=pre, data0=rmask, data1=sg_, initial=0.0, op0=ALU.mult, op1=ALU.add),
                                 reads=['rmask', 'sg_'], writes=['pre'])
                            psrc, psn = pre, 'pre'
                            pre3 = pre.rearrange("p (c t) -> p c t", t=64)
                            S.op('dve', lambda e: e.tensor_tensor(
                                out=cum3, in0=pre3[:, :, 63:64].to_broadcast([128, NCH, 64]), in1=pre3, op=ALU.subtract),
                                reads=['pre'], writes=['cum'])
                            S.op('dve', lambda e: e.tensor_tensor(out=cum, in0=cum, in1=sg_, op=ALU.add), reads=['cum', 'sg_'], writes=['cum'])
                        p3_ = psrc.rearrange("p (c t) -> p c t", t=64)
                        tot_b = p3_[:, :, 63:64].to_broadcast([128, NCH, 64])
                        S.op('act', lambda e: e.activation(out=pcst, in_=p3_[:, :, 63], func=AF.Exp, scale=-CDEC),
                             reads=[psn], writes=['pcst'])
                        S.dma('sp', lambda e: e.dma_start(out=pcs_d[d, cs, :], in_=pcst), reads=['pcst'], writes=['scr'])
                        S.op('dve', lambda e: e.tensor_tensor(out=tB, in0=kk, in1=aa, op=ALU.mult), reads=['kk', 'aa'], writes=['tB'])
                        S.op('dve', lambda e: e.tensor_scalar(out=raw, in0=aa, scalar1=-1.0, scalar2=vecT[:, 1, cc:cc + 1],
                                                              op0=ALU.add, op1=ALU.mult), reads=['aa', 'vecT'], writes=['raw'])
                        S.op('dve', lambda e: e.scalar_tensor_tensor(out=raw, in0=raw, scalar=1.0, in1=kT, op0=ALU.add, op1=ALU.mult),
                             reads=['raw', 'kT'], writes=['raw'])
                        S.op('act', lambda e: e.activation(out=tA, in_=cum, func=AF.Exp, scale=-CDEC), reads=['cum'], writes=['tA'])
                        S.op('dve', lambda e: e.tensor_tensor(out=tA, in0=tA, in1=rT, op=ALU.mult), reads=['tA', 'rT'], writes=['tA'])
                        S.op('act', lambda e: e.activation(out=ob, in_=tA, func=AF.Copy), reads=['tA'], writes=['ob'])
                        for hh_ in range(2):
                            S.dma('sp', lambda e, hh_=hh_: e.dma_start(
                                out=scrT[d, :, :, 2 * cc + hh_, 1, :].rearrange("c k t -> k c t"),
                                in_=ob[hh_ * 64:(hh_ + 1) * 64, :].rearrange("p (c t) -> p c t", t=64)), reads=['ob'], writes=['scr'])
                        S.op('dve', lambda e: e.tensor_tensor(out=tA, in0=cum, in1=sg_, op=ALU.subtract), reads=['cum', 'sg_'], writes=['tA'])
                        S.op('act', lambda e: e.activation(out=tA, in_=tA, func=AF.Exp, scale=-CDEC), reads=['tA'], writes=['tA'])
                        S.op('dve', lambda e: e.scalar_tensor_tensor(out=tA, in0=kk, scalar=-1.0, in1=tA, op0=ALU.mult, op1=ALU.mult),
                             reads=['kk', 'tA'], writes=['tA'])
                        S.op('act', lambda e: e.activation(out=ob, in_=tA, func=AF.Copy), reads=['tA'], writes=['ob'])
                        for hh_ in range(2):
                            S.dma('sp', lambda e, hh_=hh_: e.dma_start(
                                out=scrT[d, :, :, 2 * cc + hh_, 0, :].rearrange("c k t -> k c t"),
                                in_=ob[hh_ * 64:(hh_ + 1) * 64, :].rearrange("p (c t) -> p c t", t=64)), reads=['ob'], writes=['scr'])
                        S.op('act', lambda e: e.activation(out=tA, in_=cum, func=AF.Exp, scale=CDEC), reads=['cum'], writes=['tA'])
                        S.op('dve', lambda e: e.tensor_tensor(out=tC, in0=tA, in1=tB, op=ALU.mult), reads=['tA', 'tB'], writes=['tC'])
                        S.op('act', lambda e: e.activation(out=ob, in_=tC, func=AF.Copy), reads=['tC'], writes=['ob'])
                        for hh_ in range(2):
                            S.dma('sp', lambda e, hh_=hh_: e.dma_start(
                                out=scrT[d, :, :, 2 * cc + hh_, 2, :].rearrange("c k t -> k c t"),
                                in_=ob[hh_ * 64:(hh_ + 1) * 64, :].rearrange("p (c t) -> p c t", t=64)), reads=['ob'], writes=['scr'])
                        S.op('dve', lambda e: e.tensor_tensor(out=tA, in0=tA, in1=raw, op=ALU.mult), reads=['tA', 'raw'], writes=['tA'])
                        S.op('act', lambda e: e.activation(out=ob, in_=tA, func=AF.Copy), reads=['tA'], writes=['ob'])
                        for hh_ in range(2):
                            S.dma('sp', lambda e, hh_=hh_: e.dma_start(
                                out=scrT[d, :, :, 2 * cc + hh_, 3, :].rearrange("c k t -> k c t"),
                                in_=ob[hh_ * 64:(hh_ + 1) * 64, :].rearrange("p (c t) -> p c t", t=64)), reads=['ob'], writes=['scr'])
                        if d == 0:
                            S.op('dve', lambda e: e.tensor_tensor(
                                out=tA.rearrange("p (c t) -> p c t", t=64), in0=tot_b, in1=cum3, op=ALU.subtract),
                                reads=['cum'], writes=['tA'])
                        else:
                            S.op('dve', lambda e: e.tensor_tensor(out=tA, in0=pre, in1=sg_, op=ALU.subtract),
                                 reads=['pre', 'sg_'], writes=['tA'])
                        S.op('act', lambda e: e.activation(out=tA, in_=tA, func=AF.Exp, scale=-CDEC), reads=['tA'], writes=['tA'])
                        S.op('dve', lambda e: e.tensor_tensor(out=tB, in0=tB, in1=tA, op=ALU.mult), reads=['tA', 'tB'], writes=['tB'])
                        to_tm(tB, 'tB', lambda d=d, cs=cs: scrM[d, 0, :, cs])
                        S.op('dve', lambda e: e.tensor_tensor(out=raw, in0=raw, in1=tA, op=ALU.mult), reads=['tA', 'raw'], writes=['raw'])
                        to_tm(raw, 'raw', lambda d=d, cs=cs: scrM[d, 1, :, cs])
                        if d == 0:
                            S.op('dve', lambda e: e.tensor_copy(out=asum, in_=aa), reads=['aa'], writes=['asum'])
                        else:
                            S.op('dve', lambda e: e.tensor_tensor(out=asum, in0=asum, in1=aa, op=ALU.add), reads=['aa', 'asum'], writes=['asum'])
                    S.op('dve', lambda e: e.tensor_scalar(out=tA, in0=asum, scalar1=-2.0, scalar2=vecT[:, 1, cc:cc + 1],
                                                          op0=ALU.add, op1=ALU.mult), reads=['asum', 'vecT'], writes=['tA'])
                    S.op('dve', lambda e: e.scalar_tensor_tensor(out=tA, in0=tA, scalar=2.0, in1=kT, op0=ALU.add, op1=ALU.mult),
                         reads=['tA', 'kT'], writes=['tA'])
                    S.op('dve', lambda e: e.scalar_tensor_tensor(out=tA, in0=tA, scalar=vecT[:, 2, cc:cc + 1], in1=rT, op0=ALU.mult, op1=ALU.mult),
                         reads=['tA', 'rT', 'vecT'], writes=['tA'])

                    def ev_b(ps, pn, c0, c1, n):
                        S.op('dve', lambda e: e.tensor_tensor(out=tB[:, c0:c1], in0=ps[:, :n], in1=vT[:, c0:c1], op=ALU.mult),
                             reads=[pn, 'vT'], writes=['tB'])
                    fm_matmul(bones, 'bones', tA, 'tA', ev_b)
                    S.dma('sp', lambda e, cs=cs: e.dma_start(out=bon_d[cs, :], in_=tB), reads=['tB'], writes=['scr'])

                    def ev_g(ps, pn, c0, c1, n):
                        S.op('act', lambda e: e.activation(out=tC[:, c0:c1], in_=ps[:, :n], func=AF.Copy), reads=[pn], writes=['tC'])
                    fm_matmul(gup[:, cs], 'gup', sgdT, 'sgdT', ev_g)
                    S.dma('sp', lambda e, cs=cs: e.dma_start(out=g_d[cs, :], in_=tC), reads=['tC'], writes=['scr'])
                S.barrier()

            if g.ret_stop == 'rw_r0':
                return
            with ExitStack() as ph:
                bank = make_banks(ph, 8)
                bp = Bump([(0, XTW), (XTW + HXW, AW)])
                Ysum = bp.alloc([128, 4, NT], F32)
                S.op('pool', lambda e: e.memset(Ysum, 0.0), writes=['Ysum'])
                precast_merge(l, W)
                CM, LM, PCs, ST, AB, BK, VT, MA, KA, Lc, Mc, Xs, Xb, STb, ARB = [], [], [], [], [], [], [], [], [], [], [], [], [], [], []
                Mst, Lst, BDM, BDL, CM2, LM2 = [], [], [], [], [], []
                for d in range(2):
                    CM.append(bp.alloc([64, 128], F32))
                    LM.append(bp.alloc([64, 64], F32))
                    i_s, i_i, i_l = (0, 6, 1) if d == 0 else (1, 5, 0)
                    S.op('dve', lambda e, d=d, i_s=i_s: e.tensor_copy(out=CM[d][:, 0:64], in_=cmask[0:64, i_s, 0:64]), reads=['cmask'], writes=['CM%d' % d])
                    S.op('dve', lambda e, d=d, i_i=i_i: e.tensor_copy(out=CM[d][:, 64:128], in_=cmask[0:64, i_i, 0:64]), reads=['cmask'], writes=['CM%d' % d])
                    S.op('dve', lambda e, d=d, i_l=i_l: e.tensor_copy(out=LM[d], in_=cmask[0:64, i_l, 0:64]), reads=['cmask'], writes=['LM%d' % d])
                    PCs.append(bp.alloc([64, 8, NCH], F32))
                    S.dma('sp', lambda e, d=d: e.dma_start(out=PCs[d], in_=pcs_d[d].rearrange("(h k) c -> k h c", k=64)), writes=['PC%d' % d])
                    ST.append(bp.alloc([64, 8, 64], F32))
                    S.op('pool', lambda e, d=d: e.memset(ST[d], 0.0), writes=['ST%d' % d])
                    AB.append([bp.alloc([64, 8, 4, 64], BF16) for i in range(2)])
                    BK.append([bp.alloc([64, 2, 512], BF16) for i in range(2)])
                    VT.append([bp.alloc([64, 512], BF16) for i in range(2)])
                    ARB.append(bp.alloc([64, 8, 64], BF16))
                    KA.append(bp.alloc([64, 8, 128], BF16))
                    Xs.append(bp.alloc([128, 4, 64], F32))
                    Mst.append([bp.alloc([128, 4, 64], F32) for i in range(2)])
                    Lst.append([bp.alloc([128, 4, 64], F32) for i in range(2)])
                    BDM.append([bp.alloc([128, 4, 128], F32) for i in range(2)])
                    BDL.append([bp.alloc([128, 4, 128], F32) for i in range(2)])
                    for i in range(2):
                        S.op('pool', lambda e, d=d, i=i: e.memset(BDM[d][i], 0.0), writes=['BDM%d%d' % (d, i)])
                        S.op('pool', lambda e, d=d, i=i: e.memset(BDL[d][i], 0.0), writes=['BDL%d%d' % (d, i)])
                    CM2.append(bp.alloc([128, 128], F32))
                    LM2.append(bp.alloc([128, 64], F32))
                    for (lo, hi) in ((0, 64), (64, 128)):
                        S.op('dve', lambda e, d=d, lo=lo, hi=hi, i_s=i_s: e.tensor_copy(out=CM2[d][lo:hi, 0:64], in_=cmask[lo:hi, i_s, lo:hi]), reads=['cmask'], writes=['CM2%d' % d])
                        S.op('dve', lambda e, d=d, lo=lo, hi=hi, i_i=i_i: e.tensor_copy(out=CM2[d][lo:hi, 64:128], in_=cmask[lo:hi, i_i, lo:hi]), reads=['cmask'], writes=['CM2%d' % d])
                        S.op('dve', lambda e, d=d, lo=lo, hi=hi, i_l=i_l: e.tensor_copy(out=LM2[d][lo:hi, :], in_=cmask[lo:hi, i_l, lo:hi]), reads=['cmask'], writes=['LM2%d' % d])
                    Xb.append(bp.alloc([64, 8, 64], BF16))
                    STb.append(bp.alloc([64, 8, 64], BF16))
                    S.op('pool', lambda e, d=d: e.memset(STb[d], 0.0), writes=['STb%d' % d])
                order = [list(range(NCH)), [3, 2, 1, 0] + list(range(NCH - 1, 3, -1))]

                def loads(d, s):
                    c = order[d][s]
                    par = s % 2
                    c0, c1 = c * 64, (c + 1) * 64
                    S.dma('sp', lambda e: e.dma_start(out=AB[d][par].rearrange("p h q t -> p (h q t)"),
                                                      in_=scrT[d, c].rearrange("k h q t -> k (h q t)")),
                          writes=['AB%d%d' % (d, par)])
                    S.dma('sp', lambda e: e.dma_start(out=BK[d][par], in_=scrM[d, :, c0:c1, :].rearrange("q t c -> t q c")),
                          writes=['BK%d%d' % (d, par)])
                    S.dma('sp', lambda e: e.dma_start(out=VT[d][par], in_=vtm_d[c0:c1, :]), writes=['VT%d%d' % (d, par)])

                def step_stages(d, s):
                    c = order[d][s]
                    par = s % 2
                    c0, c1 = c * 64, (c + 1) * 64
                    ab, bk, vt = AB[d][par], BK[d][par], VT[d][par]
                    abn, bkn, vtn = 'AB%d%d' % (d, par), 'BK%d%d' % (d, par), 'VT%d%d' % (d, par)
                    kan, xn, stn = 'KA%d' % d, 'X%d' % d, 'ST%d' % d
                    ka, X, st_ = KA[d], Xs[d], ST[d]
                    xb, stb, arb = Xb[d], STb[d], ARB[d]
                    xbn, stbn, arbn = 'Xb%d' % d, 'STb%d' % d, 'ARB%d' % d

                    def AR(h):
                        return ab[:, h, 0:2, :].rearrange("p a b -> p (a b)")

                    def v4(ps):
                        return ps[:, 0:256].rearrange("p (a c) -> p a c", a=4)

                    def to_bd(eng, src, srcn, bd, bdn):
                        for (lo, hi, k) in ((0, 64, 0), (64, 128, 1)):
                            if eng == 'act':
                                S.op('act', lambda e, lo=lo, hi=hi: e.activation(out=bd[lo:hi, :, lo:hi], in_=src[lo:hi, :, :], func=AF.Copy),
                                     reads=[srcn], writes=[(bdn, k)])
                            else:
                                S.op('dve', lambda e, lo=lo, hi=hi: e.tensor_copy(out=bd[lo:hi, :, lo:hi], in_=src[lo:hi, :, :]),
                                     reads=[srcn], writes=[(bdn, k)])

                    def s_G():
                        for hb in range(2):
                            ps, pn = bank()
                            for hh in range(4):
                                h = hb * 4 + hh
                                S.op('pe', lambda e, ps=ps, hh=hh, h=h: e.matmul(
                                    ps[0:64, hh * 128:(hh + 1) * 128], lhsT=ab[:, h, 3, :], rhs=AR(h), start=True, stop=True),
                                    reads=[abn], writes=[pn])
                            S.op('dve', lambda e, ps=ps, hb=hb: e.tensor_tensor(
                                out=ka[:, hb * 4:(hb + 1) * 4, :], in0=ps[0:64, 0:512].rearrange("p (h c) -> p h c", h=4),
                                in1=bc(CM[d], 1, [64, 4, 128]), op=ALU.mult), reads=[pn, 'CM%d' % d], writes=[(kan, hb)])
                        psM, pnM = bank()
                        psL, pnL = bank()
                        psR, pnR = bank()
                        for h in range(8):
                            hp_, hh = h // 2, h % 2
                            S.op('pe', lambda e, h=h, hp_=hp_, hh=hh: e.matmul(
                                psM[hh * 64:(hh + 1) * 64, hp_ * 64:(hp_ + 1) * 64], lhsT=ab[:, h, 2, :], rhs=ab[:, h, 0, :], start=True, stop=True),
                                reads=[abn], writes=[pnM])
                        for h in range(8):
                            hp_, hh = h // 2, h % 2
                            S.op('pe', lambda e, h=h, hp_=hp_, hh=hh: e.matmul(
                                psL[hh * 64:(hh + 1) * 64, hp_ * 64:(hp_ + 1) * 64], lhsT=ab[:, h, 0, :], rhs=ab[:, h, 2, :], start=True, stop=True),
                                reads=[abn], writes=[pnL])
                        for h in range(8):
                            S.op('pe', lambda e, h=h: e.matmul(
                                psR[0:64, h * 64:(h + 1) * 64], lhsT=ab[:, h, 2, :], rhs=ab[:, h, 1, :], start=True, stop=True),
                                reads=[abn], writes=[pnR])
                        for (lo, hi, k_) in ((0, 64, 0), (64, 128, 1)):
                            S.op('dve', lambda e, lo=lo, hi=hi: e.tensor_tensor(
                                out=BDM[d][0][lo:hi, :, lo:hi], in0=v4(psM)[lo:hi], in1=bc(CM2[d][lo:hi, 0:64], 1, [64, 4, 64]), op=ALU.mult),
                                reads=[pnM, 'CM2%d' % d], writes=[('BDM%d0' % d, k_)])
                            S.op('dve', lambda e, lo=lo, hi=hi: e.tensor_tensor(
                                out=BDL[d][0][lo:hi, :, lo:hi], in0=v4(psL)[lo:hi], in1=bc(LM2[d][lo:hi, :], 1, [64, 4, 64]), op=ALU.mult),
                                reads=[pnL, 'LM2%d' % d], writes=[('BDL%d0' % d, k_)])
                        S.op('dve', lambda e: e.tensor_tensor(
                            out=arb, in0=psR[0:64, 0:512].rearrange("p (h c) -> p h c", h=8), in1=bc(CM[d][:, 64:128], 1, [64, 8, 64]), op=ALU.mult),
                            reads=[pnR, 'CM%d' % d], writes=[arbn])

                    def s_X():
                        ps, pn = bank()
                        for h in range(8):
                            hp_, hh = h // 2, h % 2
                            o = ps[hh * 64:(hh + 1) * 64, hp_ * 64:(hp_ + 1) * 64]
                            S.op('pe', lambda e, o=o, h=h: e.matmul(o, lhsT=ab[:, h, 0, :], rhs=stb[:, h, :], start=True, stop=False),
                                 reads=[abn, stbn], writes=[pn])
                            S.op('pe', lambda e, o=o, h=h: e.matmul(o, lhsT=ka[:, h, 0:64], rhs=vt[:, h * 64:(h + 1) * 64], start=False, stop=True),
                                 reads=[kan, vtn], writes=[pn])
                        S.op('act', lambda e, ps=ps: e.activation(out=X, in_=v4(ps), func=AF.Copy), reads=[pn], writes=[xn])

                    def mk_round(r):
                        def f():
                            k0, k1 = r % 2, (r + 1) % 2
                            bdm, bdl = BDM[d][k0], BDL[d][k0]
                            bdmn, bdln = 'BDM%d%d' % (d, k0), 'BDL%d%d' % (d, k0)
                            psA, pnA = bank()
                            for q in range(4):
                                S.op('pe', lambda e, q=q: e.matmul(psA[:, q * 64:(q + 1) * 64], lhsT=bdm[:, q, :], rhs=X[:, q, :], start=True, stop=True),
                                     reads=[bdmn, xn], writes=[pnA])
                            if r < 5:
                                psM, pnM = bank()
                                psL, pnL = bank()
                                for q in range(4):
                                    S.op('pe', lambda e, q=q: e.matmul(psM[:, q * 128:(q + 1) * 128], lhsT=bdl[:, q, :], rhs=bdm[:, q, :], start=True, stop=True),
                                         reads=[bdln, bdmn], writes=[pnM])
                                if r < 4:
                                    for q in range(4):
                                        S.op('pe', lambda e, q=q: e.matmul(psL[:, q * 128:(q + 1) * 128], lhsT=bdm[:, q, :], rhs=bdl[:, q, :], start=True, stop=True),
                                             reads=[bdmn, bdln], writes=[pnL])
                            S.op('dve', lambda e: e.tensor_tensor(out=X, in0=X, in1=v4(psA), op=ALU.add), reads=[xn, pnA], writes=[xn])
                            if r < 5:
                                S.op('act', lambda e: e.activation(out=BDM[d][k1], in_=psM[:, 0:512].rearrange("p (a c) -> p a c", a=4), func=AF.Copy),
                                     reads=[pnM], writes=['BDM%d%d' % (d, k1)])
                                if r < 4:
                                    S.op('act', lambda e: e.activation(out=BDL[d][k1], in_=psL[:, 0:512].rearrange("p (a c) -> p a c", a=4), func=AF.Copy),
                                         reads=[pnL], writes=['BDL%d%d' % (d, k1)])
                            else:
                                psU, pnU = bank()
                                for q in range(4):
                                    S.op('pe', lambda e, q=q: e.matmul(psU[0:64, q * 64:(q + 1) * 64], lhsT=ident[:, 64:128], rhs=X[:, q, :], start=True, stop=True),
                                         reads=['ident', xn], writes=[pnU])
                                S.op('act', lambda e: e.activation(out=xb[:, 0:8:2, :], in_=X[0:64, :, :], func=AF.Copy), reads=[xn], writes=[(xbn, 0)])
                                S.op('dve', lambda e: e.tensor_copy(out=xb[:, 1:8:2, :], in_=psU[0:64, 0:256].rearrange("p (a c) -> p a c", a=4)),
                                     reads=[pnU], writes=[(xbn, 1)])
                        return f

                    def s_Y():
                        ps, pn = bank()
                        for h in range(8):
                            hp_, hh = h // 2, h % 2
                            o = ps[hh * 64:(hh + 1) * 64, hp_ * 64:(hp_ + 1) * 64]
                            S.op('pe', lambda e, o=o, h=h: e.matmul(o, lhsT=stb[:, h, :], rhs=ab[:, h, 1, :], start=True, stop=False),
                                 reads=[stbn, abn], writes=[pn])
                            S.op('pe', lambda e, o=o, h=h: e.matmul(o, lhsT=xb[:, h, :], rhs=arb[:, h, :], start=False, stop=False),
                                 reads=[xbn, arbn], writes=[pn])
                            S.op('pe', lambda e, o=o, h=h: e.matmul(o, lhsT=vt[:, h * 64:(h + 1) * 64], rhs=ka[:, h, 64:128], start=False, stop=True),
                                 reads=[vtn, kan], writes=[pn])
                        S.op('dve', lambda e, ps=ps: e.tensor_tensor(
                            out=Ysum[:, :, c0:c1], in0=Ysum[:, :, c0:c1], in1=ps[:, 0:256].rearrange("p (a t) -> p a t", a=4), op=ALU.add),
                            reads=[pn, ('Ysum', c)], writes=[('Ysum', c)])

                    def s_S():
                        ps, pn = bank()
                        for h in range(8):
                            o = ps[0:64, h * 64:(h + 1) * 64]
                            S.op('pe', lambda e, o=o, h=h: e.matmul(o, lhsT=bk[:, 0, h * 64:(h + 1) * 64], rhs=xb[:, h, :], start=True, stop=False),
                                 reads=[bkn, xbn], writes=[pn])
                            S.op('pe', lambda e, o=o, h=h: e.matmul(o, lhsT=bk[:, 1, h * 64:(h + 1) * 64], rhs=vt[:, h * 64:(h + 1) * 64], start=False, stop=True),
                                 reads=[bkn, vtn], writes=[pn])
                        S.op('dve', lambda e: e.tensor_tensor(out=st_, in0=st_, in1=bc(PCs[d][:, :, c], 2, [64, 8, 64]), op=ALU.mult),
                             reads=[stn, 'PC%d' % d], writes=[stn])
                        S.op('dve', lambda e, ps=ps: e.tensor_tensor(
                            out=st_, in0=st_, in1=ps[0:64, 0:512].rearrange("p (h c) -> p h c", h=8), op=ALU.add),
                            reads=[stn, pn], writes=[stn])
                        S.op('act', lambda e: e.activation(out=stb, in_=st_, func=AF.Copy), reads=[stn], writes=[stbn])
                    return [s_G, s_X] + [mk_round(r) for r in range(6)] + [s_Y, s_S]

                for d in range(2):
                    loads(d, 0)
                for s in range(NCH):
                    if s + 1 < NCH:
                        for d in range(2):
                            loads(d, s + 1)
                    stg0 = step_stages(0, s)
                    stg1 = step_stages(1, s)
                    for f0, f1 in zip(stg0, stg1):
                        f0()
                        if g.ret_stop != 'rw_one':
                            f1()

                if g.ret_stop in ('rw_r1', 'rw_one'):
                    S.barrier()
                    return
                bonesn = bp.alloc([128, 128], F32)
                S.op('pool', lambda e: e.memset(bonesn, 0.0), writes=['bonesn'])
                S.op('pool', lambda e: e.memset(bonesn[0:64, 0:64], 1.0 / 64), reads=['bonesn'], writes=['bonesn'])
                S.op('pool', lambda e: e.memset(bonesn[64:128, 64:128], 1.0 / 64), reads=['bonesn'], writes=['bonesn'])
                epsg = bp.alloc([128, 1], F32)
                S.op('pool', lambda e: e.memset(epsg, 64e-5), writes=['epsg'])
                vec2 = bp.alloc([128, 5, 4], F32)
                S.dma('sp', lambda e: e.dma_start(out=vec2, in_=I['rw_vecT'][l]), writes=['vec2'])
                ysq2 = [bp.alloc([128, 512], F32) for i in range(2)]
                mean2 = [bp.alloc([128, 512], F32) for i in range(2)]
                var2 = [bp.alloc([128, 512], F32) for i in range(2)]
                dd2 = [bp.alloc([128, 512], F32) for i in range(2)]
                bon2 = [bp.alloc([128, 512], F32) for i in range(2)]
                gg2 = [bp.alloc([128, 512], F32) for i in range(2)]
                r2it = [0]
                ost = bp.alloc([128, NT], BF16)
                for hp_ in range(4):
                    for bi, (c0, c1) in enumerate(BLOCKS):
                        n = c1 - c0
                        pp_ = r2it[0] % 2
                        r2it[0] += 1
                        ysq, mean, var, dd, bon, gg = ysq2[pp_], mean2[pp_], var2[pp_], dd2[pp_], bon2[pp_], gg2[pp_]
                        sx = '_%d' % pp_
                        S.dma('sp', lambda e, c0=c0, c1=c1, n=n: e.dma_start(out=bon[:, :n], in_=bon_d[hp_ * 128:(hp_ + 1) * 128, c0:c1]), writes=['bon' + sx])
                        S.dma('sp', lambda e, c0=c0, c1=c1, n=n: e.dma_start(out=gg[:, :n], in_=g_d[hp_ * 128:(hp_ + 1) * 128, c0:c1]), writes=['gg' + sx])
                        S.op('act', lambda e, c0=c0, c1=c1, n=n: e.activation(out=ysq[:, :n], in_=Ysum[:, hp_, c0:c1], func=AF.Square),
                             reads=['Ysum'], writes=['ysq' + sx])
                        psM, pnM = bank()
                        psQ, pnQ = bank()
                        S.op('pe', lambda e, psM=psM, c0=c0, c1=c1, n=n: e.matmul(psM[:, :n], lhsT=bonesn, rhs=Ysum[:, hp_, c0:c1], start=True, stop=True),
                             reads=['bonesn', 'Ysum'], writes=[pnM])
                        S.op('pe', lambda e, psQ=psQ, n=n: e.matmul(psQ[:, :n], lhsT=bonesn, rhs=ysq[:, :n], start=True, stop=True),
                             reads=['bonesn', 'ysq' + sx], writes=[pnQ])
                        S.op('act', lambda e, psM=psM, n=n: e.activation(out=mean[:, :n], in_=psM[:, :n], func=AF.Copy), reads=[pnM], writes=['mean' + sx])
                        S.op('pool', lambda e, n=n: e.tensor_tensor(out=var[:, :n], in0=mean[:, :n], in1=mean[:, :n], op=ALU.mult), reads=['mean' + sx], writes=['var' + sx])
                        S.op('dve', lambda e, psQ=psQ, n=n: e.tensor_tensor(out=var[:, :n], in0=psQ[:, :n], in1=var[:, :n], op=ALU.subtract),
                             reads=[pnQ, 'var' + sx], writes=['var' + sx])
                        S.op('act', lambda e, n=n: e.activation(out=var[:, :n], in_=var[:, :n], func=AF.Sqrt, bias=epsg[:, 0:1], scale=1.0),
                             reads=['var' + sx, 'epsg'], writes=['var' + sx])
                        S.op('dve', lambda e, n=n: e.reciprocal(out=var[:, :n], in_=var[:, :n]), reads=['var' + sx], writes=['var' + sx])
                        S.op('dve', lambda e, c0=c0, c1=c1, n=n: e.tensor_tensor(out=dd[:, :n], in0=Ysum[:, hp_, c0:c1], in1=mean[:, :n], op=ALU.subtract),
                             reads=['Ysum', 'mean' + sx], writes=['dd' + sx])
                        S.op('dve', lambda e, n=n: e.tensor_tensor(out=dd[:, :n], in0=dd[:, :n], in1=var[:, :n], op=ALU.mult), reads=['dd' + sx, 'var' + sx], writes=['dd' + sx])
                        S.op('act', lambda e, n=n: e.activation(out=dd[:, :n], in_=dd[:, :n], func=AF.Identity,
                                                                bias=vec2[:, 4, hp_:hp_ + 1], scale=vec2[:, 3, hp_:hp_ + 1]),
                             reads=['dd' + sx, 'vec2'], writes=['dd' + sx])
                        S.op('dve', lambda e, n=n: e.tensor_tensor(out=dd[:, :n], in0=dd[:, :n], in1=bon[:, :n], op=ALU.add), reads=['dd' + sx, 'bon' + sx], writes=['dd' + sx])
                        S.op('dve', lambda e, c0=c0, c1=c1, n=n: e.tensor_tensor(out=ost[:, c0:c1], in0=dd[:, :n], in1=gg[:, :n], op=ALU.mult),
                             reads=['dd' + sx, 'gg' + sx], writes=['ost'])
                    S.dma('sp', lambda e: e.dma_start(out=yT_d[1536 + hp_ * 128:1536 + (hp_ + 1) * 128, :], in_=ost), reads=['ost'], writes=['yT_d'])
                S.barrier()
        for l in range(2):
            last = (l == 1)
            ffn(l, 0, I['ffn1_w_in'][l], I['ffn1_w_out'][l])
            tapX('x_ffn1_%d' % l)
            if stop_after == ('ffn1', l):
                break
            W = I['mix_w_in'][l]
            hxT = compute_hx(l)
            attention(l, W, hxT, list(range(2, 18)) if last else list(range(18)))
            tapY('att%d' % l, 1024, 1536)
            if stop_after == ('att', l):
                break
            retention(l, W, hxT)
            tapY('ret%d' % l, 0, 1024)
            if stop_after == ('ret', l):
                break
            rwkv(l, W, hxT)
            tapY('rw%d' % l, 1536, 2048)
            if stop_after == ('rw', l):
                break
            bsel = [1, 2, 3, 4] if last else [0, 1, 2, 3, 4]
            merge(l, W, hxT, bsel)
            tapX('x_mix%d' % l)
            if stop_after == ('mix', l):
                break
            ffn(l, 2, I['ffn2_w_in'][l], I['ffn2_w_out'][l], bsel)
        store_out()
        S.emit()
    return nc, g


INPUT_SHAPES = [
    ("xT", [D, NT]), ("condT", [128, KC, 2]), ("ada_w", [2, D, 9 * D]), ("ada_bT", [128, 2, 72]),
    ("norm_wT", [128, 2, 3, KC]),
    ("ffn1_w_in", [2, D, 2 * DFF]), ("ffn1_w_out", [2, DFF, D]), ("ffn2_w_in", [2, D, 2 * DFF]), ("ffn2_w_out", [2, DFF, D]),
    ("mix_w_in", [2, D, NIN]), ("ret_lg", [2, 16]), ("att_qn", [2, 64]), ("att_kn", [2, 64]), ("att_sink", [2, 8]),
    ("w_branch_ret", [2, 1024, D]), ("w_branch_att", [2, 512, D]), ("w_branch_rwkv", [2, 512, D]), ("w_out", [2, D, D]),
    ("attcs", [128, 16, 2, 64]), ("retcs", [128, 18, 2, 64]), ("cmask", [128, 7, 128]), ("pos4", [128, 4]),
] + RW_INPUT_SHAPES


def _f32(a):
    return np.ascontiguousarray(a, dtype=np.float32)


def _rope_tab(ang):
    c = np.cos(ang)
    s = np.sin(ang)
    return np.stack([np.concatenate([c, c], -1), np.concatenate([-s, s], -1)], axis=1)


def host_consts():
    cst = {}
    rows = NLAT // 64
    row = np.repeat(np.arange(rows), 64).astype(np.float32)
    col = (np.arange(NLAT) % 64).astype(np.float32)
    inv16 = np.power(np.float32(10000.0), -(np.arange(16, dtype=np.float32) / 16)).astype(np.float32)
    ang = np.concatenate([row[:, None] * inv16[None, :], col[:, None] * inv16[None, :]], -1).astype(np.float32)
    tab = _rope_tab(ang.astype(np.float64))
    cst['attcs'] = _f32(tab.reshape(16, 128, 2, 64).transpose(1, 0, 2, 3))
    inv32 = np.power(np.float32(10000.0), -(np.arange(32, dtype=np.float32) / 32)).astype(np.float32)
    pos = np.arange(NT, dtype=np.float32)
    angr = (pos[:, None] * inv32[None, :]).astype(np.float32)
    tabr = _rope_tab(angr.astype(np.float64))
    cst['retcs'] = _f32(tabr.reshape(18, 128, 2, 64).transpose(1, 0, 2, 3))
    m = np.arange(128)[:, None]
    j = np.arange(128)[None, :]
    cm = np.stack([(m < j), (m > j), 2.0 * (m == j), np.maximum(j - m, 0), np.maximum(m - j, 0), (m >= j), (m <= j)], axis=1)
    cst['cmask'] = _f32(cm)
    p = np.arange(128, dtype=np.float32)
    cst['pos4'] = _f32(np.stack([127 - p, p, p + 1, 128 - p], axis=1))
    cst.update(rw_host_consts())
    return cst


def prep_shared(inp):
    sh = {}
    sh['ada_w'] = _f32(inp['ada_w'])
    sh['ada_bT'] = _f32(inp['ada_b'].reshape(2, 72, 128).transpose(2, 0, 1))
    sh['norm_wT'] = _f32(inp['norm_w'].reshape(2, 3, KC, 128).transpose(3, 0, 1, 2))
    for k in ['ffn1_w_in', 'ffn1_w_out', 'ffn2_w_in', 'ffn2_w_out', 'mix_w_in',
              'w_branch_ret', 'w_branch_att', 'w_branch_rwkv', 'w_out', 'att_sink']:
        sh[k] = _f32(inp[k])
    sh['ret_lg'] = _f32(inp['ret_decay_logit'].reshape(2, 16))
    sh['att_qn'] = _f32(inp['att_q_norm'])
    sh['att_kn'] = _f32(inp['att_k_norm'])
    sh.update(rw_prep_shared(inp))
    sh.update(host_consts())
    return sh


def prep_core(inp, b):
    pc = {}
    xs = np.concatenate([inp['ctx'][b], inp['x'][b]], axis=0)
    pc['xT'] = _f32(xs.T)
    cc = np.stack([inp['c'][b], inp['c_ctx']], axis=-1)
    pc['condT'] = _f32(cc.reshape(KC, 128, 2).transpose(1, 0, 2))
    return pc


def kernel(**inp):
    nc, g = build()
    sh = prep_shared(inp)
    in_maps = []
    for b in range(8):
        m = dict(sh)
        m.update(prep_core(inp, b))
        in_maps.append(m)
    res = run_bass_kernel_spmd(nc, in_maps, core_ids=list(range(8)))
    out = np.stack([np.asarray(res.results[b]['outT']).T for b in range(8)], axis=0)
    return np.ascontiguousarray(out.astype(np.float32))
```

```python
import numpy as np
from contextlib import ExitStack
import concourse.bass as bass
import concourse.mybir as mybir
from concourse.bass_utils import run_bass_kernel_spmd

F32 = mybir.dt.float32
BF16 = mybir.dt.bfloat16
AF = mybir.ActivationFunctionType
ALU = mybir.AluOpType
AX = mybir.AxisListType


class _Rec:
    def __init__(self):
        self.call = None

    def __getattr__(self, name):
        def f(*a, **k):
            self.call = (name, a, k)
            return self
        return f


def _capture(fn):
    r = _Rec()
    fn(r)
    name, a, k = r.call
    return lambda e: getattr(e, name)(*a, **k)


class Sched:
    ENG = ['pe', 'act', 'dve', 'pool', 'sp']

    def __init__(self, nc, stack, ndma=16):
        self.nc = nc
        self.prog = {e: [] for e in self.ENG}
        self.sems = {}
        self.count = {}
        for e in ['pe', 'act', 'dve', 'pool']:
            self.sems[e] = stack.enter_context(nc.semaphore('s_' + e))
            self.count[e] = 0
        self.ndma = ndma
        self.dma_rr = {}
        for q in ('sp', 'pool', 'act'):
            self.dma_rr[q] = 0
            for i in range(ndma):
                nm = 'd_%s%d' % (q, i)
                self.sems[nm] = stack.enter_context(nc.semaphore('s_' + nm))
                self.count[nm] = 0
        self.waited = {e: {} for e in self.ENG}
        self.state = {}
        self.keys = {}
        self.nops = 0

    def _st(self, slot):
        s = self.state.get(slot)
        if s is None:
            s = {'w': None, 'r': {}}
            self.state[slot] = s
        return s

    def _slots(self, tid, key):
        ks = self.keys.setdefault(tid, set())
        if key is None:
            return [(tid, k) for k in ks] + [(tid, None)]
        ks.add(key)
        return [(tid, key), (tid, None)]

    def _norm(self, accs):
        out = []
        for a in accs:
            if isinstance(a, tuple):
                out.append((a[0], a[1]))
            else:
                out.append((a, None))
        return out

    def _deps(self, eng, reads, writes):
        own = eng if eng in self.count else None
        need = {}

        def add(sem, val, kind):
            if sem == own:
                if eng == 'pe':
                    return
            if need.get(sem, 0) < val:
                need[sem] = val

        for (tid, key) in reads:
            is_psum = isinstance(tid, str) and tid.startswith('ps')
            for sl in self._slots(tid, key):
                s = self.state.get(sl)
                if s and s['w']:
                    add(s['w'][0], s['w'][1], 'RAW')
                if s and is_psum:
                    for sem, val in s['r'].items():
                        if sem != own:
                            add(sem, val, 'RAR')
        for (tid, key) in writes:
            for sl in self._slots(tid, key):
                s = self.state.get(sl)
                if s:
                    if s['w']:
                        add(s['w'][0], s['w'][1], 'WAW')
                    for sem, val in s['r'].items():
                        add(sem, val, 'WAR')
        waits = []
        wd = self.waited[eng]
        for sem, val in need.items():
            if wd.get(sem, 0) < val:
                wd[sem] = val
                waits.append((sem, val))
        return waits

    def _record(self, reads, writes, sem, val):
        for (tid, key) in reads:
            self._slots(tid, key)
            self._st((tid, key))['r'][sem] = val
        for (tid, key) in writes:
            if key is None:
                for k in list(self.keys.get(tid, ())):
                    self.state.pop((tid, k), None)
                self.keys[tid] = set()
            else:
                self._slots(tid, key)
            s = self._st((tid, key))
            s['w'] = (sem, val)
            s['r'] = {}

    def op(self, eng, fn, reads=(), writes=()):
        reads = self._norm(reads)
        writes = self._norm(writes)
        waits = self._deps(eng, reads, writes)
        self.count[eng] += 1
        val = self.count[eng]
        self.prog[eng].append((waits, _capture(fn), (eng, 1)))
        self._record(reads, writes, eng, val)
        self.nops += 1

    def dma(self, queue, fn, reads=(), writes=()):
        reads = self._norm(reads)
        writes = self._norm(writes)
        nm = 'd_%s%d' % (queue, self.dma_rr[queue])
        self.dma_rr[queue] = (self.dma_rr[queue] + 1) % self.ndma
        waits = self._deps(queue, reads, writes)
        wd = self.waited[queue]
        if wd.get(nm, 0) < self.count[nm]:
            wd[nm] = self.count[nm]
            waits.append((nm, self.count[nm]))
        self.count[nm] += 16
        val = self.count[nm]
        self.prog[queue].append((waits, _capture(fn), (nm, 16)))
        self._record(reads, writes, nm, val)
        self.nops += 1

    def barrier(self):
        for e in self.ENG:
            waits = []
            wd = self.waited[e]
            for sem, cnt in self.count.items():
                if cnt > 0 and wd.get(sem, 0) < cnt:
                    wd[sem] = cnt
                    waits.append((sem, cnt))
            if waits:
                self.prog[e].append((waits, None, None))
        self.state = {}
        self.keys = {}

    def emit(self):
        nc = self.nc
        self.barrier()
        with nc.Block() as block:
            def run(name):
                def f(e):
                    for waits, fn, inc in self.prog[name]:
                        for (s, v) in waits:
                            e.wait_ge(self.sems[s], v)
                        if fn is not None:
                            ins = fn(e)
                            ins.then_inc(self.sems[inc[0]], inc[1])
                return f
            block.tensor(run('pe'))
            block.scalar(run('act'))
            block.vector(run('dve'))
            block.gpsimd(run('pool'))
            block.sync(run('sp'))


NT = 2304
NCTX = 256
NLAT = 2048
D = 1024
KC = 8
DFF = 2816
NFC = 22
NIN = 8832
BLOCKS = [(0, 256), (256, 768), (768, 1280), (1280, 1792), (1792, 2304)]
SUPER = [[0, 1], [2], [3], [4]]
EPS = 1e-6


RW_INPUT_SHAPES = [
    ("rw_shT", [2, 128, 15, 3]), ("rw_w0T", [2, 128, 2, 4]), ("rw_a0T", [2, 128, 2, 4]), ("rw_vecT", [2, 128, 5, 4]),
    ("rw_v0T", [128, 4]), ("rwkv_w_up", [2, 2, 64, 512]), ("rwkv_a_up", [2, 2, 64, 512]), ("rwkv_g_up", [2, 128, 512]),
    ("rwkv_v_down", [1, 1024, 32]), ("rwkv_v_up", [1, 32, 512]),
]


def rw_host_consts():
    return {}


def rw_prep_shared(inp):
    f = lambda a: np.ascontiguousarray(a, dtype=np.float32)
    sh = {}
    sh['rw_shT'] = f(inp['rwkv_shift'].reshape(2, 3, 15, 128).transpose(0, 3, 2, 1))
    sh['rw_w0T'] = f(inp['rwkv_w0'].reshape(2, 2, 4, 128).transpose(0, 3, 1, 2))
    sh['rw_a0T'] = f(inp['rwkv_a0'].reshape(2, 2, 4, 128).transpose(0, 3, 1, 2))
    vec = np.stack([inp['rwkv_k_k'], inp['rwkv_k_a'], inp['rwkv_r_k'].reshape(2, 512), inp['rwkv_gn_w'], inp['rwkv_gn_b']], axis=1)
    sh['rw_vecT'] = f(vec.reshape(2, 5, 4, 128).transpose(0, 3, 1, 2))
    sh['rw_v0T'] = f(inp['rwkv_v0'][0].reshape(4, 128).T)
    for k in ['rwkv_w_up', 'rwkv_a_up', 'rwkv_g_up', 'rwkv_v_down', 'rwkv_v_up']:
        sh[k] = f(inp[k])
    return sh


class Ctx:
    pass


AW = 50000
XTW = KC * NT


def build(stop_after=None, taps=(), ret_stop=None):
    nc = bass.Bass("TRN2", target_bir_lowering=False)
    g = Ctx()
    g.nc = nc
    g.taps = {}
    g.ret_stop = ret_stop

    def din(name, shape):
        return nc.dram_tensor(name, list(shape), F32, kind="ExternalInput").ap()

    def dout(name, shape, dt=F32):
        return nc.dram_tensor(name, list(shape), dt, kind="ExternalOutput").ap()

    I = {}
    for name, shape in INPUT_SHAPES:
        I[name] = din(name, shape)
    outT = dout("outT", [D, NLAT])
    g.I = I
    yT_d = nc.dram_tensor("yT_d", [2048, NT], BF16).ap()
    xsp_d = nc.dram_tensor("xsp_d", [128, KC, NT], F32).ap()
    vf_d = nc.dram_tensor("vf_d", [512, NT], F32).ap()
    scrT = nc.dram_tensor("scrT", [2, 36, 64, 8, 4, 64], BF16).ap()
    scrM = nc.dram_tensor("scrM", [2, 2, NT, 512], BF16).ap()
    vtm_d = nc.dram_tensor("vtm_d", [NT, 512], BF16).ap()
    pcs_d = nc.dram_tensor("pcs_d", [2, 512, 36], F32).ap()
    bon_d = nc.dram_tensor("bon_d", [512, NT], F32).ap()
    g_d = nc.dram_tensor("g_d", [512, NT], F32).ap()
    wg2_d = nc.dram_tensor("wg2_d", [2, 4, 128, KC, 3, 256], BF16).ap()
    wb2_d = nc.dram_tensor("wb2_d", [2, 4, 128, 16, 256], BF16).ap()
    wo2_d = nc.dram_tensor("wo2_d", [2, 4, 128, KC, 256], BF16).ap()

    with ExitStack() as st:
        S = Sched(nc, st)
        g.S = S
        uid = [0]

        def T(stack, name, shape, dt):
            uid[0] += 1
            return stack.enter_context(nc.sbuf_tensor("%s_%d" % (name, uid[0]), list(shape), dt))

        def PS(stack, name, shape, dt):
            uid[0] += 1
            return stack.enter_context(nc.psum_tensor("%s_%d" % (name, uid[0]), list(shape), dt))

        arena = T(st, "arena", [128, AW], F32)

        class Bump:
            def __init__(self, ranges):
                self.ranges = [list(r) for r in ranges]

            def alloc(self, shape, dt):
                n = 1
                for s_ in shape[1:]:
                    n *= s_
                words = n if dt == F32 else (n + 1) // 2
                for r in self.ranges:
                    if r[0] + words <= r[1]:
                        off = r[0]
                        r[0] += words
                        break
                else:
                    raise RuntimeError("arena overflow %s %s" % (shape, self.ranges))
                ap = arena[:shape[0], off:off + words]
                if dt != F32:
                    ap = ap.bitcast(dt)[:, :n]
                if len(shape) > 2:
                    names = ["d%d" % i for i in range(len(shape) - 1)]
                    kw = {nm: s_ for nm, s_ in zip(names, shape[1:])}
                    ap = ap.rearrange("p (%s) -> p %s" % (" ".join(names), " ".join(names)), **kw)
                return ap
        g.Bump = Bump

        XT = arena[:, 0:XTW].rearrange("p (k n) -> p k n", k=KC)
        g.XT = XT

        cond = T(st, "cond", [128, KC, 2], F32)
        modT = T(st, "modT", [128, 2, 72, 2], F32)
        Amod = T(st, "Amod", [128, 2, 3, KC, 2], F32)
        Gmod = T(st, "Gmod", [128, 2, 3, KC, 2], F32)
        normw = T(st, "normw", [128, 2, 3, KC], F32)
        adab = T(st, "adab", [128, 2, 72], F32)
        ones_bf = T(st, "ones_bf", [128, 128], BF16)
        ident = T(st, "ident", [128, 128], F32)
        ident_bf = T(st, "ident_bf", [128, 128], BF16)
        epst = T(st, "epst", [128, 1], F32)
        cmask = T(st, "cmask", [128, 7, 128], F32)
        g.cmask = cmask

        def blk_of(c0):
            for i, (a, b) in enumerate(BLOCKS):
                if a <= c0 < b:
                    return i
            raise ValueError(c0)

        for bi, (c0, c1) in enumerate(BLOCKS):
            S.dma('sp', lambda e, c0=c0, c1=c1: e.dma_start(
                out=XT[:, :, c0:c1], in_=I['xT'][:, c0:c1].rearrange("(k p) n -> p k n", p=128)),
                writes=[('XT', bi)])
        S.dma('sp', lambda e: e.dma_start(out=cond[:], in_=I['condT']), writes=['cond'])
        S.dma('sp', lambda e: e.dma_start(out=normw[:], in_=I['norm_wT']), writes=['normw'])
        S.dma('sp', lambda e: e.dma_start(out=adab[:], in_=I['ada_bT']), writes=['adab'])
        S.dma('sp', lambda e: e.dma_start(out=cmask[:], in_=I['cmask']), writes=['cmask'])
        S.op('pool', lambda e: e.memset(ones_bf[:], 1.0), writes=['ones_bf'])
        S.op('pool', lambda e: e.memset(epst[:], EPS), writes=['epst'])
        S.op('pool', lambda e: e.memset(ident[:], 1.0), writes=['ident'])
        S.op('pool', lambda e: e.affine_select(out=ident[:], in_=ident[:], pattern=[[-1, 128]],
                                               compare_op=ALU.is_equal, fill=0.0, base=0, channel_multiplier=1),
             reads=['ident'], writes=['ident'])
        S.op('dve', lambda e: e.tensor_copy(out=ident_bf[:], in_=ident[:]), reads=['ident'], writes=['ident_bf'])
        S.op('act', lambda e: e.activation(out=cond[:], in_=cond[:], func=AF.Silu), reads=['cond'], writes=['cond'])

        def make_banks(ph, nf32=8):
            banks = [PS(ph, "ps%d" % i, [128, 512], F32) for i in range(nf32)]
            g.bank_i = 0

            def bank():
                i = g.bank_i
                g.bank_i = (i + 1) % nf32
                return banks[i], "ps%d" % i
            g.bank = bank
            return bank

        with ExitStack() as ph:
            bp = Bump([(XTW, AW)])
            wa = [bp.alloc([128, KC, 512], F32) for i in range(3)]
            psm = PS(ph, "psmod", [128, 72, 2], F32)
            n = 0
            for l in range(2):
                for slab in range(18):
                    w = wa[n % 3]
                    wn = "adaw%d" % (n % 3)
                    n += 1
                    S.dma('sp', lambda e, w=w, l=l, slab=slab: e.dma_start(
                        out=w, in_=I['ada_w'][l, :, slab * 512:(slab + 1) * 512].rearrange("(k p) n -> p k n", p=128)),
                        writes=[wn])
                    for j in range(4):
                        ch = slab * 4 + j
                        for k in range(KC):
                            S.op('pe', lambda e, w=w, j=j, k=k, ch=ch: e.matmul(
                                psm[:, ch, :], lhsT=w[:, k, j * 128:(j + 1) * 128], rhs=cond[:, k, :],
                                start=(k == 0), stop=(k == KC - 1)),
                                reads=[wn, 'cond'], writes=['psmod'])
                S.op('dve', lambda e, l=l: e.tensor_tensor(
                    out=modT[:, l], in0=psm[:], in1=adab[:, l].unsqueeze(2).to_broadcast([128, 72, 2]), op=ALU.add),
                    reads=['psmod', 'adab'], writes=['modT'])
                for sub in range(3):
                    ms = 3 * sub + 1
                    mg = 3 * sub + 2
                    S.op('dve', lambda e, l=l, sub=sub, ms=ms: e.scalar_tensor_tensor(
                        out=Amod[:, l, sub], in0=modT[:, l, ms * 8:(ms + 1) * 8, :], scalar=1.0,
                        in1=normw[:, l, sub].unsqueeze(2).to_broadcast([128, KC, 2]), op0=ALU.add, op1=ALU.mult),
                        reads=['modT', 'normw'], writes=['Amod'])
                    S.op('dve', lambda e, l=l, sub=sub, mg=mg: e.tensor_scalar(
                        out=Gmod[:, l, sub], in0=modT[:, l, mg * 8:(mg + 1) * 8, :],
                        scalar1=(1.0 if sub == 1 else 0.5), scalar2=None, op0=ALU.mult),
                        reads=['modT'], writes=['Gmod'])
            S.barrier()

        def tap(name, ap_fn, shape, reads, dt=F32):
            if name in taps:
                t = dout("tap_" + name, shape, dt)
                g.taps[name] = t
                S.dma('sp', lambda e: e.dma_start(out=t, in_=ap_fn()), reads=reads)

        tap('modT', lambda: modT[:], [128, 2, 72, 2], ['modT'])

        def norm_mod(ph_t, l, sub, c0, c1, hT, hname, hkey, off):
            n = c1 - c0
            bi = blk_of(c0)
            which = 1 if c0 < NCTX else 0
            sq, rstd, tmps = ph_t['sq'], ph_t['rstd'], ph_t['tmp']
            S.op('act', lambda e: e.activation(out=sq[:, :, :n], in_=XT[:, :, c0:c1], func=AF.Square),
                 reads=[('XT', bi)], writes=['sq'])
            ps, pn = g.bank()
            for k in range(KC):
                S.op('pe', lambda e, k=k: e.matmul(ps[:, :n], lhsT=ones_bf[:], rhs=sq[:, k, :n],
                                                   start=(k == 0), stop=(k == KC - 1)),
                     reads=['sq', 'ones_bf'], writes=[pn])
            S.op('act', lambda e: e.activation(out=rstd[:, :n], in_=ps[:, :n], func=AF.Sqrt,
                                               bias=epst[:, 0:1], scale=1.0 / D),
                 reads=[pn, 'epst'], writes=['rstd'])
            S.op('dve', lambda e: e.reciprocal(out=rstd[:, :n], in_=rstd[:, :n]), reads=['rstd'], writes=['rstd'])
            msh = 3 * sub
            for k in range(KC):
                tmp = tmps[k % 2]
                tn = 'nm_tmp%d' % (k % 2)
                S.op('dve', lambda e, k=k, tmp=tmp: e.scalar_tensor_tensor(
                    out=tmp[:, :n], in0=XT[:, k, c0:c1], scalar=Amod[:, l, sub, k, which:which + 1],
                    in1=rstd[:, :n], op0=ALU.mult, op1=ALU.mult),
                    reads=[('XT', bi), 'Amod', 'rstd'], writes=[tn])
                S.op('act', lambda e, k=k, tmp=tmp: e.activation(
                    out=hT[:, k, off:off + n], in_=tmp[:, :n], func=AF.Identity,
                    bias=modT[:, l, msh * 8 + k, which:which + 1], scale=1.0),
                    reads=[tn, 'modT'], writes=[(hname, hkey)])

        def norm_tiles(bp):
            return {'sq': bp.alloc([128, KC, 512], BF16), 'rstd': bp.alloc([128, 512], F32),
                    'tmp': [bp.alloc([128, 512], F32), bp.alloc([128, 512], F32)]}

        def ffn(l, sub, w_in, w_out, blocks_sel=None):
            with ExitStack() as ph:
                bank = make_banks(ph, 8)
                bp = Bump([(XTW, AW)])
                nt = norm_tiles(bp)
                hT = bp.alloc([128, KC, 768], BF16)
                actT = bp.alloc([128, NFC, 768], BF16)
                wi = [bp.alloc([128, KC, 2, 512], BF16) for i in range(2)]
                wo = [bp.alloc([128, NFC, 256], BF16) for i in range(2)]
                sg = [bp.alloc([128, 512], F32) for i in range(2)]
                cnt = {'wi': 0, 'wo': 0, 'sg': 0}
                sbs = []
                for sb0 in SUPER:
                    sb = [b for b in sb0 if (blocks_sel is None or b in blocks_sel)]
                    if not sb:
                        continue
                    offs = []
                    o = 0
                    for b in sb:
                        offs.append(o)
                        o += BLOCKS[b][1] - BLOCKS[b][0]
                    sbs.append((sb, offs))

                def do_norm(i):
                    sb, offs = sbs[i]
                    for lb, b in enumerate(sb):
                        norm_mod(nt, l, sub, BLOCKS[b][0], BLOCKS[b][1], hT, 'hT', lb, offs[lb])

                def do_win(i):
                    sb, offs = sbs[i]
                    for gi in range(6):
                        nfc = min(4, NFC - gi * 4)
                        w = wi[cnt['wi'] % 2]
                        wn = "wi%d" % (cnt['wi'] % 2)
                        cnt['wi'] += 1
                        for half in range(2):
                            cs = half * DFF + gi * 512
                            S.dma('pool', lambda e: e.dma_start(
                                out=w[:, :, half, :nfc * 128],
                                in_=w_in[:, cs:cs + nfc * 128].rearrange("(k p) n -> p k n", p=128)),
                                writes=[(wn, half)])
                        for j in range(nfc):
                            fc = gi * 4 + j
                            for lb, b in enumerate(sb):
                                n = BLOCKS[b][1] - BLOCKS[b][0]
                                off = offs[lb]
                                psg, png = bank()
                                psu, pnu = bank()
                                for half, (pp, pnn) in enumerate(((psg, png), (psu, pnu))):
                                    for k in range(KC):
                                        S.op('pe', lambda e: e.matmul(
                                            pp[:, :n], lhsT=w[:, k, half, j * 128:(j + 1) * 128], rhs=hT[:, k, off:off + n],
                                            start=(k == 0), stop=(k == KC - 1)),
                                            reads=[(wn, half), ('hT', lb)], writes=[pnn])
                                s_ = sg[cnt['sg'] % 2]
                                sn = "sg%d" % (cnt['sg'] % 2)
                                cnt['sg'] += 1
                                S.op('act', lambda e: e.activation(out=s_[:, :n], in_=psg[:, :n], func=AF.Silu),
                                     reads=[png], writes=[sn])
                                S.op('dve', lambda e: e.tensor_tensor(
                                    out=actT[:, fc, off:off + n], in0=psu[:, :n], in1=s_[:, :n], op=ALU.mult),
                                    reads=[pnu, sn], writes=[('actT', (fc, lb))])

                def do_wout(i):
                    sb, offs = sbs[i]
                    for dp in range(4):
                        w = wo[cnt['wo'] % 2]
                        wn = "wo%d" % (cnt['wo'] % 2)
                        cnt['wo'] += 1
                        S.dma('pool', lambda e: e.dma_start(
                            out=w, in_=w_out[:, dp * 256:(dp + 1) * 256].rearrange("(f p) n -> p f n", p=128)),
                            writes=[wn])
                        for dl in range(2):
                            dc = dp * 2 + dl
                            for lb, b in enumerate(sb):
                                c0, c1 = BLOCKS[b]
                                n = c1 - c0
                                off = offs[lb]
                                which = 1 if c0 < NCTX else 0
                                ps, pn = bank()
                                for f in range(NFC):
                                    S.op('pe', lambda e: e.matmul(
                                        ps[:, :n], lhsT=w[:, f, dl * 128:(dl + 1) * 128], rhs=actT[:, f, off:off + n],
                                        start=(f == 0), stop=(f == NFC - 1)),
                                        reads=[wn, ('actT', (f, lb))], writes=[pn])
                                S.op('dve', lambda e: e.scalar_tensor_tensor(
                                    out=XT[:, dc, c0:c1], in0=ps[:, :n], scalar=Gmod[:, l, sub, dc, which:which + 1],
                                    in1=XT[:, dc, c0:c1], op0=ALU.mult, op1=ALU.add),
                                    reads=[pn, 'Gmod', ('XT', b)], writes=[('XT', b)])

                do_norm(0)
                for i in range(len(sbs)):
                    do_win(i)
                    if i + 1 < len(sbs):
                        do_norm(i + 1)
                    do_wout(i)
                S.barrier()

        def store_out():
            for bi, (c0, c1) in enumerate(BLOCKS):
                if c0 < NCTX:
                    continue
                S.dma('sp', lambda e, c0=c0, c1=c1: e.dma_start(
                    out=outT[:, c0 - NCTX:c1 - NCTX].rearrange("(k p) n -> p k n", p=128), in_=XT[:, :, c0:c1]),
                    reads=[('XT', bi)])

        def tapX(name):
            tap(name, lambda: XT, [128, KC, NT], ['XT'])

        def tapY(name, r0, r1):
            tap(name, lambda: yT_d[r0:r1, :], [r1 - r0, NT], ['yT_d'], BF16)
        HXW = KC * NT // 2

        def bc(ap, axis, shape):
            return ap.unsqueeze(axis).to_broadcast(list(shape))

        def compute_hx(l):
            hxT = Bump([(XTW, XTW + HXW)]).alloc([128, KC, NT], BF16)
            with ExitStack() as ph:
                make_banks(ph, 8)
                bp = Bump([(XTW + HXW, AW)])
                nt = norm_tiles(bp)
                for bi, (c0, c1) in enumerate(BLOCKS):
                    norm_mod(nt, l, 1, c0, c1, hxT, 'hxT', bi, c0)
                for bi, (c0, c1) in enumerate(BLOCKS):
                    S.dma('sp', lambda e, c0=c0, c1=c1: e.dma_start(out=xsp_d[:, :, c0:c1], in_=XT[:, :, c0:c1]),
                          reads=[('XT', bi)], writes=[('xsp', bi)])
                S.barrier()
            return hxT

        def attention(l, W, hxT, qtiles):
            with ExitStack() as ph:
                bank = make_banks(ph, 6)
                pstT = [PS(ph, "pst%d" % i, [128, 1024], BF16) for i in range(2)]
                bp = Bump([(0, XTW), (XTW + HXW, AW)])
                wA = bp.alloc([128, KC, 768], BF16)
                S.dma('pool', lambda e: e.dma_start(out=wA, in_=W[:, 3072:3840].rearrange("(k p) n -> p k n", p=128)),
                      writes=['wA'])
                qw = bp.alloc([128, 64], F32)
                kw = bp.alloc([128, 64], F32)
                esink = bp.alloc([128, 8], F32)
                acs = bp.alloc([128, 16, 2, 64], F32)
                S.dma('sp', lambda e: e.dma_start(out=qw, in_=I['att_qn'][l:l + 1, :].partition_broadcast(128)), writes=['qw'])
                S.dma('sp', lambda e: e.dma_start(out=kw, in_=I['att_kn'][l:l + 1, :].partition_broadcast(128)), writes=['kw'])
                S.dma('sp', lambda e: e.dma_start(out=esink, in_=I['att_sink'][l:l + 1, :].partition_broadcast(128)), writes=['esink'])
                S.dma('sp', lambda e: e.dma_start(out=acs, in_=I['attcs']), writes=['acs'])
                S.op('act', lambda e: e.activation(out=esink, in_=esink, func=AF.Exp), reads=['esink'], writes=['esink'])
                vext = bp.alloc([128, 18, 2, 65], BF16)
                S.op('pool', lambda e: e.memset(vext, 1.0), writes=['vext'])
                qkT = bp.alloc([128, 18, 5, 128], BF16)
                mge = bp.alloc([128, 128], BF16)
                mle = bp.alloc([128, 128], BF16)
                S.op('dve', lambda e: e.tensor_copy(out=mge, in_=cmask[:, 5, :]), reads=['cmask'], writes=['mge'])
                S.op('dve', lambda e: e.tensor_copy(out=mle, in_=cmask[:, 6, :]), reads=['cmask'], writes=['mle'])
                qsq = bp.alloc([128, 512], F32)
                ss = bp.alloc([128, 8], F32)
                qn = bp.alloc([128, 8, 64], F32)
                t1 = bp.alloc([128, 8, 64], F32)
                t2 = bp.alloc([128, 8, 64], F32)
                qr = bp.alloc([128, 10, 64], BF16)
                def a1_proj(t):
                    bi = blk_of(t * 128)
                    c0, c1 = t * 128, (t + 1) * 128
                    psq, pnq = bank()
                    pkv, pnk = bank()
                    for k in range(KC):
                        S.op('pe', lambda e, k=k, psq=psq, c0=c0, c1=c1: e.matmul(
                            psq[:, :512], lhsT=hxT[:, k, c0:c1], rhs=wA[:, k, 0:512], start=(k == 0), stop=(k == KC - 1)),
                            reads=[('hxT', bi), 'wA'], writes=[pnq])
                    for k in range(KC):
                        S.op('pe', lambda e, k=k, pkv=pkv, c0=c0, c1=c1: e.matmul(
                            pkv[:, :256], lhsT=hxT[:, k, c0:c1], rhs=wA[:, k, 512:768], start=(k == 0), stop=(k == KC - 1)),
                            reads=[('hxT', bi), 'wA'], writes=[pnk])
                    return psq, pnq, pkv, pnk

                def a1_rest(t, psq, pnq, pkv, pnk):
                    bi = blk_of(t * 128)
                    c0, c1 = t * 128, (t + 1) * 128
                    S.op('act', lambda e, pkv=pkv, t=t: e.activation(
                        out=vext[:, t, :, 0:64], in_=pkv[:, 128:256].rearrange("p (g d) -> p g d", g=2), func=AF.Copy),
                        reads=[pnk], writes=[('vext', t)])
                    for (src, pn_, nh, wt, wtn, off) in ((psq[:, 0:512], pnq, 8, qw, 'qw', 0), (pkv[:, 0:128], pnk, 2, kw, 'kw', 8)):
                        src3 = src.rearrange("p (h d) -> p h d", h=nh)
                        S.op('act', lambda e, src=src, nh=nh: e.activation(out=qsq[:, :nh * 64], in_=src, func=AF.Square),
                             reads=[pn_], writes=['qsq'])
                        S.op('dve', lambda e, nh=nh: e.tensor_reduce(
                            out=ss[:, :nh], in_=qsq[:, :nh * 64].rearrange("p (h d) -> p h d", h=nh), axis=AX.X, op=ALU.add),
                            reads=['qsq'], writes=['ss'])
                        S.op('act', lambda e, nh=nh: e.activation(out=ss[:, :nh], in_=ss[:, :nh], func=AF.Sqrt,
                                                                  bias=epst[:, 0:1], scale=1.0 / 64),
                             reads=['ss', 'epst'], writes=['ss'])
                        S.op('dve', lambda e, nh=nh: e.reciprocal(out=ss[:, :nh], in_=ss[:, :nh]), reads=['ss'], writes=['ss'])
                        S.op('dve', lambda e, nh=nh, src3=src3: e.tensor_tensor(
                            out=qn[:, :nh], in0=src3, in1=bc(ss[:, :nh], 2, [128, nh, 64]), op=ALU.mult),
                            reads=[pn_, 'ss'], writes=['qn'])
                        S.op('pool', lambda e, nh=nh, wt=wt: e.tensor_tensor(
                            out=qn[:, :nh], in0=qn[:, :nh], in1=bc(wt, 1, [128, nh, 64]), op=ALU.mult),
                            reads=['qn', wtn], writes=['qn'])
                        if t >= 2:
                            xi = t - 2
                            S.op('dve', lambda e, nh=nh, xi=xi: e.tensor_tensor(
                                out=t1[:, :nh], in0=qn[:, :nh], in1=bc(acs[:, xi, 0, :], 1, [128, nh, 64]), op=ALU.mult),
                                reads=['qn', 'acs'], writes=['t1'])
                            S.op('pool', lambda e, nh=nh, xi=xi: e.tensor_tensor(
                                out=t2[:, :nh, 0:32], in0=qn[:, :nh, 32:64], in1=bc(acs[:, xi, 1, 0:32], 1, [128, nh, 32]), op=ALU.mult),
                                reads=['qn', 'acs'], writes=[('t2', 0)])
                            S.op('pool', lambda e, nh=nh, xi=xi: e.tensor_tensor(
                                out=t2[:, :nh, 32:64], in0=qn[:, :nh, 0:32], in1=bc(acs[:, xi, 1, 32:64], 1, [128, nh, 32]), op=ALU.mult),
                                reads=['qn', 'acs'], writes=[('t2', 1)])
                            if nh == 8:
                                S.op('dve', lambda e: e.tensor_tensor(
                                    out=qr[:, 0:8].rearrange("p (j g) d -> p g j d", g=2),
                                    in0=t1.rearrange("p (g j) d -> p g j d", g=2),
                                    in1=t2.rearrange("p (g j) d -> p g j d", g=2), op=ALU.add),
                                    reads=['t1', 't2'], writes=['qr'])
                            else:
                                S.op('dve', lambda e, nh=nh, off=off: e.tensor_tensor(
                                    out=qr[:, off:off + nh], in0=t1[:, :nh], in1=t2[:, :nh], op=ALU.add),
                                    reads=['t1', 't2'], writes=['qr'])
                        else:
                            if nh == 8:
                                S.op('dve', lambda e: e.tensor_copy(
                                    out=qr[:, 0:8].rearrange("p (j g) d -> p g j d", g=2),
                                    in_=qn.rearrange("p (g j) d -> p g j d", g=2)),
                                    reads=['qn'], writes=['qr'])
                            else:
                                S.op('dve', lambda e, nh=nh, off=off: e.tensor_copy(out=qr[:, off:off + nh], in_=qn[:, :nh]),
                                     reads=['qn'], writes=['qr'])
                    pst = pstT[t % 2]
                    ptn = "pst%d" % (t % 2)
                    for j in range(4):
                        S.op('pe', lambda e, j=j, pst=pst: e.transpose(
                            out=pst[:, j * 128:(j + 1) * 128], in_=qr[:, 2 * j:2 * j + 2, :].rearrange("p a d -> p (a d)"), identity=ident_bf[:]),
                            reads=['qr', 'ident_bf'], writes=[ptn])
                    S.op('pe', lambda e, pst=pst: e.transpose(
                        out=pst[:, 512:640], in_=qr[:, 8:10, :].rearrange("p a d -> p (a d)"), identity=ident_bf[:]),
                        reads=['qr', 'ident_bf'], writes=[ptn])
                    S.op('act', lambda e, pst=pst, t=t: e.activation(
                        out=qkT[:, t], in_=pst[:, 0:640].rearrange("p (a b) -> p a b", a=5), func=AF.Copy),
                        reads=[ptn], writes=[('qkT', t)])

                nxp = a1_proj(0)
                for t in range(18):
                    cup = nxp
                    if t + 1 < 18:
                        nxp = a1_proj(t + 1)
                    a1_rest(t, *cup)
                pTs = [bp.alloc([128, 512], BF16) for i in range(10)]
                npt = 0
                den = bp.alloc([128, 4], F32)
                atoks = [bp.alloc([128, 8, 64], BF16) for i in range(2)]
                astage = [bp.alloc([128, 4, 512], BF16) for i in range(2)]
                nst = 0
                ast2 = {'npt': 0, 'nst': 0}

                def at_scores(qi, n, g_):
                    if n >= 2:
                        keys = ([(n - 1, 'prev')] if n - 1 >= 2 else []) + [(n, 'self')] + \
                               ([(n + 1, 'next')] if n + 1 <= 17 else []) + [(0, 'ctx'), (1, 'ctx')]
                    else:
                        keys = [(0, 'ctx'), (1, 'ctx')]
                    r0, r1 = g_ * 64, (g_ + 1) * 64
                    cur = []
                    for (kt, kind) in keys:
                        ps, pn = bank()
                        S.op('pe', lambda e: e.matmul(
                            ps[:, :512], lhsT=qkT[r0:r1, kt, 4, :], rhs=qkT[r0:r1, n, 0:4, :].rearrange("p a b -> p (a b)"), start=True, stop=True),
                            reads=[('qkT', kt), ('qkT', n)], writes=[pn])
                        pT = pTs[ast2['npt'] % 10]
                        ptn = "pT%d" % (ast2['npt'] % 10)
                        ast2['npt'] += 1
                        S.op('act', lambda e: e.activation(out=pT, in_=ps[:, :512], func=AF.Exp, scale=0.125), reads=[pn], writes=[ptn])
                        if kind in ('prev', 'next'):
                            mk, mkn = (mge, 'mge') if kind == 'prev' else (mle, 'mle')
                            S.op('dve', lambda e: e.tensor_tensor(
                                out=pT.rearrange("p (h q) -> p h q", h=4), in0=pT.rearrange("p (h q) -> p h q", h=4),
                                in1=bc(mk, 1, [128, 4, 128]), op=ALU.mult), reads=[ptn, mkn], writes=[ptn])
                        cur.append((pT, ptn, kt))
                    return cur

                def at_pv(qi, n, g_, cur):
                    atok = atoks[qi % 2]
                    an = "atok%d" % (qi % 2)
                    po, pon = bank()
                    for hh in range(4):
                        for i, (pT, ptn, kt) in enumerate(cur):
                            S.op('pe', lambda e, hh=hh, pT=pT, kt=kt, i=i: e.matmul(
                                po[:, hh * 65:(hh + 1) * 65], lhsT=pT[:, hh * 128:(hh + 1) * 128], rhs=vext[:, kt, g_, :],
                                start=(i == 0), stop=(i == len(cur) - 1)),
                                reads=[ptn, ('vext', kt)], writes=[pon])
                    po3 = po[:, 0:260].rearrange("p (h d) -> p h d", h=4)
                    S.op('dve', lambda e: e.tensor_tensor(out=den, in0=po3[:, :, 64], in1=esink[:, g_ * 4:(g_ + 1) * 4], op=ALU.add),
                         reads=[pon, 'esink'], writes=['den'])
                    S.op('dve', lambda e: e.reciprocal(out=den, in_=den), reads=['den'], writes=['den'])
                    S.op('dve', lambda e: e.tensor_tensor(
                        out=atok[:, g_ * 4:(g_ + 1) * 4, :], in0=po3[:, :, 0:64], in1=bc(den, 2, [128, 4, 64]), op=ALU.mult),
                        reads=[pon, 'den'], writes=[(an, g_)])
                    if g_ == 1:
                        pst = pstT[qi % 2]
                        ptn2 = "pst%d" % (qi % 2)
                        for c in range(4):
                            S.op('pe', lambda e, c=c: e.transpose(
                                out=pst[:, c * 128:(c + 1) * 128], in_=atok[:, 2 * c:2 * c + 2, :].rearrange("p a d -> p (a d)"), identity=ident_bf[:]),
                                reads=[an, 'ident_bf'], writes=[ptn2])
                        slot = qi % 4
                        ast = astage[ast2['nst'] % 2]
                        asn = "astage%d" % (ast2['nst'] % 2)
                        S.op('act', lambda e: e.activation(
                            out=ast[:, :, slot * 128:(slot + 1) * 128], in_=pst[:, 0:512].rearrange("p (a b) -> p a b", a=4), func=AF.Copy),
                            reads=[ptn2], writes=[(asn, slot)])
                        if slot == 3 or qi == len(qtiles) - 1:
                            n0 = qtiles[qi - slot]
                            ncol = (slot + 1) * 128
                            S.dma('sp', lambda e: e.dma_start(
                                out=yT_d[1024:1536, n0 * 128:n0 * 128 + ncol].rearrange("(c p) n -> p c n", p=128),
                                in_=ast[:, :, 0:ncol]), reads=[asn], writes=['yT_d'])
                            ast2['nst'] += 1

                work = [(qi, n, g_) for qi, n in enumerate(qtiles) for g_ in range(2)]
                nxa = at_scores(*work[0])
                for wi, w_ in enumerate(work):
                    cua = nxa
                    if wi + 1 < len(work):
                        nxa = at_scores(*work[wi + 1])
                    at_pv(w_[0], w_[1], w_[2], cua)
                S.barrier()

        def retention(l, W, hxT):
            with ExitStack() as ph:
                bank = make_banks(ph, 6)
                pstT = [PS(ph, "pst%d" % i, [128, 1024], BF16) for i in range(2)]
                bp = Bump([(0, XTW), (XTW + HXW, AW)])
                lg = bp.alloc([128, 16], F32)
                pos4 = bp.alloc([128, 4], F32)
                rcs = bp.alloc([128, 18, 2, 64], F32)
                S.dma('sp', lambda e: e.dma_start(out=lg, in_=I['ret_lg'][l:l + 1, :].partition_broadcast(128)), writes=['lg'])
                S.dma('sp', lambda e: e.dma_start(out=pos4, in_=I['pos4']), writes=['pos4'])
                S.dma('sp', lambda e: e.dma_start(out=rcs, in_=I['retcs']), writes=['rcs'])
                S.op('act', lambda e: e.activation(out=lg, in_=lg, func=AF.Sigmoid), reads=['lg'], writes=['lg'])
                S.op('act', lambda e: e.activation(out=lg, in_=lg, func=AF.Ln), reads=['lg'], writes=['lg'])
                ZX = bp.alloc([128, 4, 8], F32)
                for i, (d0, pc) in enumerate(((0, 0), (8, 1), (0, 2), (8, 3))):
                    S.op('act', lambda e, i=i, d0=d0, pc=pc: e.activation(
                        out=ZX[:, i, :], in_=lg[:, d0:d0 + 8], func=AF.Exp, scale=pos4[:, pc:pc + 1]),
                        reads=['lg', 'pos4'], writes=['ZX'])
                S.op('dve', lambda e: e.tensor_scalar(out=ZX[:, 0:2, :], in0=ZX[:, 0:2, :], scalar1=0.125, scalar2=None, op0=ALU.mult),
                     reads=['ZX'], writes=['ZX'])
                XIb = bp.alloc([128, 8, 2, 64], F32)
                ZEb = bp.alloc([128, 8, 2, 64], F32)
                S.op('dve', lambda e: e.tensor_copy(out=XIb, in_=ZX[:, 2:4, :].rearrange("p d h -> p h d").unsqueeze(3).to_broadcast([128, 8, 2, 64])),
                     reads=['ZX'], writes=['XIb'])
                S.op('dve', lambda e: e.tensor_copy(out=ZEb, in_=ZX[:, 0:2, :].rearrange("p d h -> p h d").unsqueeze(3).to_broadcast([128, 8, 2, 64])),
                     reads=['ZX'], writes=['ZEb'])
                GAM = bp.alloc([128, 8], F32)
                S.op('act', lambda e: e.activation(out=GAM[0:64, :], in_=lg[0:64, 0:8], func=AF.Exp, scale=128.0),
                     reads=['lg'], writes=['GAM'])
                S.op('act', lambda e: e.activation(out=GAM[64:128, :], in_=lg[64:128, 8:16], func=AF.Exp, scale=128.0),
                     reads=['lg'], writes=['GAM'])
                maskT = bp.alloc([128, 8, 128], F32)
                e1 = bp.alloc([128, 128], F32)
                e2 = bp.alloc([128, 128], F32)
                for h in range(8):
                    S.op('act', lambda e, h=h: e.activation(out=e1, in_=cmask[:, 3, :], func=AF.Exp, scale=lg[:, h:h + 1]),
                         reads=['cmask', 'lg'], writes=['e1'])
                    S.op('dve', lambda e: e.tensor_tensor(out=e1, in0=e1, in1=cmask[:, 0, :], op=ALU.mult),
                         reads=['e1', 'cmask'], writes=['e1'])
                    S.op('act', lambda e, h=h: e.activation(out=e2, in_=cmask[:, 4, :], func=AF.Exp, scale=lg[:, 8 + h:9 + h]),
                         reads=['cmask', 'lg'], writes=['e2'])
                    S.op('dve', lambda e: e.tensor_tensor(out=e2, in0=e2, in1=cmask[:, 1, :], op=ALU.mult),
                         reads=['e2', 'cmask'], writes=['e2'])
                    S.op('dve', lambda e: e.tensor_tensor(out=e1, in0=e1, in1=e2, op=ALU.add), reads=['e1', 'e2'], writes=['e1'])
                    S.op('dve', lambda e: e.tensor_tensor(out=e1, in0=e1, in1=cmask[:, 2, :], op=ALU.add), reads=['e1', 'cmask'], writes=['e1'])
                    S.op('dve', lambda e, h=h: e.tensor_scalar(out=maskT[:, h, :], in0=e1, scalar1=0.125, scalar2=None, op0=ALU.mult),
                         reads=['e1'], writes=[('maskT', h)])
                if g.ret_stop == 'const':
                    S.barrier()
                    return
                onesF = bp.alloc([128, 128], F32)
                S.op('pool', lambda e: e.memset(onesF, 1.0 / 128), writes=['onesF'])
                wRs = [bp.alloc([128, KC, 512], BF16) for i in range(2)]
                wGs = [bp.alloc([128, KC, 256], BF16) for i in range(2)]
                RT = bp.alloc([128, 18, 4, 128], BF16)
                kz = bp.alloc([128, 18, 2, 128], BF16)
                vb = bp.alloc([128, 18, 2, 128], BF16)
                sgT = bp.alloc([128, 2, NT], BF16)
                t1s = [bp.alloc([128, 4, 64], F32) for i in range(2)]
                t2s = [bp.alloc([128, 4, 64], F32) for i in range(2)]
                qkrs = [bp.alloc([128, 4, 64], F32) for i in range(2)]
                qbs = [bp.alloc([128, 4, 64], BF16) for i in range(2)]
                qxs = [bp.alloc([128, 2, 2, 64], BF16) for i in range(2)]
                qkss = [bp.alloc([128, 4, 64], F32) for i in range(2)]
                U = bp.alloc([128, 18, 128], F32)
                Sst = bp.alloc([128, 18, 128], BF16)
                R = bp.alloc([128, 128], F32)
                oT = bp.alloc([128, NT], F32)
                Ps = [bp.alloc([128, 128], BF16) for i in range(3)]
                osq = bp.alloc([128, 512], F32)
                mean = bp.alloc([128, 512], F32)
                var = bp.alloc([128, 512], F32)
                dd = bp.alloc([128, 512], F32)
                retS = bp.alloc([128, NT], BF16)
                for hp in range(4):
                    wR = wRs[hp % 2]
                    wrn = "wR%d" % (hp % 2)
                    wG = wGs[hp % 2]
                    wgn = "wG%d" % (hp % 2)
                    for (d0, s0, n_) in ((0, hp * 128, 128), (128, 512 + hp * 128, 128), (256, 1024 + hp * 256, 256)):
                        S.dma('pool', lambda e, wR=wR, d0=d0, s0=s0, n_=n_: e.dma_start(
                            out=wR[:, :, d0:d0 + n_], in_=W[:, s0:s0 + n_].rearrange("(k p) n -> p k n", p=128)),
                            writes=[(wrn, d0)])
                    S.dma('pool', lambda e, wG=wG, hp=hp: e.dma_start(
                        out=wG, in_=W[:, 2048 + hp * 256:2048 + (hp + 1) * 256].rearrange("(k p) n -> p k n", p=128)),
                        writes=[wgn])
                    def p1_proj(t):
                        bi = blk_of(t * 128)
                        c0, c1 = t * 128, (t + 1) * 128
                        t1, t2, qkr, qb, qx, qks = t1s[t % 2], t2s[t % 2], qkrs[t % 2], qbs[t % 2], qxs[t % 2], qkss[t % 2]
                        sfx = '_%d' % (t % 2)
                        ps, pn = bank()
                        for k in range(KC):
                            S.op('pe', lambda e, k=k, ps=ps, c0=c0, c1=c1, wR=wR: e.matmul(
                                ps[:, :512], lhsT=hxT[:, k, c0:c1], rhs=wR[:, k, :], start=(k == 0), stop=(k == KC - 1)),
                                reads=[('hxT', bi), wrn], writes=[pn])
                        return ps, pn

                    def p1_rest(t, ps, pn):
                        t1, t2, qkr, qb, qx, qks = t1s[t % 2], t2s[t % 2], qkrs[t % 2], qbs[t % 2], qxs[t % 2], qkss[t % 2]
                        sfx = '_%d' % (t % 2)
                        S.op('act', lambda e, ps=ps: e.activation(out=qks, in_=ps[:, 0:256].rearrange("p (a d) -> p a d", a=4), func=AF.Copy),
                             reads=[pn], writes=['qks' + sfx])
                        S.op('act', lambda e, ps=ps, t=t: e.activation(
                            out=vb[:, t], in_=ps[:, 256:512].rearrange("p (a d) -> p a d", a=2), func=AF.Copy),
                            reads=[pn], writes=[('vb', t)])
                        S.op('dve', lambda e, t=t: e.tensor_tensor(
                            out=t1, in0=qks, in1=bc(rcs[:, t, 0, :], 1, [128, 4, 64]), op=ALU.mult),
                            reads=['qks' + sfx, 'rcs'], writes=['t1' + sfx])
                        S.op('pool', lambda e, t=t: e.tensor_tensor(
                            out=t2[:, :, 0:32], in0=qks[:, :, 32:64], in1=bc(rcs[:, t, 1, 0:32], 1, [128, 4, 32]), op=ALU.mult),
                            reads=['qks' + sfx, 'rcs'], writes=[('t2' + sfx, 0)])
                        S.op('pool', lambda e, t=t: e.tensor_tensor(
                            out=t2[:, :, 32:64], in0=qks[:, :, 0:32], in1=bc(rcs[:, t, 1, 32:64], 1, [128, 4, 32]), op=ALU.mult),
                            reads=['qks' + sfx, 'rcs'], writes=[('t2' + sfx, 1)])
                        S.op('pool', lambda e: e.tensor_tensor(out=qkr, in0=t1, in1=t2, op=ALU.add),
                             reads=['t1' + sfx, 't2' + sfx], writes=['qkr' + sfx])
                        S.op('act', lambda e: e.activation(out=qb, in_=qkr, func=AF.Copy), reads=['qkr' + sfx], writes=['qb' + sfx])
                        S.op('dve', lambda e: e.tensor_tensor(
                            out=qx, in0=qkr[:, 0:2, :].unsqueeze(2).to_broadcast([128, 2, 2, 64]), in1=XIb[:, hp * 2:hp * 2 + 2], op=ALU.mult),
                            reads=['qkr' + sfx, 'XIb'], writes=['qx' + sfx])
                        S.op('dve', lambda e, t=t: e.tensor_tensor(
                            out=kz[:, t].rearrange("p h (d k) -> p h d k", d=2), in0=qkr[:, 2:4, :].unsqueeze(2).to_broadcast([128, 2, 2, 64]),
                            in1=ZEb[:, hp * 2:hp * 2 + 2], op=ALU.mult),
                            reads=['qkr' + sfx, 'ZEb'], writes=[('kz', t)])
                        pst = pstT[t % 2]
                        ptn = "pst%d" % (t % 2)
                        srcs = (qb[:, 0:2, :].rearrange("p a d -> p (a d)"), qb[:, 2:4, :].rearrange("p a d -> p (a d)"), qx[:, 0].rearrange("p a d -> p (a d)"), qx[:, 1].rearrange("p a d -> p (a d)"))
                        for j in range(4):
                            S.op('pe', lambda e, j=j, pst=pst, s_=srcs[j]: e.transpose(
                                out=pst[:, j * 128:(j + 1) * 128], in_=s_, identity=ident_bf[:]),
                                reads=['qb' + sfx, 'qx' + sfx, 'ident_bf'], writes=[ptn])
                        S.op('act', lambda e, pst=pst, t=t: e.activation(
                            out=RT[:, t], in_=pst[:, 0:512].rearrange("p (a b) -> p a b", a=4), func=AF.Copy),
                            reads=[ptn], writes=[('RT', t)])

                    nxt = p1_proj(0)
                    for t in range(18):
                        cur = nxt
                        if t + 1 < 18:
                            nxt = p1_proj(t + 1)
                        p1_rest(t, *cur)
                    if g.ret_stop == 'pass1':
                        S.barrier()
                        return
                    for hc in range(2):
                        for bi, (c0, c1) in enumerate(BLOCKS):
                            n = c1 - c0
                            ps, pn = bank()
                            for k in range(KC):
                                S.op('pe', lambda e, k=k, ps=ps, c0=c0, c1=c1, n=n, hc=hc, wG=wG: e.matmul(
                                    ps[:, :n], lhsT=wG[:, k, hc * 128:(hc + 1) * 128], rhs=hxT[:, k, c0:c1],
                                    start=(k == 0), stop=(k == KC - 1)),
                                    reads=[('hxT', bi), wgn], writes=[pn])
                            S.op('act', lambda e, ps=ps, c0=c0, c1=c1, n=n, hc=hc: e.activation(
                                out=sgT[:, hc, c0:c1], in_=ps[:, :n], func=AF.Silu), reads=[pn], writes=[('sgT', (hc, bi))])
                    for h in range(2):
                        hd = hp * 2 + h
                        for t in range(18):
                            ps, pn = bank()
                            S.op('pe', lambda e, ps=ps, t=t, h=h: e.matmul(
                                ps[:, :128], lhsT=kz[:, t, h, :], rhs=vb[:, t, h, :], start=True, stop=True),
                                reads=['kz', ('vb', t)], writes=[pn])
                            S.op('act', lambda e, ps=ps, t=t: e.activation(out=U[:, t, :], in_=ps[:, :128], func=AF.Copy),
                                 reads=[pn], writes=[('U', t)])
                        S.op('pool', lambda e: e.memset(R, 0.0), writes=['R'])
                        ts_f = list(range(18))
                        ts_b = [1, 0] + list(range(17, 1, -1))
                        for i in range(18):
                            for (lo, hi, tt, rn, se) in ((0, 64, ts_f[i], 'Rf', 'act'), (64, 128, ts_b[i], 'Rb', 'pool')):
                                if se == 'act':
                                    S.op('act', lambda e, lo=lo, hi=hi, tt=tt: e.activation(out=Sst[lo:hi, tt, :], in_=R[lo:hi, :], func=AF.Copy),
                                         reads=[('R', rn)], writes=[('Sst', (tt, rn))])
                                else:
                                    S.op('pool', lambda e, lo=lo, hi=hi, tt=tt: e.tensor_copy(out=Sst[lo:hi, tt, :], in_=R[lo:hi, :]),
                                         reads=[('R', rn)], writes=[('Sst', (tt, rn))])
                                S.op('dve', lambda e, lo=lo, hi=hi, tt=tt, hd=hd: e.scalar_tensor_tensor(
                                    out=R[lo:hi, :], in0=R[lo:hi, :], scalar=GAM[lo:hi, hd:hd + 1], in1=U[lo:hi, tt, :],
                                    op0=ALU.mult, op1=ALU.add),
                                    reads=[('R', rn), 'GAM', ('U', tt)], writes=[('R', rn)])
                        if g.ret_stop == 'scan':
                            S.barrier()
                            return
                        r0, r1 = h * 64, (h + 1) * 64
                        cst = {'npp': 0, 'psO': None, 'pnO': None}

                        def rc_score(t):
                            psS, pnS = bank()
                            S.op('pe', lambda e: e.matmul(
                                psS[:, :128], lhsT=RT[r0:r1, t, 1, :], rhs=RT[r0:r1, t, 0, :], start=True, stop=True),
                                reads=[('RT', t)], writes=[pnS])
                            P_ = Ps[cst['npp'] % 3]
                            ppn = "Pm%d" % (cst['npp'] % 3)
                            cst['npp'] += 1
                            S.op('dve', lambda e: e.tensor_tensor(out=P_, in0=psS[:, :128], in1=maskT[:, hd, :], op=ALU.mult),
                                 reads=[pnS, ('maskT', hd)], writes=[ppn])
                            return P_, ppn

                        def rc_pv(t, P_, ppn):
                            if t % 4 == 0:
                                cst['psO'], cst['pnO'] = bank()
                            psO, pnO = cst['psO'], cst['pnO']
                            cb = (t % 4) * 128
                            S.op('pe', lambda e: e.matmul(psO[:, cb:cb + 128], lhsT=vb[:, t, h, :], rhs=P_, start=True, stop=False),
                                 reads=[('vb', t), ppn], writes=[pnO])
                            S.op('pe', lambda e: e.matmul(psO[:, cb:cb + 128], lhsT=Sst[:, t, :], rhs=RT[:, t, 2 + h, :], start=False, stop=True),
                                 reads=[('Sst', (t, 'Rf')), ('Sst', (t, 'Rb')), ('RT', t)], writes=[pnO])
                            if t % 4 == 3 or t == 17:
                                t0 = t - (t % 4)
                                ncol = (t - t0 + 1) * 128
                                S.op('act', lambda e: e.activation(out=oT[:, t0 * 128:t0 * 128 + ncol], in_=psO[:, :ncol], func=AF.Copy),
                                     reads=[pnO], writes=[('oT', t0)])
                        nx = rc_score(0)
                        for t in range(18):
                            cu = nx
                            if t + 1 < 18:
                                nx = rc_score(t + 1)
                            rc_pv(t, *cu)
                        for bi, (c0, c1) in enumerate(BLOCKS):
                            n = c1 - c0
                            S.op('act', lambda e, c0=c0, c1=c1, n=n: e.activation(out=osq[:, :n], in_=oT[:, c0:c1], func=AF.Square),
                                 reads=['oT'], writes=['osq'])
                            psM, pnM = bank()
                            psQ, pnQ = bank()
                            S.op('pe', lambda e, psM=psM, c0=c0, c1=c1, n=n: e.matmul(
                                psM[:, :n], lhsT=onesF, rhs=oT[:, c0:c1], start=True, stop=True), reads=['onesF', 'oT'], writes=[pnM])
                            S.op('pe', lambda e, psQ=psQ, n=n: e.matmul(
                                psQ[:, :n], lhsT=onesF, rhs=osq[:, :n], start=True, stop=True), reads=['onesF', 'osq'], writes=[pnQ])
                            S.op('act', lambda e, psM=psM, n=n: e.activation(out=mean[:, :n], in_=psM[:, :n], func=AF.Copy),
                                 reads=[pnM], writes=['mean'])
                            S.op('pool', lambda e, n=n: e.tensor_tensor(out=var[:, :n], in0=mean[:, :n], in1=mean[:, :n], op=ALU.mult),
                                 reads=['mean'], writes=['var'])
                            S.op('dve', lambda e, psQ=psQ, n=n: e.tensor_tensor(out=var[:, :n], in0=psQ[:, :n], in1=var[:, :n], op=ALU.subtract),
                                 reads=[pnQ, 'var'], writes=['var'])
                            S.op('act', lambda e, n=n: e.activation(out=var[:, :n], in_=var[:, :n], func=AF.Sqrt, bias=epst[:, 0:1], scale=1.0),
                                 reads=['var', 'epst'], writes=['var'])
                            S.op('dve', lambda e, n=n: e.reciprocal(out=var[:, :n], in_=var[:, :n]), reads=['var'], writes=['var'])
                            S.op('pool', lambda e, c0=c0, c1=c1, n=n: e.tensor_tensor(out=dd[:, :n], in0=oT[:, c0:c1], in1=mean[:, :n], op=ALU.subtract),
                                 reads=['oT', 'mean'], writes=['dd'])
                            S.op('dve', lambda e, n=n: e.tensor_tensor(out=dd[:, :n], in0=dd[:, :n], in1=var[:, :n], op=ALU.mult),
                                 reads=['dd', 'var'], writes=['dd'])
                            S.op('dve', lambda e, c0=c0, c1=c1, n=n, h=h: e.tensor_tensor(
                                out=retS[:, c0:c1], in0=dd[:, :n], in1=sgT[:, h, c0:c1], op=ALU.mult),
                                reads=['dd', ('sgT', (h, bi))], writes=['retS'])
                        S.dma('sp', lambda e, hd=hd: e.dma_start(out=yT_d[hd * 128:(hd + 1) * 128, :], in_=retS),
                              reads=['retS'], writes=['yT_d'])
                S.barrier()

        def precast_merge(l, W):
            BW = [I['w_branch_ret'][l], I['w_branch_att'][l], I['w_branch_rwkv'][l]]
            KB = [(0, 8), (8, 12), (12, 16)]
            for dp in range(4):
                for br in range(3):
                    cs = 5760 + br * 1024 + dp * 256
                    S.dma('pool', lambda e: e.dma_start(
                        out=wg2_d[l, dp][:, :, br, :], in_=W[:, cs:cs + 256].rearrange("(k p) n -> p k n", p=128)),
                        writes=[('wg2', (l, dp, br))])
                    k0, k1 = KB[br]
                    S.dma('pool', lambda e: e.dma_start(
                        out=wb2_d[l, dp][:, k0:k1, :], in_=BW[br][:, dp * 256:(dp + 1) * 256].rearrange("(k p) n -> p k n", p=128)),
                        writes=[('wb2', (l, dp, br))])
                S.dma('pool', lambda e: e.dma_start(
                    out=wo2_d[l, dp], in_=I['w_out'][l][:, dp * 256:(dp + 1) * 256].rearrange("(k p) n -> p k n", p=128)),
                    writes=[('wo2', (l, dp))])

        def merge(l, W, hxT, blocks_sel):
            with ExitStack() as ph:
                bank = make_banks(ph, 8)
                for bi, (c0, c1) in enumerate(BLOCKS):
                    S.dma('sp', lambda e, c0=c0, c1=c1: e.dma_start(out=XT[:, :, c0:c1], in_=xsp_d[:, :, c0:c1]),
                          writes=[('XT', bi)])
                bp = Bump([(XTW + HXW, AW)])
                yS = bp.alloc([128, 16, 512], BF16)
                mT = bp.alloc([128, KC, 512], BF16)
                wgs = [bp.alloc([128, KC, 3, 256], BF16) for i in range(2)]
                wbs = [bp.alloc([128, 16, 256], BF16) for i in range(2)]
                wos = [bp.alloc([128, KC, 256], BF16) for i in range(2)]
                gs = [bp.alloc([128, 512], F32) for i in range(3)]
                acc = bp.alloc([128, 512], F32)
                tt_ = bp.alloc([128, 512], F32)
                nw = 0
                nwo = 0
                BW = [I['w_branch_ret'][l], I['w_branch_att'][l], I['w_branch_rwkv'][l]]
                KB = [(0, 8), (8, 12), (12, 16)]
                for sb0 in [[0], [1], [2], [3], [4]]:
                    sb = [b for b in sb0 if b in blocks_sel]
                    if not sb:
                        continue
                    offs = []
                    o = 0
                    for b in sb:
                        offs.append(o)
                        o += BLOCKS[b][1] - BLOCKS[b][0]
                    for lb, b in enumerate(sb):
                        c0, c1 = BLOCKS[b]
                        S.dma('sp', lambda e, c0=c0, c1=c1, off=offs[lb]: e.dma_start(
                            out=yS[:, :, off:off + c1 - c0], in_=yT_d[:, c0:c1].rearrange("(kb p) n -> p kb n", p=128)),
                            writes=[('yS', lb)])
                    for dp in range(4):
                        wg = wgs[nw % 2]
                        wgn = "wg%d" % (nw % 2)
                        wb = wbs[nw % 2]
                        wbn = "wb%d" % (nw % 2)
                        nw += 1
                        S.dma('sp', lambda e: e.dma_start(out=wg.rearrange("p k b n -> p (k b n)"), in_=wg2_d[l, dp].rearrange("p k b n -> p (k b n)")),
                              writes=[wgn])
                        S.dma('sp', lambda e: e.dma_start(out=wb.rearrange("p k n -> p (k n)"), in_=wb2_d[l, dp].rearrange("p k n -> p (k n)")),
                              writes=[wbn])
                        for dl in range(2):
                            dc = dp * 2 + dl
                            for lb, b in enumerate(sb):
                                c0, c1 = BLOCKS[b]
                                n = c1 - c0
                                off = offs[lb]
                                for br in range(3):
                                    psg, png = bank()
                                    for k in range(KC):
                                        S.op('pe', lambda e, k=k, psg=psg, wg=wg, br=br, dl=dl, c0=c0, c1=c1, n=n: e.matmul(
                                            psg[:, :n], lhsT=wg[:, k, br, dl * 128:(dl + 1) * 128], rhs=hxT[:, k, c0:c1],
                                            start=(k == 0), stop=(k == KC - 1)),
                                            reads=[wgn, ('hxT', b)], writes=[png])
                                    S.op('act', lambda e, psg=psg, br=br, n=n: e.activation(out=gs[br][:, :n], in_=psg[:, :n], func=AF.Sigmoid),
                                         reads=[png], writes=['gs%d' % br])
                                for br in range(3):
                                    k0, k1 = KB[br]
                                    psp, pnp = bank()
                                    for kb in range(k0, k1):
                                        S.op('pe', lambda e, kb=kb, psp=psp, wb=wb, dl=dl, off=off, n=n, k0=k0, k1=k1: e.matmul(
                                            psp[:, :n], lhsT=wb[:, kb, dl * 128:(dl + 1) * 128], rhs=yS[:, kb, off:off + n],
                                            start=(kb == k0), stop=(kb == k1 - 1)),
                                            reads=[wbn, ('yS', lb)], writes=[pnp])
                                    if br == 0:
                                        S.op('dve', lambda e, psp=psp, n=n: e.tensor_tensor(out=acc[:, :n], in0=psp[:, :n], in1=gs[0][:, :n], op=ALU.mult),
                                             reads=[pnp, 'gs0'], writes=['acc'])
                                    elif br == 1:
                                        S.op('dve', lambda e, psp=psp, n=n: e.tensor_tensor(out=tt_[:, :n], in0=psp[:, :n], in1=gs[1][:, :n], op=ALU.mult),
                                             reads=[pnp, 'gs1'], writes=['tt_'])
                                        S.op('pool', lambda e, n=n: e.tensor_tensor(out=acc[:, :n], in0=acc[:, :n], in1=tt_[:, :n], op=ALU.add),
                                             reads=['acc', 'tt_'], writes=['acc'])
                                    else:
                                        S.op('dve', lambda e, psp=psp, n=n: e.tensor_tensor(out=tt_[:, :n], in0=psp[:, :n], in1=gs[2][:, :n], op=ALU.mult),
                                             reads=[pnp, 'gs2'], writes=['tt_'])
                                        S.op('pool', lambda e, n=n, dc=dc, off=off: e.tensor_tensor(
                                            out=mT[:, dc, off:off + n], in0=acc[:, :n], in1=tt_[:, :n], op=ALU.add),
                                            reads=['acc', 'tt_'], writes=[('mT', (dc, lb))])
                    for dp in range(4):
                        wo = wos[nwo % 2]
                        won = "wom%d" % (nwo % 2)
                        nwo += 1
                        S.dma('sp', lambda e: e.dma_start(out=wo.rearrange("p k n -> p (k n)"), in_=wo2_d[l, dp].rearrange("p k n -> p (k n)")),
                              writes=[won])
                        for dl in range(2):
                            dc = dp * 2 + dl
                            for lb, b in enumerate(sb):
                                c0, c1 = BLOCKS[b]
                                n = c1 - c0
                                off = offs[lb]
                                which = 1 if c0 < NCTX else 0
                                ps, pn = bank()
                                for k in range(KC):
                                    S.op('pe', lambda e, k=k, ps=ps, wo=wo, dl=dl, off=off, n=n: e.matmul(
                                        ps[:, :n], lhsT=wo[:, k, dl * 128:(dl + 1) * 128], rhs=mT[:, k, off:off + n],
                                        start=(k == 0), stop=(k == KC - 1)),
                                        reads=[won, ('mT', (k, lb))], writes=[pn])
                                S.op('dve', lambda e, ps=ps, n=n, dc=dc, c0=c0, c1=c1, which=which: e.scalar_tensor_tensor(
                                    out=XT[:, dc, c0:c1], in0=ps[:, :n], scalar=Gmod[:, l, 1, dc, which:which + 1],
                                    in1=XT[:, dc, c0:c1], op0=ALU.mult, op1=ALU.add),
                                    reads=[pn, 'Gmod', ('XT', b)], writes=[('XT', b)])
                S.barrier()
        CDEC = 0.6065306597126334
        RO = 3840
        NCH = NT // 64

        def rwkv(l, W, hxT):
            with ExitStack() as ph:
                bank = make_banks(ph, 8)
                bp = Bump([(0, XTW), (XTW + HXW, AW)])

                def FT():
                    return bp.alloc([128, NT], F32)
                raw = FT()
                rT = FT()
                kT = FT()
                vT = FT()
                kk = FT()
                sg_ = FT()
                aa = FT()
                cum = FT()
                tA = FT()
                tB = FT()
                tC = FT()
                asum = FT()
                pre = FT()
                rmask = bp.alloc([128, NT], BF16)
                stg = bp.alloc([128, 6, 128], BF16)
                pcst = bp.alloc([128, 36], F32)
                ob = bp.alloc([128, NT], BF16)
                shT = bp.alloc([128, 15, 3], F32)
                w0T = bp.alloc([128, 2, 4], F32)
                a0T = bp.alloc([128, 2, 4], F32)
                vecT = bp.alloc([128, 5, 4], F32)
                v0T = bp.alloc([128, 4], F32)
                S.dma('sp', lambda e: e.dma_start(out=shT, in_=I['rw_shT'][l]), writes=['shT'])
                S.dma('sp', lambda e: e.dma_start(out=w0T, in_=I['rw_w0T'][l]), writes=['w0T'])
                S.dma('sp', lambda e: e.dma_start(out=a0T, in_=I['rw_a0T'][l]), writes=['a0T'])
                S.dma('sp', lambda e: e.dma_start(out=vecT, in_=I['rw_vecT'][l]), writes=['vecT'])
                S.dma('sp', lambda e: e.dma_start(out=v0T, in_=I['rw_v0T']), writes=['v0T'])
                wupP = [bp.alloc([128, 512], BF16) for d in range(2)]
                aupP = [bp.alloc([128, 512], BF16) for d in range(2)]
                for d in range(2):
                    S.op('pool', lambda e, d=d: e.memset(wupP[d], 0.0), writes=['wupP%d' % d])
                    S.op('pool', lambda e, d=d: e.memset(aupP[d], 0.0), writes=['aupP%d' % d])
                    S.dma('pool', lambda e, d=d: e.dma_start(out=wupP[d][d * 64:(d + 1) * 64, :], in_=I['rwkv_w_up'][l, d]),
                          reads=['wupP%d' % d], writes=['wupP%d' % d])
                    S.dma('pool', lambda e, d=d: e.dma_start(out=aupP[d][d * 64:(d + 1) * 64, :], in_=I['rwkv_a_up'][l, d]),
                          reads=['aupP%d' % d], writes=['aupP%d' % d])
                gup = bp.alloc([128, 512], BF16)
                S.dma('pool', lambda e: e.dma_start(out=gup, in_=I['rwkv_g_up'][l]), writes=['gup'])
                if l == 1:
                    vdn = bp.alloc([128, KC, 32], BF16)
                    vup = bp.alloc([32, 512], BF16)
                    S.dma('pool', lambda e: e.dma_start(out=vdn, in_=I['rwkv_v_down'][0].rearrange("(k p) n -> p k n", p=128)), writes=['vdn'])
                    S.dma('pool', lambda e: e.dma_start(out=vup, in_=I['rwkv_v_up'][0]), writes=['vup'])
                    vdT = bp.alloc([32, NT], BF16)
                bones = bp.alloc([128, 128], F32)
                S.op('pool', lambda e: e.memset(bones, 0.0), writes=['bones'])
                S.op('pool', lambda e: e.memset(bones[0:64, 0:64], 1.0), reads=['bones'], writes=['bones'])
                S.op('pool', lambda e: e.memset(bones[64:128, 64:128], 1.0), reads=['bones'], writes=['bones'])
                S.op('pool', lambda e: e.memset(rmask, 1.0), writes=['rmask'])
                S.op('pool', lambda e: e.memset(rmask.rearrange("p (c t) -> p c t", t=64)[:, :, 0:1], 0.0),
                     reads=['rmask'], writes=['rmask'])
                twT = bp.alloc([128, NT], BF16)
                adT = bp.alloc([128, NT], BF16)
                sgdT = bp.alloc([128, NT], BF16)
                wq = [bp.alloc([128, KC, 128], BF16) for i in range(2)]
                nwq = [0]

                def proj_chunk(ci, dst, dstn):
                    w = wq[nwq[0] % 2]
                    wn = 'wq%d' % (nwq[0] % 2)
                    nwq[0] += 1
                    S.dma('pool', lambda e: e.dma_start(
                        out=w, in_=W[:, RO + ci * 128:RO + (ci + 1) * 128].rearrange("(k p) n -> p k n", p=128)), writes=[wn])
                    for bi, (c0, c1) in enumerate(BLOCKS):
                        n = c1 - c0
                        ps, pn = bank()
                        for k in range(KC):
                            S.op('pe', lambda e, k=k, ps=ps, c0=c0, c1=c1, n=n: e.matmul(
                                ps[:, :n], lhsT=w[:, k, :], rhs=hxT[:, k, c0:c1], start=(k == 0), stop=(k == KC - 1)),
                                reads=[wn, ('hxT', bi)], writes=[pn])
                        S.op('act', lambda e, ps=ps, c0=c0, c1=c1, n=n: e.activation(out=dst[:, c0:c1], in_=ps[:, :n], func=AF.Copy),
                             reads=[pn], writes=[(dstn, bi)])

                def conv(raw, rawn, out, outn, ci):
                    for (a, b) in ((0, NCTX), (NCTX, NT)):
                        S.op('act', lambda e, a=a, b=b: e.activation(out=out[:, a:b], in_=raw[:, a:b], func=AF.Identity,
                                                                     scale=shT[:, ci, 1:2]),
                             reads=[rawn, 'shT'], writes=[(outn, a)])
                        S.op('dve', lambda e, a=a, b=b: e.scalar_tensor_tensor(
                            out=out[:, a + 1:b], in0=raw[:, a:b - 1], scalar=shT[:, ci, 0:1], in1=out[:, a + 1:b],
                            op0=ALU.mult, op1=ALU.add), reads=[rawn, 'shT', (outn, a)], writes=[(outn, a)])
                        S.op('dve', lambda e, a=a, b=b: e.scalar_tensor_tensor(
                            out=out[:, a:b - 1], in0=raw[:, a + 1:b], scalar=shT[:, ci, 2:3], in1=out[:, a:b - 1],
                            op0=ALU.mult, op1=ALU.add), reads=[rawn, 'shT', (outn, a)], writes=[(outn, a)])

                def fm_matmul(lhsT, ln, rhs_tile, rn, evac, kparts=128):
                    for bi, (c0, c1) in enumerate(BLOCKS):
                        n = c1 - c0
                        ps, pn = bank()
                        S.op('pe', lambda e, ps=ps, c0=c0, c1=c1, n=n: e.matmul(
                            ps[:, :n], lhsT=lhsT, rhs=rhs_tile[:kparts, c0:c1], start=True, stop=True),
                            reads=[ln, rn], writes=[pn])
                        evac(ps, pn, c0, c1, n)

                for (ci, dst, dn, fn) in ((12, twT, 'twT', AF.Tanh), (13, adT, 'adT', AF.Identity), (14, sgdT, 'sgdT', AF.Sigmoid)):
                    proj_chunk(ci, raw, 'raw')
                    conv(raw, 'raw', tA, 'tA', ci)
                    S.op('act', lambda e, dst=dst, fn=fn: e.activation(out=dst, in_=tA, func=fn), reads=['tA'], writes=[dn])
                if l == 1:
                    for bi, (c0, c1) in enumerate(BLOCKS):
                        n = c1 - c0
                        ps, pn = bank()
                        for k in range(KC):
                            S.op('pe', lambda e, k=k, ps=ps, c0=c0, c1=c1, n=n: e.matmul(
                                ps[:32, :n], lhsT=vdn[:, k, :], rhs=hxT[:, k, c0:c1], start=(k == 0), stop=(k == KC - 1)),
                                reads=['vdn', ('hxT', bi)], writes=[pn])
                        S.op('act', lambda e, ps=ps, c0=c0, c1=c1, n=n: e.activation(out=vdT[:, c0:c1], in_=ps[:32, :n], func=AF.Copy),
                             reads=[pn], writes=['vdT'])

                def to_tm(src, srcn, dst_ap_fn):
                    for part in range(3):
                        for tt in range(6):
                            t = part * 6 + tt
                            ps, pn = bank()
                            S.op('pe', lambda e, ps=ps, t=t: e.transpose(out=ps[:, 0:128], in_=src[:, t * 128:(t + 1) * 128], identity=ident[:]),
                                 reads=[srcn, 'ident'], writes=[pn])
                            S.op('act', lambda e, ps=ps, tt=tt: e.activation(out=stg[:, tt, :], in_=ps[:, 0:128], func=AF.Copy),
                                 reads=[pn], writes=[('stg', tt)])
                        r0 = part * 6 * 128
                        S.dma('sp', lambda e: e.dma_start(
                            out=dst_ap_fn()[r0:r0 + 6 * 128, :].rearrange("(t p) c -> p t c", p=128), in_=stg),
                            reads=['stg'], writes=['scr'])

                for cc in range(4):
                    cs = slice(cc * 128, (cc + 1) * 128)
                    for (q, dst, dn) in ((0, rT, 'rT'), (1, kT, 'kT'), (2, vT, 'vT')):
                        proj_chunk(q * 4 + cc, raw, 'raw')
                        conv(raw, 'raw', dst, dn, q * 4 + cc)
                    if l == 0:
                        S.dma('sp', lambda e, cs=cs: e.dma_start(out=vf_d[cs, :], in_=vT), reads=['vT'], writes=['vf_d'])
                    else:
                        S.dma('sp', lambda e, cs=cs: e.dma_start(out=tA, in_=vf_d[cs, :]), writes=['tA'])

                        def ev_v(ps, pn, c0, c1, n):
                            S.op('act', lambda e: e.activation(out=tB[:, c0:c1], in_=ps[:, :n], func=AF.Sigmoid, bias=v0T[:, cc:cc + 1]),
                                 reads=[pn, 'v0T'], writes=['tB'])
                        fm_matmul(vup[:, cs], 'vup', vdT, 'vdT', ev_v, kparts=32)
                        S.op('dve', lambda e: e.tensor_tensor(out=tA, in0=tA, in1=vT, op=ALU.subtract), reads=['tA', 'vT'], writes=['tA'])
                        S.op('dve', lambda e: e.tensor_tensor(out=tA, in0=tA, in1=tB, op=ALU.mult), reads=['tA', 'tB'], writes=['tA'])
                        S.op('dve', lambda e: e.tensor_tensor(out=vT, in0=vT, in1=tA, op=ALU.add), reads=['tA', 'vT'], writes=['vT'])
                    to_tm(vT, 'vT', lambda cs=cs: vtm_d[:, cs])
                    S.op('dve', lambda e: e.tensor_scalar(out=tA, in0=kT, scalar1=vecT[:, 0, cc:cc + 1], scalar2=None, op0=ALU.mult),
                         reads=['kT', 'vecT'], writes=['tA'])
                    S.op('act', lambda e: e.activation(out=tB, in_=tA, func=AF.Square), reads=['tA'], writes=['tB'])

                    def ev_kk(ps, pn, c0, c1, n):
                        S.op('act', lambda e: e.activation(out=tC[:, c0:c1], in_=ps[:, :n], func=AF.Sqrt), reads=[pn], writes=['tC'])
                    fm_matmul(bones, 'bones', tB, 'tB', ev_kk)
                    S.op('dve', lambda e: e.tensor_scalar(out=tC, in0=tC, scalar1=1e-12, scalar2=None, op0=ALU.max),
                         reads=['tC'], writes=['tC'])
                    S.op('dve', lambda e: e.reciprocal(out=tC, in_=tC), reads=['tC'], writes=['tC'])
                    S.op('dve', lambda e: e.tensor_tensor(out=kk, in0=tA, in1=tC, op=ALU.mult), reads=['tA', 'tC'], writes=['kk'])
                    for d in range(2):
                        def ev_a(ps, pn, c0, c1, n, d=d):
                            S.op('act', lambda e: e.activation(out=aa[:, c0:c1], in_=ps[:, :n], func=AF.Sigmoid, bias=a0T[:, d, cc:cc + 1]),
                                 reads=[pn, 'a0T'], writes=['aa'])
                        fm_matmul(aupP[d][:, cs], 'aupP%d' % d, adT, 'adT', ev_a)

                        def ev_s(ps, pn, c0, c1, n, d=d):
                            S.op('act', lambda e: e.activation(out=sg_[:, c0:c1], in_=ps[:, :n], func=AF.Sigmoid, bias=w0T[:, d, cc:cc + 1]),
                                 reads=[pn, 'w0T'], writes=['sg_'])
                        fm_matmul(wupP[d][:, cs], 'wupP%d' % d, twT, 'twT', ev_s)
                        cum3 = cum.rearrange("p (c t) -> p c t", t=64)
                        if d == 0:
                            S.op('dve', lambda e: e.tensor_tensor_scan(out=cum, data0=rmask, data1=sg_, initial=0.0, op0=ALU.mult, op1=ALU.add),
                                 reads=['rmask', 'sg_'], writes=['cum'])
                            psrc, psn = cum, 'cum'
                        else:
                            S.op('dve', lambda e: e.tensor_tensor_scan(out=pre, data0=rmask, data1=sg_, initial=0.0, op0=ALU.mult, op1=ALU.add),
                                 reads=['rmask', 'sg_'], writes=['pre'])
                            psrc, psn = pre, 'pre'
                            pre3 = pre.rearrange("p (c t) -> p c t", t=64)
                            S.op('dve', lambda e: e.tensor_tensor(
                                out=cum3, in0=pre3[:, :, 63:64].to_broadcast([128, NCH, 64]), in1=pre3, op=ALU.subtract),
                                reads=['pre'], writes=['cum'])
                            S.op('dve', lambda e: e.tensor_tensor(out=cum, in0=cum, in1=sg_, op=ALU.add), reads=['cum', 'sg_'], writes=['cum'])
                        p3_ = psrc.rearrange("p (c t) -> p c t", t=64)
                        tot_b = p3_[:, :, 63:64].to_broadcast([128, NCH, 64])
                        S.op('act', lambda e: e.activation(out=pcst, in_=p3_[:, :, 63], func=AF.Exp, scale=-CDEC),
                             reads=[psn], writes=['pcst'])
                        S.dma('sp', lambda e: e.dma_start(out=pcs_d[d, cs, :], in_=pcst), reads=['pcst'], writes=['scr'])
                        S.op('pool', lambda e: e.tensor_tensor(out=tB, in0=kk, in1=aa, op=ALU.mult), reads=['kk', 'aa'], writes=['tB'])
                        S.op('dve', lambda e: e.tensor_scalar(out=raw, in0=aa, scalar1=-1.0, scalar2=vecT[:, 1, cc:cc + 1],
                                                              op0=ALU.add, op1=ALU.mult), reads=['aa', 'vecT'], writes=['raw'])
                        S.op('dve', lambda e: e.scalar_tensor_tensor(out=raw, in0=raw, scalar=1.0, in1=kT, op0=ALU.add, op1=ALU.mult),
                             reads=['raw', 'kT'], writes=['raw'])
                        tCb = tC.bitcast(BF16)
                        obr = [(ob, 'ob'), (tCb[:, 0:NT], ('tC', 0)), (tCb[:, NT:2 * NT], ('tC', 1))]

                        def store_q(buf, bufn, qi):
                            for hh_ in range(2):
                                S.dma('sp', lambda e, hh_=hh_: e.dma_start(
                                    out=scrT[d, :, :, 2 * cc + hh_, qi, :].rearrange("c k t -> k c t"),
                                    in_=buf[hh_ * 64:(hh_ + 1) * 64, :].rearrange("p (c t) -> p c t", t=64)), reads=[bufn], writes=['scr'])
                        S.op('act', lambda e: e.activation(out=tA, in_=cum, func=AF.Exp, scale=-CDEC), reads=['cum'], writes=['tA'])
                        S.op('dve', lambda e: e.tensor_tensor(out=obr[0][0], in0=tA, in1=rT, op=ALU.mult), reads=['tA', 'rT'], writes=[obr[0][1]])
                        store_q(obr[0][0], obr[0][1], 1)
                        S.op('pool', lambda e: e.tensor_tensor(out=tA, in0=cum, in1=sg_, op=ALU.subtract), reads=['cum', 'sg_'], writes=['tA'])
                        S.op('act', lambda e: e.activation(out=tA, in_=tA, func=AF.Exp, scale=-CDEC), reads=['tA'], writes=['tA'])
                        S.op('dve', lambda e: e.scalar_tensor_tensor(out=obr[1][0], in0=kk, scalar=-1.0, in1=tA, op0=ALU.mult, op1=ALU.mult),
                             reads=['kk', 'tA'], writes=[obr[1][1]])
                        store_q(obr[1][0], obr[1][1], 0)
                        S.op('act', lambda e: e.activation(out=tA, in_=cum, func=AF.Exp, scale=CDEC), reads=['cum'], writes=['tA'])
                        S.op('dve', lambda e: e.tensor_tensor(out=obr[2][0], in0=tA, in1=tB, op=ALU.mult), reads=['tA', 'tB'], writes=[obr[2][1]])
                        store_q(obr[2][0], obr[2][1], 2)
                        S.op('dve', lambda e: e.tensor_tensor(out=obr[0][0], in0=tA, in1=raw, op=ALU.mult), reads=['tA', 'raw'], writes=[obr[0][1]])
                        store_q(obr[0][0], obr[0][1], 3)
                        if d == 0:
                            S.op('dve', lambda e: e.tensor_tensor(
                                out=tA.rearrange("p (c t) -> p c t", t=64), in0=tot_b, in1=cum3, op=ALU.subtract),
                                reads=['cum'], writes=['tA'])
                        else:
                            S.op('pool', lambda e: e.tensor_tensor(out=tA, in0=pre, in1=sg_, op=ALU.subtract),
                                 reads=['pre', 'sg_'], writes=['tA'])
                        S.op('act', lambda e: e.activation(out=tA, in_=tA, func=AF.Exp, scale=-CDEC), reads=['tA'], writes=['tA'])
                        S.op('dve', lambda e: e.tensor_tensor(out=tB, in0=tB, in1=tA, op=ALU.mult), reads=['tA', 'tB'], writes=['tB'])
                        to_tm(tB, 'tB', lambda d=d, cs=cs: scrM[d, 0, :, cs])
                        S.op('pool', lambda e: e.tensor_tensor(out=raw, in0=raw, in1=tA, op=ALU.mult), reads=['tA', 'raw'], writes=['raw'])
                        to_tm(raw, 'raw', lambda d=d, cs=cs: scrM[d, 1, :, cs])
                        if d == 0:
                            S.op('pool', lambda e: e.tensor_copy(out=asum, in_=aa), reads=['aa'], writes=['asum'])
                        else:
                            S.op('pool', lambda e: e.tensor_tensor(out=asum, in0=asum, in1=aa, op=ALU.add), reads=['aa', 'asum'], writes=['asum'])
                    S.op('dve', lambda e: e.tensor_scalar(out=tA, in0=asum, scalar1=-2.0, scalar2=vecT[:, 1, cc:cc + 1],
                                                          op0=ALU.add, op1=ALU.mult), reads=['asum', 'vecT'], writes=['tA'])
                    S.op('dve', lambda e: e.scalar_tensor_tensor(out=tA, in0=tA, scalar=2.0, in1=kT, op0=ALU.add, op1=ALU.mult),
                         reads=['tA', 'kT'], writes=['tA'])
                    S.op('dve', lambda e: e.scalar_tensor_tensor(out=tA, in0=tA, scalar=vecT[:, 2, cc:cc + 1], in1=rT, op0=ALU.mult, op1=ALU.mult),
                         reads=['tA', 'rT', 'vecT'], writes=['tA'])

                    def ev_b(ps, pn, c0, c1, n):
                        S.op('dve', lambda e: e.tensor_tensor(out=tB[:, c0:c1], in0=ps[:, :n], in1=vT[:, c0:c1], op=ALU.mult),
                             reads=[pn, 'vT'], writes=['tB'])
                    fm_matmul(bones, 'bones', tA, 'tA', ev_b)
                    S.dma('sp', lambda e, cs=cs: e.dma_start(out=bon_d[cs, :], in_=tB), reads=['tB'], writes=['scr'])

                    def ev_g(ps, pn, c0, c1, n):
                        S.op('act', lambda e: e.activation(out=tC[:, c0:c1], in_=ps[:, :n], func=AF.Copy), reads=[pn], writes=['tC'])
                    fm_matmul(gup[:, cs], 'gup', sgdT, 'sgdT', ev_g)
                    S.dma('sp', lambda e, cs=cs: e.dma_start(out=g_d[cs, :], in_=tC), reads=['tC'], writes=['scr'])
                S.barrier()

            if g.ret_stop == 'rw_r0':
                return
            with ExitStack() as ph:
                bank = make_banks(ph, 8)
                bp = Bump([(0, XTW), (XTW + HXW, AW)])
                Ysum = bp.alloc([128, 4, NT], F32)
                S.op('pool', lambda e: e.memset(Ysum, 0.0), writes=['Ysum'])
                precast_merge(l, W)
                CM, LM, PCs, ST, AB, BK, VT, MA, KA, Lc, Mc, Xs, Xb, STb, ARB = [], [], [], [], [], [], [], [], [], [], [], [], [], [], []
                Mst, Lst, BDM, BDL, CM2, LM2 = [], [], [], [], [], []
                for d in range(2):
                    CM.append(bp.alloc([64, 128], F32))
                    LM.append(bp.alloc([64, 64], F32))
                    i_s, i_i, i_l = (0, 6, 1) if d == 0 else (1, 5, 0)
                    S.op('dve', lambda e, d=d, i_s=i_s: e.tensor_copy(out=CM[d][:, 0:64], in_=cmask[0:64, i_s, 0:64]), reads=['cmask'], writes=['CM%d' % d])
                    S.op('dve', lambda e, d=d, i_i=i_i: e.tensor_copy(out=CM[d][:, 64:128], in_=cmask[0:64, i_i, 0:64]), reads=['cmask'], writes=['CM%d' % d])
                    S.op('dve', lambda e, d=d, i_l=i_l: e.tensor_copy(out=LM[d], in_=cmask[0:64, i_l, 0:64]), reads=['cmask'], writes=['LM%d' % d])
                    PCs.append(bp.alloc([64, 8, NCH], F32))
                    S.dma('sp', lambda e, d=d: e.dma_start(out=PCs[d], in_=pcs_d[d].rearrange("(h k) c -> k h c", k=64)), writes=['PC%d' % d])
                    ST.append(bp.alloc([64, 8, 64], F32))
                    S.op('pool', lambda e, d=d: e.memset(ST[d], 0.0), writes=['ST%d' % d])
                    AB.append([bp.alloc([64, 8, 4, 64], BF16) for i in range(2)])
                    BK.append([bp.alloc([64, 2, 512], BF16) for i in range(2)])
                    VT.append([bp.alloc([64, 512], BF16) for i in range(2)])
                    ARB.append(bp.alloc([64, 8, 64], BF16))
                    KA.append(bp.alloc([64, 8, 128], BF16))
                    Xs.append(bp.alloc([128, 4, 64], F32))
                    Mst.append([bp.alloc([128, 4, 64], F32) for i in range(2)])
                    Lst.append([bp.alloc([128, 4, 64], F32) for i in range(2)])
                    BDM.append([bp.alloc([128, 4, 128], F32) for i in range(2)])
                    BDL.append([bp.alloc([128, 4, 128], F32) for i in range(2)])
                    for i in range(2):
                        S.op('pool', lambda e, d=d, i=i: e.memset(BDM[d][i], 0.0), writes=['BDM%d%d' % (d, i)])
                        S.op('pool', lambda e, d=d, i=i: e.memset(BDL[d][i], 0.0), writes=['BDL%d%d' % (d, i)])
                    CM2.append(bp.alloc([128, 128], F32))
                    LM2.append(bp.alloc([128, 64], F32))
                    for (lo, hi) in ((0, 64), (64, 128)):
                        S.op('dve', lambda e, d=d, lo=lo, hi=hi, i_s=i_s: e.tensor_copy(out=CM2[d][lo:hi, 0:64], in_=cmask[lo:hi, i_s, lo:hi]), reads=['cmask'], writes=['CM2%d' % d])
                        S.op('dve', lambda e, d=d, lo=lo, hi=hi, i_i=i_i: e.tensor_copy(out=CM2[d][lo:hi, 64:128], in_=cmask[lo:hi, i_i, lo:hi]), reads=['cmask'], writes=['CM2%d' % d])
                        S.op('dve', lambda e, d=d, lo=lo, hi=hi, i_l=i_l: e.tensor_copy(out=LM2[d][lo:hi, :], in_=cmask[lo:hi, i_l, lo:hi]), reads=['cmask'], writes=['LM2%d' % d])
                    Xb.append(bp.alloc([64, 8, 64], BF16))
                    STb.append(bp.alloc([64, 8, 64], BF16))
                    S.op('pool', lambda e, d=d: e.memset(STb[d], 0.0), writes=['STb%d' % d])
                order = [list(range(NCH)), [3, 2, 1, 0] + list(range(NCH - 1, 3, -1))]

                def loads(d, s):
                    c = order[d][s]
                    par = s % 2
                    c0, c1 = c * 64, (c + 1) * 64
                    S.dma('sp', lambda e: e.dma_start(out=AB[d][par].rearrange("p h q t -> p (h q t)"),
                                                      in_=scrT[d, c].rearrange("k h q t -> k (h q t)")),
                          writes=['AB%d%d' % (d, par)])
                    S.dma('sp', lambda e: e.dma_start(out=BK[d][par], in_=scrM[d, :, c0:c1, :].rearrange("q t c -> t q c")),
                          writes=['BK%d%d' % (d, par)])
                    S.dma('sp', lambda e: e.dma_start(out=VT[d][par], in_=vtm_d[c0:c1, :]), writes=['VT%d%d' % (d, par)])

                def step_stages(d, s):
                    c = order[d][s]
                    par = s % 2
                    c0, c1 = c * 64, (c + 1) * 64
                    ab, bk, vt = AB[d][par], BK[d][par], VT[d][par]
                    abn, bkn, vtn = 'AB%d%d' % (d, par), 'BK%d%d' % (d, par), 'VT%d%d' % (d, par)
                    kan, xn, stn = 'KA%d' % d, 'X%d' % d, 'ST%d' % d
                    ka, X, st_ = KA[d], Xs[d], ST[d]
                    xb, stb, arb = Xb[d], STb[d], ARB[d]
                    xbn, stbn, arbn = 'Xb%d' % d, 'STb%d' % d, 'ARB%d' % d

                    def AR(h):
                        return ab[:, h, 0:2, :].rearrange("p a b -> p (a b)")

                    def v4(ps):
                        return ps[:, 0:256].rearrange("p (a c) -> p a c", a=4)

                    def to_bd(eng, src, srcn, bd, bdn):
                        for (lo, hi, k) in ((0, 64, 0), (64, 128, 1)):
                            if eng == 'act':
                                S.op('act', lambda e, lo=lo, hi=hi: e.activation(out=bd[lo:hi, :, lo:hi], in_=src[lo:hi, :, :], func=AF.Copy),
                                     reads=[srcn], writes=[(bdn, k)])
                            else:
                                S.op('dve', lambda e, lo=lo, hi=hi: e.tensor_copy(out=bd[lo:hi, :, lo:hi], in_=src[lo:hi, :, :]),
                                     reads=[srcn], writes=[(bdn, k)])

                    def s_G():
                        for hb in range(2):
                            ps, pn = bank()
                            for hh in range(4):
                                h = hb * 4 + hh
                                S.op('pe', lambda e, ps=ps, hh=hh, h=h: e.matmul(
                                    ps[0:64, hh * 128:(hh + 1) * 128], lhsT=ab[:, h, 3, :], rhs=AR(h), start=True, stop=True),
                                    reads=[abn], writes=[pn])
                            S.op('dve', lambda e, ps=ps, hb=hb: e.tensor_tensor(
                                out=ka[:, hb * 4:(hb + 1) * 4, :], in0=ps[0:64, 0:512].rearrange("p (h c) -> p h c", h=4),
                                in1=bc(CM[d], 1, [64, 4, 128]), op=ALU.mult), reads=[pn, 'CM%d' % d], writes=[(kan, hb)])
                        psM, pnM = bank()
                        psL, pnL = bank()
                        psR, pnR = bank()
                        for h in range(8):
                            hp_, hh = h // 2, h % 2
                            S.op('pe', lambda e, h=h, hp_=hp_, hh=hh: e.matmul(
                                psM[hh * 64:(hh + 1) * 64, hp_ * 64:(hp_ + 1) * 64], lhsT=ab[:, h, 2, :], rhs=ab[:, h, 0, :], start=True, stop=True),
                                reads=[abn], writes=[pnM])
                        for h in range(8):
                            hp_, hh = h // 2, h % 2
                            S.op('pe', lambda e, h=h, hp_=hp_, hh=hh: e.matmul(
                                psL[hh * 64:(hh + 1) * 64, hp_ * 64:(hp_ + 1) * 64], lhsT=ab[:, h, 0, :], rhs=ab[:, h, 2, :], start=True, stop=True),
                                reads=[abn], writes=[pnL])
                        for h in range(8):
                            S.op('pe', lambda e, h=h: e.matmul(
                                psR[0:64, h * 64:(h + 1) * 64], lhsT=ab[:, h, 2, :], rhs=ab[:, h, 1, :], start=True, stop=True),
                                reads=[abn], writes=[pnR])
                        for (lo, hi, k_) in ((0, 64, 0), (64, 128, 1)):
                            S.op('dve', lambda e, lo=lo, hi=hi: e.tensor_tensor(
                                out=BDM[d][0][lo:hi, :, lo:hi], in0=v4(psM)[lo:hi], in1=bc(CM2[d][lo:hi, 0:64], 1, [64, 4, 64]), op=ALU.mult),
                                reads=[pnM, 'CM2%d' % d], writes=[('BDM%d0' % d, k_)])
                            S.op('dve', lambda e, lo=lo, hi=hi: e.tensor_tensor(
                                out=BDL[d][0][lo:hi, :, lo:hi], in0=v4(psL)[lo:hi], in1=bc(LM2[d][lo:hi, :], 1, [64, 4, 64]), op=ALU.mult),
                                reads=[pnL, 'LM2%d' % d], writes=[('BDL%d0' % d, k_)])
                        S.op('dve', lambda e: e.tensor_tensor(
                            out=arb, in0=psR[0:64, 0:512].rearrange("p (h c) -> p h c", h=8), in1=bc(CM[d][:, 64:128], 1, [64, 8, 64]), op=ALU.mult),
                            reads=[pnR, 'CM%d' % d], writes=[arbn])

                    def s_X():
                        ps, pn = bank()
                        for h in range(8):
                            hp_, hh = h // 2, h % 2
                            o = ps[hh * 64:(hh + 1) * 64, hp_ * 64:(hp_ + 1) * 64]
                            S.op('pe', lambda e, o=o, h=h: e.matmul(o, lhsT=ab[:, h, 0, :], rhs=stb[:, h, :], start=True, stop=False),
                                 reads=[abn, stbn], writes=[pn])
                            S.op('pe', lambda e, o=o, h=h: e.matmul(o, lhsT=ka[:, h, 0:64], rhs=vt[:, h * 64:(h + 1) * 64], start=False, stop=True),
                                 reads=[kan, vtn], writes=[pn])
                        S.op('act', lambda e, ps=ps: e.activation(out=X, in_=v4(ps), func=AF.Copy), reads=[pn], writes=[xn])

                    def mk_round(r):
                        def f():
                            k0, k1 = r % 2, (r + 1) % 2
                            bdm, bdl = BDM[d][k0], BDL[d][k0]
                            bdmn, bdln = 'BDM%d%d' % (d, k0), 'BDL%d%d' % (d, k0)
                            psA, pnA = bank()
                            for q in range(4):
                                S.op('pe', lambda e, q=q: e.matmul(psA[:, q * 64:(q + 1) * 64], lhsT=bdm[:, q, :], rhs=X[:, q, :], start=True, stop=True),
                                     reads=[bdmn, xn], writes=[pnA])
                            if r < 5:
                                psM, pnM = bank()
                                psL, pnL = bank()
                                for q in range(4):
                                    S.op('pe', lambda e, q=q: e.matmul(psM[:, q * 128:(q + 1) * 128], lhsT=bdl[:, q, :], rhs=bdm[:, q, :], start=True, stop=True),
                                         reads=[bdln, bdmn], writes=[pnM])
                                if r < 4:
                                    for q in range(4):
                                        S.op('pe', lambda e, q=q: e.matmul(psL[:, q * 128:(q + 1) * 128], lhsT=bdm[:, q, :], rhs=bdl[:, q, :], start=True, stop=True),
                                             reads=[bdmn, bdln], writes=[pnL])
                            S.op('dve', lambda e: e.tensor_tensor(out=X, in0=X, in1=v4(psA), op=ALU.add), reads=[xn, pnA], writes=[xn])
                            if r < 5:
                                S.op('act', lambda e: e.activation(out=BDM[d][k1], in_=psM[:, 0:512].rearrange("p (a c) -> p a c", a=4), func=AF.Copy),
                                     reads=[pnM], writes=['BDM%d%d' % (d, k1)])
                                if r < 4:
                                    S.op('act', lambda e: e.activation(out=BDL[d][k1], in_=psL[:, 0:512].rearrange("p (a c) -> p a c", a=4), func=AF.Copy),
                                         reads=[pnL], writes=['BDL%d%d' % (d, k1)])
                            else:
                                psU, pnU = bank()
                                for q in range(4):
                                    S.op('pe', lambda e, q=q: e.matmul(psU[0:64, q * 64:(q + 1) * 64], lhsT=ident[:, 64:128], rhs=X[:, q, :], start=True, stop=True),
                                         reads=['ident', xn], writes=[pnU])
                                S.op('act', lambda e: e.activation(out=xb[:, 0:8:2, :], in_=X[0:64, :, :], func=AF.Copy), reads=[xn], writes=[(xbn, 0)])
                                S.op('dve', lambda e: e.tensor_copy(out=xb[:, 1:8:2, :], in_=psU[0:64, 0:256].rearrange("p (a c) -> p a c", a=4)),
                                     reads=[pnU], writes=[(xbn, 1)])
                        return f

                    def s_Y():
                        ps, pn = bank()
                        for h in range(8):
                            hp_, hh = h // 2, h % 2
                            o = ps[hh * 64:(hh + 1) * 64, hp_ * 64:(hp_ + 1) * 64]
                            S.op('pe', lambda e, o=o, h=h: e.matmul(o, lhsT=stb[:, h, :], rhs=ab[:, h, 1, :], start=True, stop=False),
                                 reads=[stbn, abn], writes=[pn])
                            S.op('pe', lambda e, o=o, h=h: e.matmul(o, lhsT=xb[:, h, :], rhs=arb[:, h, :], start=False, stop=False),
                                 reads=[xbn, arbn], writes=[pn])
                            S.op('pe', lambda e, o=o, h=h: e.matmul(o, lhsT=vt[:, h * 64:(h + 1) * 64], rhs=ka[:, h, 64:128], start=False, stop=True),
                                 reads=[vtn, kan], writes=[pn])
                        S.op('dve', lambda e, ps=ps: e.tensor_tensor(
                            out=Ysum[:, :, c0:c1], in0=Ysum[:, :, c0:c1], in1=ps[:, 0:256].rearrange("p (a t) -> p a t", a=4), op=ALU.add),
                            reads=[pn, ('Ysum', c)], writes=[('Ysum', c)])

                    def s_S():
                        ps, pn = bank()
                        for h in range(8):
                            o = ps[0:64, h * 64:(h + 1) * 64]
                            S.op('pe', lambda e, o=o, h=h: e.matmul(o, lhsT=bk[:, 0, h * 64:(h + 1) * 64], rhs=xb[:, h, :], start=True, stop=False),
                                 reads=[bkn, xbn], writes=[pn])
                            S.op('pe', lambda e, o=o, h=h: e.matmul(o, lhsT=bk[:, 1, h * 64:(h + 1) * 64], rhs=vt[:, h * 64:(h + 1) * 64], start=False, stop=True),
                                 reads=[bkn, vtn], writes=[pn])
                        S.op('dve', lambda e: e.tensor_tensor(out=st_, in0=st_, in1=bc(PCs[d][:, :, c], 2, [64, 8, 64]), op=ALU.mult),
                             reads=[stn, 'PC%d' % d], writes=[stn])
                        S.op('dve', lambda e, ps=ps: e.tensor_tensor(
                            out=st_, in0=st_, in1=ps[0:64, 0:512].rearrange("p (h c) -> p h c", h=8), op=ALU.add),
                            reads=[stn, pn], writes=[stn])
                        S.op('act', lambda e: e.activation(out=stb, in_=st_, func=AF.Copy), reads=[stn], writes=[stbn])
                    return [s_G, s_X] + [mk_round(r) for r in range(6)] + [s_Y, s_S]

                for d in range(2):
                    loads(d, 0)
                for s in range(NCH):
                    if s + 1 < NCH:
                        for d in range(2):
                            loads(d, s + 1)
                    stg0 = step_stages(0, s)
                    stg1 = step_stages(1, s)
                    for f0, f1 in zip(stg0, stg1):
                        f0()
                        if g.ret_stop != 'rw_one':
                            f1()

                if g.ret_stop in ('rw_r1', 'rw_one'):
                    S.barrier()
                    return
                bonesn = bp.alloc([128, 128], F32)
                S.op('pool', lambda e: e.memset(bonesn, 0.0), writes=['bonesn'])
                S.op('pool', lambda e: e.memset(bonesn[0:64, 0:64], 1.0 / 64), reads=['bonesn'], writes=['bonesn'])
                S.op('pool', lambda e: e.memset(bonesn[64:128, 64:128], 1.0 / 64), reads=['bonesn'], writes=['bonesn'])
                epsg = bp.alloc([128, 1], F32)
                S.op('pool', lambda e: e.memset(epsg, 64e-5), writes=['epsg'])
                vec2 = bp.alloc([128, 5, 4], F32)
                S.dma('sp', lambda e: e.dma_start(out=vec2, in_=I['rw_vecT'][l]), writes=['vec2'])
                ysq2 = [bp.alloc([128, 512], F32) for i in range(2)]
                mean2 = [bp.alloc([128, 512], F32) for i in range(2)]
                var2 = [bp.alloc([128, 512], F32) for i in range(2)]
                dd2 = [bp.alloc([128, 512], F32) for i in range(2)]
                bon2 = [bp.alloc([128, 512], F32) for i in range(2)]
                gg2 = [bp.alloc([128, 512], F32) for i in range(2)]
                r2it = [0]
                ost = bp.alloc([128, NT], BF16)
                for hp_ in range(4):
                    for bi, (c0, c1) in enumerate(BLOCKS):
                        n = c1 - c0
                        pp_ = r2it[0] % 2
                        r2it[0] += 1
                        ysq, mean, var, dd, bon, gg = ysq2[pp_], mean2[pp_], var2[pp_], dd2[pp_], bon2[pp_], gg2[pp_]
                        sx = '_%d' % pp_
                        S.dma('sp', lambda e, c0=c0, c1=c1, n=n: e.dma_start(out=bon[:, :n], in_=bon_d[hp_ * 128:(hp_ + 1) * 128, c0:c1]), writes=['bon' + sx])
                        S.dma('sp', lambda e, c0=c0, c1=c1, n=n: e.dma_start(out=gg[:, :n], in_=g_d[hp_ * 128:(hp_ + 1) * 128, c0:c1]), writes=['gg' + sx])
                        S.op('act', lambda e, c0=c0, c1=c1, n=n: e.activation(out=ysq[:, :n], in_=Ysum[:, hp_, c0:c1], func=AF.Square),
                             reads=['Ysum'], writes=['ysq' + sx])
                        psM, pnM = bank()
                        psQ, pnQ = bank()
                        S.op('pe', lambda e, psM=psM, c0=c0, c1=c1, n=n: e.matmul(psM[:, :n], lhsT=bonesn, rhs=Ysum[:, hp_, c0:c1], start=True, stop=True),
                             reads=['bonesn', 'Ysum'], writes=[pnM])
                        S.op('pe', lambda e, psQ=psQ, n=n: e.matmul(psQ[:, :n], lhsT=bonesn, rhs=ysq[:, :n], start=True, stop=True),
                             reads=['bonesn', 'ysq' + sx], writes=[pnQ])
                        S.op('act', lambda e, psM=psM, n=n: e.activation(out=mean[:, :n], in_=psM[:, :n], func=AF.Copy), reads=[pnM], writes=['mean' + sx])
                        S.op('pool', lambda e, n=n: e.tensor_tensor(out=var[:, :n], in0=mean[:, :n], in1=mean[:, :n], op=ALU.mult), reads=['mean' + sx], writes=['var' + sx])
                        S.op('dve', lambda e, psQ=psQ, n=n: e.tensor_tensor(out=var[:, :n], in0=psQ[:, :n], in1=var[:, :n], op=ALU.subtract),
                             reads=[pnQ, 'var' + sx], writes=['var' + sx])
                        S.op('act', lambda e, n=n: e.activation(out=var[:, :n], in_=var[:, :n], func=AF.Sqrt, bias=epsg[:, 0:1], scale=1.0),
                             reads=['var' + sx, 'epsg'], writes=['var' + sx])
                        S.op('dve', lambda e, n=n: e.reciprocal(out=var[:, :n], in_=var[:, :n]), reads=['var' + sx], writes=['var' + sx])
                        S.op('dve', lambda e, c0=c0, c1=c1, n=n: e.tensor_tensor(out=dd[:, :n], in0=Ysum[:, hp_, c0:c1], in1=mean[:, :n], op=ALU.subtract),
                             reads=['Ysum', 'mean' + sx], writes=['dd' + sx])
                        S.op('dve', lambda e, n=n: e.tensor_tensor(out=dd[:, :n], in0=dd[:, :n], in1=var[:, :n], op=ALU.mult), reads=['dd' + sx, 'var' + sx], writes=['dd' + sx])
                        S.op('act', lambda e, n=n: e.activation(out=dd[:, :n], in_=dd[:, :n], func=AF.Identity,
                                                                bias=vec2[:, 4, hp_:hp_ + 1], scale=vec2[:, 3, hp_:hp_ + 1]),
                             reads=['dd' + sx, 'vec2'], writes=['dd' + sx])
                        S.op('dve', lambda e, n=n: e.tensor_tensor(out=dd[:, :n], in0=dd[:, :n], in1=bon[:, :n], op=ALU.add), reads=['dd' + sx, 'bon' + sx], writes=['dd' + sx])
                        S.op('dve', lambda e, c0=c0, c1=c1, n=n: e.tensor_tensor(out=ost[:, c0:c1], in0=dd[:, :n], in1=gg[:, :n], op=ALU.mult),
                             reads=['dd' + sx, 'gg' + sx], writes=['ost'])
                    S.dma('sp', lambda e: e.dma_start(out=yT_d[1536 + hp_ * 128:1536 + (hp_ + 1) * 128, :], in_=ost), reads=['ost'], writes=['yT_d'])
                S.barrier()
        for l in range(2):
            last = (l == 1)
            ffn(l, 0, I['ffn1_w_in'][l], I['ffn1_w_out'][l])
            tapX('x_ffn1_%d' % l)
            if stop_after == ('ffn1', l):
                break
            W = I['mix_w_in'][l]
            hxT = compute_hx(l)
            attention(l, W, hxT, list(range(2, 18)) if last else list(range(18)))
            tapY('att%d' % l, 1024, 1536)
            if stop_after == ('att', l):
                break
            retention(l, W, hxT)
            tapY('ret%d' % l, 0, 1024)
            if stop_after == ('ret', l):
                break
            rwkv(l, W, hxT)
            tapY('rw%d' % l, 1536, 2048)
            if stop_after == ('rw', l):
                break
            bsel = [1, 2, 3, 4] if last else [0, 1, 2, 3, 4]
            merge(l, W, hxT, bsel)
            tapX('x_mix%d' % l)
            if stop_after == ('mix', l):
                break
            ffn(l, 2, I['ffn2_w_in'][l], I['ffn2_w_out'][l], bsel)
        store_out()
        S.emit()
    return nc, g


INPUT_SHAPES = [
    ("xT", [D, NT]), ("condT", [128, KC, 2]), ("ada_w", [2, D, 9 * D]), ("ada_bT", [128, 2, 72]),
    ("norm_wT", [128, 2, 3, KC]),
    ("ffn1_w_in", [2, D, 2 * DFF]), ("ffn1_w_out", [2, DFF, D]), ("ffn2_w_in", [2, D, 2 * DFF]), ("ffn2_w_out", [2, DFF, D]),
    ("mix_w_in", [2, D, NIN]), ("ret_lg", [2, 16]), ("att_qn", [2, 64]), ("att_kn", [2, 64]), ("att_sink", [2, 8]),
    ("w_branch_ret", [2, 1024, D]), ("w_branch_att", [2, 512, D]), ("w_branch_rwkv", [2, 512, D]), ("w_out", [2, D, D]),
    ("attcs", [128, 16, 2, 64]), ("retcs", [128, 18, 2, 64]), ("cmask", [128, 7, 128]), ("pos4", [128, 4]),
] + RW_INPUT_SHAPES


def _f32(a):
    return np.ascontiguousarray(a, dtype=np.float32)


def _rope_tab(ang):
    c = np.cos(ang)
    s = np.sin(ang)
    return np.stack([np.concatenate([c, c], -1), np.concatenate([-s, s], -1)], axis=1)


def host_consts():
    cst = {}
    rows = NLAT // 64
    row = np.repeat(np.arange(rows), 64).astype(np.float32)
    col = (np.arange(NLAT) % 64).astype(np.float32)
    inv16 = np.power(np.float32(10000.0), -(np.arange(16, dtype=np.float32) / 16)).astype(np.float32)
    ang = np.concatenate([row[:, None] * inv16[None, :], col[:, None] * inv16[None, :]], -1).astype(np.float32)
    tab = _rope_tab(ang.astype(np.float64))
    cst['attcs'] = _f32(tab.reshape(16, 128, 2, 64).transpose(1, 0, 2, 3))
    inv32 = np.power(np.float32(10000.0), -(np.arange(32, dtype=np.float32) / 32)).astype(np.float32)
    pos = np.arange(NT, dtype=np.float32)
    angr = (pos[:, None] * inv32[None, :]).astype(np.float32)
    tabr = _rope_tab(angr.astype(np.float64))
    cst['retcs'] = _f32(tabr.reshape(18, 128, 2, 64).transpose(1, 0, 2, 3))
    m = np.arange(128)[:, None]
    j = np.arange(128)[None, :]
    cm = np.stack([(m < j), (m > j), 2.0 * (m == j), np.maximum(j - m, 0), np.maximum(m - j, 0), (m >= j), (m <= j)], axis=1)
    cst['cmask'] = _f32(cm)
    p = np.arange(128, dtype=np.float32)
    cst['pos4'] = _f32(np.stack([127 - p, p, p + 1, 128 - p], axis=1))
    cst.update(rw_host_consts())
    return cst


def prep_shared(inp):
    sh = {}
    sh['ada_w'] = _f32(inp['ada_w'])
    sh['ada_bT'] = _f32(inp['ada_b'].reshape(2, 72, 128).transpose(2, 0, 1))
    sh['norm_wT'] = _f32(inp['norm_w'].reshape(2, 3, KC, 128).transpose(3, 0, 1, 2))
    for k in ['ffn1_w_in', 'ffn1_w_out', 'ffn2_w_in', 'ffn2_w_out', 'mix_w_in',
              'w_branch_ret', 'w_branch_att', 'w_branch_rwkv', 'w_out', 'att_sink']:
        sh[k] = _f32(inp[k])
    sh['ret_lg'] = _f32(inp['ret_decay_logit'].reshape(2, 16))
    sh['att_qn'] = _f32(inp['att_q_norm'])
    sh['att_kn'] = _f32(inp['att_k_norm'])
    sh.update(rw_prep_shared(inp))
    sh.update(host_consts())
    return sh


def prep_core(inp, b):
    pc = {}
    xs = np.concatenate([inp['ctx'][b], inp['x'][b]], axis=0)
    pc['xT'] = _f32(xs.T)
    cc = np.stack([inp['c'][b], inp['c_ctx']], axis=-1)
    pc['condT'] = _f32(cc.reshape(KC, 128, 2).transpose(1, 0, 2))
    return pc


def kernel(**inp):
    nc, g = build()
    sh = prep_shared(inp)
    in_maps = []
    for b in range(8):
        m = dict(sh)
        m.update(prep_core(inp, b))
        in_maps.append(m)
    res = run_bass_kernel_spmd(nc, in_maps, core_ids=list(range(8)))
    out = np.stack([np.asarray(res.results[b]['outT']).T for b in range(8)], axis=0)
    return np.ascontiguousarray(out.astype(np.float32))
```
